# Optimizing a Trainium2 kernel written in Bass

```python
import jax, jax.numpy as jnp
from jax import lax
import numpy as np

D_MODEL = 2048
BATCH = 4
SEQ = 2048
DEPTH = 4
DEC_BATCH = 32
DEC_SEQ = 32
PAST_LEN = 2048

CHUNK = 64
N_REC_LAYERS = (DEPTH + 1) // 2
N_ATT_LAYERS = DEPTH // 2

D_RNN = D_MODEL
RNN_BLOCKS = 16
RNN_BLOCK_W = D_RNN // RNN_BLOCKS
CONV_W = 4
RG_C = 8.0

D_POOL = D_MODEL // 2
POOL_WINDOWS = (2, 4, 8, 16)
POOL_GROUPS = len(POOL_WINDOWS)
POOL_GW = D_POOL // POOL_GROUPS
POOL_BUF = max(POOL_WINDOWS) - 1

N_HEADS = 16
HEAD_DIM = 128
D_ATT = N_HEADS * HEAD_DIM
Q_BLOCK = 128
EPS = 1e-6

kernel_name = "hybrid_rglru_pool_stickbreaking_stream_step"


def rms_norm(x, g):
    x32 = x.astype(jnp.float32)
    y = x32 * lax.rsqrt(jnp.mean(x32 * x32, axis=-1, keepdims=True) + EPS) * g.astype(jnp.float32)
    return y.astype(x.dtype)


def causal_conv(x, buf, w, b):
    T = x.shape[1]
    xp = jnp.concatenate([buf.astype(x.dtype), x], axis=1)
    y = b.astype(x.dtype) + xp[:, 0:T] * w[0]
    for k in range(1, CONV_W):
        y = y + xp[:, k:k + T] * w[k]
    return y, xp[:, -(CONV_W - 1):]


def rg_lru(x, h0, w_r, b_r, w_i, b_i, lam):
    B, T, _ = x.shape
    f32 = jnp.float32
    x32 = x.astype(f32)
    xb = x32.reshape(B, T, RNN_BLOCKS, RNN_BLOCK_W)
    r = jax.nn.sigmoid(jnp.einsum('btnc,ncd->btnd', xb, w_r.astype(f32)).reshape(B, T, D_RNN) + b_r.astype(f32))
    i = jax.nn.sigmoid(jnp.einsum('btnc,ncd->btnd', xb, w_i.astype(f32)).reshape(B, T, D_RNN) + b_i.astype(f32))
    log_a = -RG_C * r * jax.nn.softplus(-lam.astype(f32))
    a = jnp.exp(log_a)
    u = jnp.sqrt(-jnp.expm1(2.0 * log_a)) * (i * x32)

    def step(h, au):
        a_t, u_t = au
        h = a_t * h + u_t
        return h, h

    h_last, hs = lax.scan(step, h0.astype(f32), (jnp.swapaxes(a, 0, 1), jnp.swapaxes(u, 0, 1)))
    return jnp.swapaxes(hs, 0, 1), h_last


def pool_mix(xb, buf, pos0, w_pool, scale):
    B, T, _ = xb.shape
    f32 = jnp.float32
    xp = jnp.concatenate([buf.astype(f32), xb.astype(f32)], axis=1)
    cs = jnp.concatenate([jnp.zeros((B, 1, D_POOL), f32), jnp.cumsum(xp, axis=1)], axis=1)
    pos = pos0 + jnp.arange(T)
    P = POOL_BUF
    outs = []
    for g, w in enumerate(POOL_WINDOWS):
        sl = slice(g * POOL_GW, (g + 1) * POOL_GW)
        win = cs[:, P + 1:P + 1 + T, sl] - cs[:, P + 1 - w:P + 1 - w + T, sl]
        cnt = jnp.minimum(w, pos + 1).astype(f32)[None, :, None]
        outs.append(win / cnt - xp[:, P:, sl])
    m = jnp.stack(outs, axis=2)
    y = jnp.einsum('btgc,gcd->btgd', m, w_pool.astype(f32)).reshape(B, T, D_POOL) * scale.astype(f32)
    return y, xp[:, -P:]


def rec_layer(x, h0, conv_buf, pool_buf, pos0, g, w_in, conv_w, conv_b, w_r, b_r, w_i, b_i, lam,
              pool_w, pool_scale, w_out):
    h = rms_norm(x, g)
    proj = h @ w_in
    xa, ga, xb, gb = jnp.split(proj, [D_RNN, 2 * D_RNN, 2 * D_RNN + D_POOL], axis=-1)
    xa_c, conv_new = causal_conv(xa, conv_buf, conv_w, conv_b)
    ya, h_new = rg_lru(xa_c, h0, w_r, b_r, w_i, b_i, lam)
    yb, pool_new = pool_mix(xb, pool_buf, pos0, pool_w, pool_scale)
    mixed = jnp.concatenate([ya.astype(x.dtype) * jax.nn.silu(ga),
                             yb.astype(x.dtype) * jax.nn.silu(gb)], axis=-1)
    return x + mixed @ w_out, h_new.astype(x.dtype), conv_new.astype(x.dtype), pool_new.astype(x.dtype)


def sb_attend(q, k, v, q_pos, k_pos):
    f32 = jnp.float32
    z = jnp.einsum('bqhd,bkhd->bhqk', q.astype(f32), k.astype(f32)) * (HEAD_DIM ** -0.5)
    mask = (k_pos[None, :] < q_pos[:, None])[None, None]
    log_1m = jnp.where(mask, -jax.nn.softplus(z), 0.0)
    suffix = lax.cumsum(log_1m, axis=3, reverse=True) - log_1m
    w = jnp.where(mask, jnp.exp(jax.nn.log_sigmoid(z) + suffix), 0.0)
    return jnp.einsum('bhqk,bkhd->bqhd', w, v.astype(f32))


def att_inputs(x, g, w_in):
    B, T, _ = x.shape
    proj = rms_norm(x, g) @ w_in
    q, k, v, gt = jnp.split(proj, 4, axis=-1)
    shp = (B, T, N_HEADS, HEAD_DIM)
    return q.reshape(shp), k.reshape(shp), v.reshape(shp), gt


def att_layer_prompt(x, g, w_in, w_out):
    B, T, _ = x.shape
    q, k, v, gt = att_inputs(x, g, w_in)
    nb = T // Q_BLOCK
    qb = jnp.swapaxes(q.reshape(B, nb, Q_BLOCK, N_HEADS, HEAD_DIM), 0, 1)
    qpos = jnp.arange(T).reshape(nb, Q_BLOCK)
    kpos = jnp.arange(T)
    o = lax.map(lambda a: sb_attend(a[0], k, v, a[1], kpos), (qb, qpos))
    o = jnp.swapaxes(o, 0, 1).reshape(B, T, D_ATT)
    return x + (o.astype(x.dtype) * jax.nn.silu(gt)) @ w_out, k, v


def att_layer_sample(x, k_cache, v_cache, g, w_in, w_out):
    B, T, _ = x.shape
    q, k, v, gt = att_inputs(x, g, w_in)
    P = k_cache.shape[1]
    k_all = jnp.concatenate([k_cache.astype(k.dtype), k], axis=1)
    v_all = jnp.concatenate([v_cache.astype(v.dtype), v], axis=1)
    o = sb_attend(q, k_all, v_all, P + jnp.arange(T), jnp.arange(P + T)).reshape(B, T, D_ATT)
    return x + (o.astype(x.dtype) * jax.nn.silu(gt)) @ w_out, k, v


def setup_inputs(seed: int = 0) -> dict:
    key = jax.random.key(seed)
    ks = jax.random.split(key, 24)
    f32 = jnp.float32

    def nrm(k, shape, scale):
        return jax.random.normal(k, shape, f32) * scale

    d_in_rec = 2 * D_RNN + 2 * D_POOL
    u = jax.random.uniform(ks[12], (N_REC_LAYERS, D_RNN), f32, 0.9, 0.999)
    a0 = u ** (1.0 / RG_C)
    rg_lambda = jnp.log(a0) - jnp.log1p(-a0)
    return {
        'x_prompt': nrm(ks[0], (BATCH, SEQ, D_MODEL), 1.0),
        'x_sample': nrm(ks[1], (DEC_BATCH, DEC_SEQ, D_MODEL), 1.0),
        'cache_k': nrm(ks[2], (N_ATT_LAYERS, DEC_BATCH, PAST_LEN, N_HEADS, HEAD_DIM), 1.0),
        'cache_v': nrm(ks[3], (N_ATT_LAYERS, DEC_BATCH, PAST_LEN, N_HEADS, HEAD_DIM), 1.0),
        'state_h': nrm(ks[4], (N_REC_LAYERS, DEC_BATCH, D_RNN), 0.5),
        'state_conv': nrm(ks[5], (N_REC_LAYERS, DEC_BATCH, CONV_W - 1, D_RNN), 1.0),
        'state_pool': nrm(ks[6], (N_REC_LAYERS, DEC_BATCH, POOL_BUF, D_POOL), 1.0),
        'norm_rec': 1.0 + nrm(ks[7], (N_REC_LAYERS, D_MODEL), 0.05),
        'w_in_rec': nrm(ks[8], (N_REC_LAYERS, D_MODEL, d_in_rec), D_MODEL ** -0.5),
        'conv_w': nrm(ks[9], (N_REC_LAYERS, CONV_W, D_RNN), CONV_W ** -0.5),
        'conv_b': nrm(ks[10], (N_REC_LAYERS, D_RNN), 0.02),
        'gate_r_w': nrm(ks[11], (N_REC_LAYERS, RNN_BLOCKS, RNN_BLOCK_W, RNN_BLOCK_W), RNN_BLOCK_W ** -0.5),
        'gate_r_b': nrm(ks[13], (N_REC_LAYERS, D_RNN), 0.1),
        'gate_i_w': nrm(ks[14], (N_REC_LAYERS, RNN_BLOCKS, RNN_BLOCK_W, RNN_BLOCK_W), RNN_BLOCK_W ** -0.5),
        'gate_i_b': nrm(ks[15], (N_REC_LAYERS, D_RNN), 0.1),
        'rg_lambda': rg_lambda,
        'pool_w': nrm(ks[16], (N_REC_LAYERS, POOL_GROUPS, POOL_GW, POOL_GW), POOL_GW ** -0.5),
        'pool_scale': 1.0 + nrm(ks[17], (N_REC_LAYERS, D_POOL), 0.1),
        'w_out_rec': nrm(ks[18], (N_REC_LAYERS, D_RNN + D_POOL, D_MODEL), (D_RNN + D_POOL) ** -0.5),
        'norm_att': 1.0 + nrm(ks[19], (N_ATT_LAYERS, D_MODEL), 0.05),
        'w_in_att': nrm(ks[20], (N_ATT_LAYERS, D_MODEL, 4 * D_ATT), D_MODEL ** -0.5),
        'w_out_att': nrm(ks[21], (N_ATT_LAYERS, D_ATT, D_MODEL), D_ATT ** -0.5),
        'norm_final': 1.0 + nrm(ks[22], (D_MODEL,), 0.05),
    }


def reference(x_prompt, x_sample, cache_k, cache_v, state_h, state_conv, state_pool,
              norm_rec, w_in_rec, conv_w, conv_b, gate_r_w, gate_r_b, gate_i_w, gate_i_b, rg_lambda,
              pool_w, pool_scale, w_out_rec, norm_att, w_in_att, w_out_att, norm_final):
    xp, xs = x_prompt, x_sample
    Bp = xp.shape[0]
    kp_l, vp_l, ks_l, vs_l = [], [], [], []
    hp_l, cp_l, pp_l, hs_l, cs_l, ps_l = [], [], [], [], [], []
    for l in range(DEPTH):
        j = l // 2
        if l % 2 == 0:
            prm = (norm_rec[j], w_in_rec[j], conv_w[j], conv_b[j], gate_r_w[j], gate_r_b[j],
                   gate_i_w[j], gate_i_b[j], rg_lambda[j], pool_w[j], pool_scale[j], w_out_rec[j])
            xp, hp, cp, pp = rec_layer(
                xp, jnp.zeros((Bp, D_RNN), xp.dtype), jnp.zeros((Bp, CONV_W - 1, D_RNN), xp.dtype),
                jnp.zeros((Bp, POOL_BUF, D_POOL), xp.dtype), 0, *prm)
            xs, hs, cs, ps = rec_layer(xs, state_h[j], state_conv[j], state_pool[j], PAST_LEN, *prm)
            hp_l.append(hp); cp_l.append(cp); pp_l.append(pp)
            hs_l.append(hs); cs_l.append(cs); ps_l.append(ps)
        else:
            xp, kp, vp = att_layer_prompt(xp, norm_att[j], w_in_att[j], w_out_att[j])
            xs, kn, vn = att_layer_sample(xs, cache_k[j], cache_v[j], norm_att[j], w_in_att[j], w_out_att[j])
            kp_l.append(kp); vp_l.append(vp); ks_l.append(kn); vs_l.append(vn)
    y_prompt = rms_norm(xp, norm_final)
    y_sample = rms_norm(xs, norm_final)
    return (y_prompt, y_sample,
            jnp.stack(kp_l), jnp.stack(vp_l), jnp.stack(hp_l), jnp.stack(cp_l), jnp.stack(pp_l),
            jnp.stack(ks_l), jnp.stack(vs_l), jnp.stack(hs_l), jnp.stack(cs_l), jnp.stack(ps_l))
```

```python
import contextlib
import os
import numpy as np
import concourse.bass as bass
import concourse.mybir as mybir
from concourse.bass_utils import run_bass_kernel_spmd

F32 = mybir.dt.float32
BF16 = mybir.dt.bfloat16
ALU = mybir.AluOpType
AF = mybir.ActivationFunctionType

ENGS = ("pe", "act", "dve", "pool", "sp")
NEG = -10000.0
EPS = 1e-6
D = 2048
NCH = 16
NTA = 1152
NTB = 1024
NTT = NTA + NTB
SCALE = 128 ** -0.5
DEPTH = 4
KMODE = 31


class Buf:
    __slots__ = ("name", "w", "r")

    def __init__(self, name=""):
        self.name = name
        self.w = None
        self.r = {}


class Chan:
    def __init__(self, sem):
        self.sem = sem
        self.cnt = 0


class Rec:
    def __init__(self, nc, stack):
        self.nc = nc
        self.stack = stack
        self.ops = {e: [] for e in ENGS}
        self.esem = {e: stack.enter_context(nc.semaphore("es_" + e)) for e in ENGS}
        self.chans = []

    def chan(self, name=None):
        c = Chan(self.stack.enter_context(self.nc.semaphore(name or ("ch%d" % len(self.chans)))))
        self.chans.append(c)
        return c

    def _add(self, eng, fn, reads, writes, chan=None):
        deps = []
        for b in reads:
            if b.w is not None:
                deps.append(b.w)
        for b in writes:
            if b.w is not None:
                deps.append(b.w)
            deps.extend(b.r.values())
        ops = self.ops[eng]
        idx = len(ops)
        if chan is not None:
            chan.cnt += 16
            ev = ("d", chan, chan.cnt)
            key = ("d", id(chan))
        else:
            ev = ("e", eng, idx)
            key = ("e", eng)
        ops.append({"fn": fn, "deps": deps, "chan": chan, "inc": False})
        for b in reads:
            b.r[key] = ev
        for b in writes:
            b.w = ev
            b.r = {}
        return ev

    def op(self, eng, fn, reads=(), writes=()):
        return self._add(eng, fn, reads, writes)

    def dma(self, chan, out, in_, reads=(), writes=(), eng="sp"):
        def fn(e, out=out, in_=in_):
            return e.dma_start(out=out, in_=in_)
        return self._add(eng, fn, reads, writes, chan=chan)

    def fence(self, srcs, dsts):
        snap = [(id(s), s.w, list(s.r.items())) for s in srcs]
        for d in dsts:
            for sid, w, items in snap:
                if w is not None:
                    d.r[("w", sid)] = w
                for k, ev in items:
                    d.r[(k, sid)] = ev

    def emit(self):
        nc = self.nc
        for eng in ENGS:
            for o in self.ops[eng]:
                nd = []
                for d in o["deps"]:
                    if d[0] == "e":
                        if d[1] == "pe" and eng == "pe":
                            continue
                        self.ops[d[1]][d[2]]["inc"] = True
                    nd.append(d)
                o["deps"] = nd
        val = {}
        for eng in ENGS:
            c = 0
            v = []
            for o in self.ops[eng]:
                if o["inc"] and o["chan"] is None:
                    c += 1
                v.append(c)
            val[eng] = v
        self.stats = {e: len(self.ops[e]) for e in ENGS}
        engobj = {"pe": "tensor", "act": "scalar", "dve": "vector", "pool": "gpsimd", "sp": "sync"}
        with nc.Block() as block:
            for eng in ENGS:
                ops = self.ops[eng]
                if not ops and eng != "sp":
                    continue

                def body(e, eng=eng, ops=ops):
                    known = {}
                    for o in ops:
                        need = {}
                        for d in o["deps"]:
                            if d[0] == "e":
                                sem = self.esem[d[1]]
                                v = val[d[1]][d[2]]
                            else:
                                sem = d[1].sem
                                v = d[2]
                            k = id(sem)
                            if known.get(k, 0) >= v:
                                continue
                            if k not in need or need[k][1] < v:
                                need[k] = (sem, v)
                        for k, (sem, v) in need.items():
                            e.wait_ge(sem, v)
                            known[k] = v
                        ins = o["fn"](e)
                        if o["chan"] is not None:
                            ins.then_inc(o["chan"].sem, 16)
                        elif o["inc"]:
                            ins.then_inc(self.esem[eng], 1)
                    if eng == "sp":
                        for c in self.chans:
                            if c.cnt > 0:
                                e.wait_ge(c.sem, c.cnt)
                getattr(block, engobj[eng])(body)


def build(nlayers=DEPTH, ltypes=None):
    nc = bass.Bass("TRN2", target_bir_lowering=False)

    def din(name, shape, dt=F32):
        return nc.dram_tensor(name, list(shape), dt, kind="ExternalInput").ap()

    def dout(name, shape):
        return nc.dram_tensor(name, list(shape), F32, kind="ExternalOutput").ap()

    def dscr(name, shape, dt):
        return nc.dram_tensor(name, list(shape), dt, kind="Internal").ap()

    xp_d = din("xp", [2048, D]); xs_d = din("xs", [128, D])
    ck_d = din("ck", [2, 4, 2048, D]); cv_d = din("cv", [2, 4, 2048, D])
    sh_d = din("sh", [2, 4, D]); sc_d = din("sc", [2, 12, D]); spl_d = din("spl", [2, 60, 1024])
    norm_rec = din("norm_rec", [2, D]); w_in_rec = din("w_in_rec", [2, D, 6144])
    conv_w = din("conv_w", [2, 4, D]); conv_b = din("conv_b", [2, D])
    gate_r_w = din("gate_r_w", [2, 16, 128, 128]); gate_r_b = din("gate_r_b", [2, D])
    gate_i_w = din("gate_i_w", [2, 16, 128, 128]); gate_i_b = din("gate_i_b", [2, D])
    rg_lambda = din("rg_lambda", [2, D]); pool_w = din("pool_w", [2, 4, 256, 256])
    pool_scale = din("pool_scale", [2, 1024]); w_out_rec = din("w_out_rec", [2, 3072, D])
    norm_att = din("norm_att", [2, D]); w_in_att = din("w_in_att", [2, D, 8192])
    w_out_att = din("w_out_att", [2, D, D]); norm_final = din("norm_final", [1, D])
    ident_d = din("ident", [128, 128]); dmask_d = din("dmask", [128, 512]); smask_d = din("smask", [128, 32])
    invc_d = din("invc", [128, 64])

    yp_o = dout("yp", [2048, D]); ys_o = dout("ys", [128, D])
    kp_o = dout("kp", [2, 2048, D]); vp_o = dout("vp", [2, 2048, D])
    hp_o = dout("hp", [2, D]); cp_o = dout("cp", [2, 3, D]); pp_o = dout("pp", [2, 15, 1024])
    ksn_o = dout("ksn", [2, 128, D]); vsn_o = dout("vsn", [2, 128, D])
    hsn_o = dout("hsn", [2, 4, D]); csn_o = dout("csn", [2, 12, D]); psn_o = dout("psn", [2, 60, 1024])

    XS = [dscr("xscr%d" % l, [NCH, 128, NTT], F32) for l in range(DEPTH + 1)]
    XSb = [[[Buf() for _ in range(2)] for _ in range(NCH)] for _ in range(DEPTH + 1)]
    KTS = [dscr("kts%d" % j, [16, 128, 1024], BF16) for j in range(2)]
    VSS = [dscr("vss%d" % j, [16, 128, 1024], BF16) for j in range(2)]
    KTSb = [[Buf() for _ in range(16)] for _ in range(2)]
    VSSb = [[Buf() for _ in range(16)] for _ in range(2)]

    PASSES = [dict(name="A", NT=NTA, col0=0, tcs=[(0, 512), (512, 512), (1024, 128)], nblk=9, pi=0),
              dict(name="B", NT=NTB, col0=NTA, tcs=[(0, 512), (512, 512)], nblk=8, pi=1)]

    with contextlib.ExitStack() as st:
        R = Rec(nc, st)

        def SB(name, shape, dt=F32):
            return st.enter_context(nc.sbuf_tensor("s_" + name, list(shape), dt))

        def X(eng, meth, *args, r=(), w=(), **kw):
            R.op(eng, lambda e: getattr(e, meth)(*args, **kw), reads=r, writes=w)

        PS = [st.enter_context(nc.psum_tensor("ps%d" % i, [128, 512], F32)) for i in range(7)]
        PSb = [Buf("ps%d" % i) for i in range(7)]
        PT = st.enter_context(nc.psum_tensor("pt", [128, 1024], BF16))
        PTh = [PT[:, 0:512], PT[:, 0:512]]
        PThb = [Buf("pt0")] * 2
        PTf = PT[:, :].bitcast(F32)
        PK = [PS[6][:, :].bitcast(BF16), PS[6][:, :].bitcast(BF16)]
        PKb = [PSb[6], PSb[6]]
        po = [PS[2][:, 0:128], PS[3][:, 0:128]]
        pob = [PSb[2], PSb[3]]

        hT = SB("hT", [128, NCH, NTA], BF16); hTb = Buf("hT")
        mixT = SB("mixT", [128, 24 * NTA], BF16)
        mixb = [Buf("mix%d" % i) for i in range(24)]

        def mixv(f):
            return mixT[:, f * NTA:(f + 1) * NTA]

        NSCR = 8
        SCW = 1232
        xa2 = SB("xa2", [128, SCW]); xa2b = Buf("xa2"); xa2ch = R.chan()
        scr = [SB("scr%d" % i, [128, SCW]) for i in range(NSCR)]
        scrb = [Buf("scr%d" % i) for i in range(NSCR)]
        scrch = [R.chan() for _ in range(NSCR)]
        ab = 16 * NTA
        def alias(off, n):
            return mixT[:, ab + off: ab + off + n]
        QT = alias(0, 1152); KTr = alias(1152, 2048); Vb = alias(3200, 1152); Vbo = alias(4352, 1024)
        KcT0 = alias(5376, 2048); Wn0 = alias(7424, 512); WT0 = alias(7936, 512); Vsn = alias(8448, 512)
        KSr = alias(8960, 128)
        QT = [QT[:, 0:1024], xa2[:, 0:512].bitcast(BF16)]; QTb = [Buf(), Buf()]
        KTr = [KTr, scr[4][:, 0:1024].bitcast(BF16)]; KTrb = [Buf(), Buf()]
        Vb = [Vb, scr[7][:, 0:576].bitcast(BF16)]; Vbb = [Buf(), Buf()]
        Vbo = [Vbo, scr[7][:, 576:1088].bitcast(BF16)]; Vbob = [Buf(), Buf()]
        Vsn = [Vsn, xa2[:, 832:1088].bitcast(BF16)]; Vsnb = [Buf(), Buf()]
        KSr = [KSr, alias(9088, 128)]; KSrb = [Buf(), Buf()]
        sacc = xa2[:, 1088:1216]; saccb = Buf()
        U = SB("U", [128, 10240], BF16)
        kraw = [U[:, i * 2048:(i + 1) * 2048] for i in range(2)]; krawb = [[Buf() for _ in range(4)] for _ in range(2)]
        vch = [U[:, 4096 + i * 2048:4096 + (i + 1) * 2048] for i in range(2)]; vchb = [[Buf() for _ in range(4)] for _ in range(2)]
        KcTc = [KcT0, U[:, 8192:10240]]; KcTcb = [Buf(), Buf()]
        Wn = [Wn0, SB("Wn1", [128, 512], BF16)]; Wnb = [Buf(), Buf()]
        WT = [WT0, SB("WT1", [128, 512], BF16)]; WTb = [Buf(), Buf()]
        Qpad = [SB("Qpad", [128, 640], BF16), xa2[:, 512:832].bitcast(BF16)]; Qpadb = [Buf(), Buf()]
        att_alias_bufs = ([QTb[0], KTrb[0], Vbb[0], Vbob[0], Vsnb[0]] + KSrb + krawb[0] + krawb[1] + vchb[0] + vchb[1] + KcTcb + Wnb + WTb
                          + [QTb[1], Vsnb[1], Qpadb[1], saccb])
        scr47_att = [KTrb[1], Vbb[1], Vbob[1]]

        hb16 = [U[:, 6144 + i * NTA:6144 + (i + 1) * NTA] for i in range(3)]
        hb16b = [Buf() for _ in range(3)]
        wslot = [SB("wslot%d" % i, [128, 4096], BF16) for i in range(2)]
        wslotb = [[Buf(), Buf()] for _ in range(2)]
        wslotch = [[R.chan(), R.chan()] for _ in range(2)]
        wctr = [0]
        gw = U[:, 0:4096].rearrange("p (a n d) -> p a n d", a=2, n=16); gwb = Buf(); gwch = R.chan()
        pw = U[:, 4096:6144].rearrange("p (g i d) -> p g i d", g=4, i=2); pwb = Buf(); pwch = R.chan()
        rec_alias_bufs = [gwb, pwb] + hb16b + [xa2b]
        ktok = [scr[5][:, 0:1152].rearrange("p (b d) -> p b d", d=128), scr[6][:, 0:1152].rearrange("p (b d) -> p b d", d=128)]
        ktokb = [scrb[5], scrb[6]]
        ktokch = [scrch[5], scrch[6]]
        kcbch = [R.chan(), R.chan()]; vcbch = [R.chan(), R.chan()]
        nbt = [SB("nbt%d" % i, [128, 512]) for i in range(2)]; nbtb = [Buf(), Buf()]
        bt = [SB("bt%d" % i, [128, 512]) for i in range(2)]; btb = [Buf(), Buf()]
        scarry = SB("scarry", [128, 1]); scarryb = Buf()
        dmask_b = SB("dmask_b", [128, 128], BF16); dmaskbb = Buf()
        smask_b = SB("smask_b", [128, 32], BF16); smaskbb = Buf()
        ptile = [SB("ptile%d" % i, [128, 513]) for i in range(2)]
        ptileb = [Buf(), Buf()]
        zeros_f = SB("zeros_f", [128, 512]); zerosb = Buf()
        ident_f = SB("ident_f", [128, 128]); identfb = Buf()
        ident_b = SB("ident_b", [128, 128], BF16); identbb = Buf()
        ones_f = SB("ones_f", [128, 128]); onesb = Buf()
        dmask = SB("dmask", [128, 128]); dmaskb = Buf()
        smask = SB("smask", [128, 32]); smaskb = Buf()
        invc = SB("invc", [128, 64]); invcb = Buf()
        eps_t = SB("eps_t", [128, 1]); epsb = Buf()
        stg = SB("stg", [64, 1024]); stgch = R.chan()
        ost = SB("ost", [64, 1024]); ostb = Buf(); ostch = R.chan()
        vecs = SB("vecs", [128, NCH, 12]); vecsb = Buf()
        psc = SB("psc", [128, 8, 1]); pscb = Buf()
        clt = SB("clt", [128, NCH, 2]); cltb = Buf()
        shT = SB("shT", [128, NCH, 4]); shTb = Buf()
        scT = SB("scT", [128, NCH, 12]); scTb = Buf()
        spT = SB("spT", [128, 8, 60]); spTb = Buf()
        CT = SB("CT", [128, NCH, 3]); CTb = Buf()
        CH = SB("CH", [128, NCH]); CHb = Buf()
        CP = SB("CP", [128, 8, 15]); CPb = Buf()
        STt = SB("STt", [128, NCH, 16]); STtb = Buf()
        STp = SB("STp", [128, 8, 60]); STpb = Buf()
        cch = R.chan()

        R.dma(cch, ident_f[:], ident_d, writes=[identfb])
        R.dma(cch, dmask[:], dmask_d[:, 0:128], writes=[dmaskb])
        R.dma(cch, smask[:], smask_d, writes=[smaskb])
        evc = R.dma(cch, invc[:], invc_d, writes=[invcb])
        for b_ in (identfb, dmaskb, smaskb, invcb):
            b_.w = evc
        X("dve", "tensor_copy", ident_b[:], ident_f[:], r=[identfb], w=[identbb])
        X("dve", "memset", ones_f[:], 1.0, w=[onesb])
        X("dve", "memset", zeros_f[:], 0.0, w=[zerosb])
        X("dve", "memset", eps_t[:], EPS, w=[epsb])
        X("dve", "memset", Qpad[0][:], 0.0, w=[Qpadb[0]])
        X("dve", "tensor_copy", dmask_b[:], dmask[:], r=[dmaskb], w=[dmaskbb])
        X("dve", "tensor_copy", smask_b[:], smask[:], r=[smaskb], w=[smaskbb])

        stg_prev = []

        def load_rows_T_safe(dst, dstb, srcs, ncols):
            nonlocal stg_prev
            for h0 in range(0, ncols, 1024):
                bufs = []
                r0 = 0
                for ap, nr in srcs:
                    b = Buf()
                    R.fence(stg_prev, [b])
                    R.dma(stgch, stg[r0:r0 + nr, 0:1024], ap[:, h0:h0 + 1024], writes=[b])
                    bufs.append(b)
                    r0 += nr
                nrows = r0
                cb0 = h0 // 128
                for i in range(8):
                    X("pe", "transpose", PS[6][0:128, i * nrows:(i + 1) * nrows], stg[0:nrows, i * 128:(i + 1) * 128],
                      ident_f[0:nrows, 0:nrows], r=bufs + [identfb], w=[PSb[6]])
                X("act", "copy", dst[:, cb0:cb0 + 8, 0:nrows], PS[6][:, 0:8 * nrows].rearrange("p (g r) -> p g r", r=nrows),
                  r=[PSb[6]], w=[dstb])
                stg_prev = bufs

        def store_rows(src, srcb, nch, ncol, dests):
            for h0 in range(0, nch, 8):
                for g0 in range(h0, h0 + 8, 4):
                    for i in range(4):
                        X("pe", "transpose", PS[6][0:ncol, i * 128:(i + 1) * 128], src[:, g0 + i, 0:ncol], ident_f[:, :],
                          r=[srcb, identfb], w=[PSb[6]])
                    X("act", "copy", ost[0:ncol, (g0 - h0) * 128:(g0 - h0 + 4) * 128], PS[6][0:ncol, 0:512], r=[PSb[6]], w=[ostb])
                for r0, nr, ap in dests:
                    R.dma(ostch, ap[:, h0 * 128:(h0 + 8) * 128], ost[r0:r0 + nr, 0:1024], reads=[ostb])

        def load_w(panels, K):
            s = wctr[0] % 2
            wctr[0] += 1
            flat = wslot[s]
            np_ = len(panels)
            view = flat[:, 0:K * 128 * np_].rearrange("p (k n) -> p k n", n=128 * np_)
            R.fence(wslotb[s], wslotb[s])
            for i, ap in enumerate(panels):
                R.dma(wslotch[s][i], view[:, :, i * 128:(i + 1) * 128], ap, writes=[wslotb[s][i]], eng="pool")
            return view, wslotb[s]

        def proj(view, pidx, wbuf, K, rhs_fn, rhs_bufs, tcs, banks, evac):
            for tci, (c0, n) in enumerate(tcs):
                bk = banks[tci % len(banks)]
                for k in range(K):
                    X("pe", "matmul", PS[bk][:, 0:n], view[:, k, pidx * 128:(pidx + 1) * 128], rhs_fn(k, c0, n),
                      start=(k == 0), stop=(k == K - 1), r=[wbuf] + rhs_bufs, w=[PSb[bk]])
                evac(tci, c0, n, bk)

        def init_phase():
            for tb in range(17):
                if tb < 8:
                    pi, col = 0, tb * 128
                elif tb < 16:
                    pi, col = 1, NTA + (tb - 8) * 128
                else:
                    pi, col = 0, 1024
                for hf in range(2):
                    src = (xp_d[tb * 128:(tb + 1) * 128, hf * 1024:(hf + 1) * 1024] if tb < 16
                           else xs_d[:, hf * 1024:(hf + 1) * 1024])
                    ii = (tb % 2) * 2 + hf; oi = 4 + ii
                    xin = scr[ii]; xinb = scrb[ii]
                    xo_ = scr[oi]; xob = scrb[oi]
                    R.dma(scrch[ii], xin[:, 0:1024], src, writes=[xinb])
                    for g in range(2):
                        bk = hf * 2 + g
                        for i in range(4):
                            c = g * 4 + i
                            X("pe", "transpose", PS[bk][:, i * 128:(i + 1) * 128], xin[:, c * 128:(c + 1) * 128], ident_f[:, :],
                              r=[xinb, identfb], w=[PSb[bk]])
                        if g == 0:
                            X("act", "copy", xo_[:, 0:512], PS[bk][:, :], r=[PSb[bk]], w=[xob])
                        else:
                            X("dve", "tensor_copy", xo_[:, 512:1024], PS[bk][:, :], r=[PSb[bk]], w=[xob])
                    dst = XS[0][hf * 8:(hf + 1) * 8, :, col:col + 128].rearrange("c p t -> p c t")
                    wb = [XSb[0][c][pi] for c in range(hf * 8, hf * 8 + 8)]
                    R.dma(scrch[oi], dst, xo_[:, 0:1024].rearrange("p (c t) -> p c t", t=128), reads=[xob], writes=wb)

        def stats(l, P):
            NT = P["NT"]; col0 = P["col0"]; pi = P["pi"]
            acc = scr[4]; accb = scrb[4]
            for c in range(NCH):
                s = c % 2
                R.dma(scrch[s], scr[s][:, 0:NT], XS[l][c, :, col0:col0 + NT], reads=[XSb[l][c][pi]], writes=[scrb[s]])
                if c == 0:
                    X("act", "activation", acc[:, 0:NT], scr[s][:, 0:NT], AF.Square, r=[scrb[s]], w=[accb])
                else:
                    X("act", "activation", scr[2 + s][:, 0:NT], scr[s][:, 0:NT], AF.Square, r=[scrb[s]], w=[scrb[2 + s]])
                    X("dve", "tensor_tensor", acc[:, 0:NT], acc[:, 0:NT], scr[2 + s][:, 0:NT], ALU.add,
                      r=[accb, scrb[2 + s]], w=[accb])
            rstd = scr[7]; rstdb = scrb[7]
            for tci, (c0, n) in enumerate(P["tcs"]):
                X("pe", "matmul", PS[tci][:, 0:n], ones_f[:, :], acc[:, c0:c0 + n], start=True, stop=True,
                  r=[onesb, accb], w=[PSb[tci]])
                X("act", "activation", rstd[:, c0:c0 + n], PS[tci][:, 0:n], AF.Sqrt, bias=eps_t[:, 0:1], scale=1.0 / D,
                  r=[PSb[tci], epsb], w=[rstdb])
            X("dve", "reciprocal", rstd[:, 0:NT], rstd[:, 0:NT], r=[rstdb], w=[rstdb])

        def phase1(l, P, gcol):
            NT = P["NT"]; col0 = P["col0"]; pi = P["pi"]
            stats(l, P)
            for c in range(NCH):
                s = c % 2
                R.dma(scrch[s], scr[s][:, 0:NT], XS[l][c, :, col0:col0 + NT], reads=[XSb[l][c][pi]], writes=[scrb[s]])
                X("dve", "scalar_tensor_tensor", hT[:, c, 0:NT], scr[s][:, 0:NT], gcol(c), scr[7][:, 0:NT], ALU.mult, ALU.mult,
                  r=[scrb[s], scrb[7], vecsb], w=[hTb])

        def phase3(l, P, wout, F):
            NT = P["NT"]; col0 = P["col0"]; pi = P["pi"]
            wv = wout.rearrange("(f p) n -> p f n", p=128)
            for j in range(NCH):
                view, wb = load_w([wv[:, :, j * 128:(j + 1) * 128]], F)
                s = j % 2
                R.dma(scrch[s], scr[s][:, 0:NT], XS[l][j, :, col0:col0 + NT], reads=[XSb[l][j][pi]], writes=[scrb[s]])

                def evac(tci, c0, n, bk, s=s):
                    X("dve", "tensor_tensor", scr[2 + s][:, c0:c0 + n], PS[bk][:, 0:n], scr[s][:, c0:c0 + n], ALU.add,
                      r=[PSb[bk], scrb[s]], w=[scrb[2 + s]])
                proj(view, 0, wb[0], F, lambda k, c0, n: mixv(k)[:, c0:c0 + n], [mixb[k] for k in range(F)], P["tcs"],
                     [0, 1, 2], evac)
                R.dma(scrch[2 + s], XS[l + 1][j, :, col0:col0 + NT], scr[2 + s][:, 0:NT], reads=[scrb[2 + s]],
                      writes=[XSb[l + 1][j][pi]])

        def rec_layer(l, j):
            load_rows_T_safe(vecs, vecsb, [(norm_rec[j:j + 1, :], 1), (conv_w[j], 4), (conv_b[j:j + 1, :], 1),
                                           (gate_r_b[j:j + 1, :], 1), (gate_i_b[j:j + 1, :], 1), (rg_lambda[j:j + 1, :], 1)], D)
            load_rows_T_safe(psc, pscb, [(pool_scale[j:j + 1, :], 1)], 1024)
            load_rows_T_safe(shT, shTb, [(sh_d[j], 4)], D)
            load_rows_T_safe(scT, scTb, [(sc_d[j], 12)], D)
            load_rows_T_safe(spT, spTb, [(spl_d[j], 60)], 1024)
            lam = vecs[:, :, 8]
            X("act", "activation", clt[:, :, 0], lam, AF.Exp, scale=-1.0, r=[vecsb], w=[cltb])
            X("act", "activation", clt[:, :, 0], clt[:, :, 0], AF.Ln, bias=ones_f[:, 0:1], r=[cltb, onesb], w=[cltb])
            X("dve", "tensor_scalar", clt[:, :, 1], clt[:, :, 0], -16.0, None, ALU.mult, r=[cltb], w=[cltb])
            X("dve", "tensor_scalar", clt[:, :, 0], clt[:, :, 0], -8.0, None, ALU.mult, r=[cltb], w=[cltb])
            R.dma(gwch, gw[:, 0], gate_r_w[j].rearrange("n c d -> c n d"), writes=[gwb], eng="pool")
            R.dma(gwch, gw[:, 1], gate_i_w[j].rearrange("n c d -> c n d"), writes=[gwb], eng="pool")
            R.dma(pwch, pw[:], pool_w[j].rearrange("g (i p) d -> p g i d", p=128), writes=[pwb], eng="pool")
            wv = w_in_rec[j].rearrange("(k p) n -> p k n", p=128)
            R.fence(att_alias_bufs, mixb[16:24] + rec_alias_bufs)
            for P in PASSES:
                rec_pass(l, j, P, wv)

        def rec_pass(l, j, P, wv):
            NT = P["NT"]; isA = P["pi"] == 0
            phase1(l, P, lambda c: vecs[:, c, 0:1])
            tcs = P["tcs"]
            xc, xcb_ = scr[1], scrb[1]
            rt, rtb = scr[2], scrb[2]
            it, itb = scr[3], scrb[3]
            At, Atb = scr[4], scrb[4]
            Ht, Htb = scr[5], scrb[5]
            Gs = [(scr[6], scrb[6]), (scr[7], scrb[7])]
            PADW = 3
            def seg_views(tile, pad, off):
                v = [tile[:, off:off + 1024]]
                if isA:
                    base = pad + 1024
                    v.append(tile[:, base:base + 4 * (32 + pad)].rearrange("p (b t) -> p b t", t=32 + pad)[:, :, off:off + 32])
                return v

            def out_views(tile):
                v = [tile[:, 0:1024]]
                if isA:
                    v.append(tile[:, 1024:1152].rearrange("p (b t) -> p b t", t=32))
                return v

            def evac_padded(dst, dstb, pad, eng="act"):
                def ev(tci, c0, n, bk):
                    if c0 < 1024:
                        X(eng, "copy" if eng == "act" else "tensor_copy", dst[:, pad + c0:pad + c0 + n], PS[bk][:, 0:n],
                          r=[PSb[bk]], w=[dstb])
                    else:
                        base = pad + 1024
                        X(eng, "copy" if eng == "act" else "tensor_copy",
                          dst[:, base:base + 4 * (32 + pad)].rearrange("p (b t) -> p b t", t=32 + pad)[:, :, pad:pad + 32],
                          PS[bk][:, 0:128].rearrange("p (b t) -> p b t", t=32), r=[PSb[bk]], w=[dstb])
                return ev

            def evac_silu(G, Gb):
                def ev(tci, c0, n, bk):
                    X("act", "activation", G[:, c0:c0 + n], PS[bk][:, 0:n], AF.Sigmoid, r=[PSb[bk]], w=[Gb])
                    X("dve", "tensor_tensor", G[:, c0:c0 + n], G[:, c0:c0 + n], PS[bk][:, 0:n], ALU.mult,
                      r=[PSb[bk], Gb], w=[Gb])
                return ev

            def set_pads(dst, dstb, pad, carry_ap, state_view):
                if isA:
                    X("dve", "memset", dst[:, 0:pad], 0.0, w=[dstb])
                    base = pad + 1024
                    X("dve", "tensor_copy",
                      dst[:, base:base + 4 * (32 + pad)].rearrange("p (b t) -> p b t", t=32 + pad)[:, :, 0:pad],
                      state_view[0], r=[state_view[1]], w=[dstb])
                else:
                    X("dve", "tensor_copy", dst[:, 0:pad], carry_ap[0], r=[carry_ap[1]], w=[dstb])

            rhsT = lambda k, c0, n: hT[:, k, c0:c0 + n]
            XAs = [(scr[0], scrb[0]), (xa2, xa2b)]
            gbanks = [(PS[6], PSb[6]), (PTf, PThb[0])]

            def pre(c):
                XA, XAb = XAs[c % 2]
                view, wb = load_w([wv[:, :, c * 128:(c + 1) * 128], wv[:, :, 2048 + c * 128:2048 + (c + 1) * 128]], NCH)
                G, Gb = Gs[c % 2]
                set_pads(XA, XAb, 3, (CT[:, c, :], CTb), (scT[:, c, :].rearrange("p (b k) -> p b k", k=3), scTb))
                proj(view, 0, wb[0], NCH, rhsT, [hTb], tcs, [0, 1, 2], evac_padded(XA, XAb, 3))
                proj(view, 1, wb[1], NCH, rhsT, [hTb], tcs, [3, 4, 5], evac_silu(G, Gb))

            H2 = [dict(lo=0, hi=512, tcis=[0]), dict(lo=512, hi=NT, tcis=list(range(1, len(tcs))))]
            hbufs = {k: [Buf(), Buf()] for k in ("xc", "rt", "it", "At", "Ht", "xh")}
            allh = [b_ for v in hbufs.values() for b_ in v]
            R.fence([scrb[1], scrb[2], scrb[3], scrb[4], scrb[5], hb16b[0]], allh)

            def post(c, part):
                XA, XAb = XAs[c % 2]
                G, Gb = Gs[c % 2]
                xh = hb16[0]
                cw = lambda k: vecs[:, c, 1 + k:2 + k]

                def conv(h):
                    lo, hi = H2[h]["lo"], min(H2[h]["hi"], 1024)
                    xb_ = hbufs["xc"][h]
                    segs = [(lambda k: XA[:, k + lo:k + hi], xc[:, lo:hi])]
                    if h == 1 and isA:
                        segs.append((lambda k: XA[:, 1027:1167].rearrange("p (b t) -> p b t", t=35)[:, :, k:k + 32],
                                     xc[:, 1024:1152].rearrange("p (b t) -> p b t", t=32)))
                    for iv, ov in segs:
                        X("dve", "tensor_scalar", ov, iv(0), cw(0), vecs[:, c, 5:6], ALU.mult, ALU.add, r=[XAb, vecsb], w=[xb_])
                    for k in range(1, 4):
                        for iv, ov in segs:
                            X("dve", "scalar_tensor_tensor", ov, iv(k), cw(k), ov, ALU.mult, ALU.add, r=[XAb, vecsb, xb_], w=[xb_])

                def xhc(h):
                    lo, hi = H2[h]["lo"], H2[h]["hi"]
                    X("act", "copy", xh[:, lo:hi], xc[:, lo:hi], r=[hbufs["xc"][h]], w=[hbufs["xh"][h]])
                    if h == 1:
                        if isA:
                            X("act", "copy", CT[:, c, :], XA[:, 1024:1027], r=[XAb], w=[CTb])
                            X("act", "copy", STt[:, c, 0:12].rearrange("p (b k) -> p b k", k=3),
                              XA[:, 1027:1167].rearrange("p (b t) -> p b t", t=35)[:, :, 32:35], r=[XAb], w=[STtb])
                        else:
                            X("act", "copy", STt[:, c, 0:3], XA[:, 1024:1027], r=[XAb], w=[STtb])

                def gates(h):
                    for tci in H2[h]["tcis"]:
                        c0, n = tcs[tci]
                        X("pe", "matmul", PS[6][:, 0:n], gw[:, 0, c, :], xh[:, c0:c0 + n], start=True, stop=True,
                          r=[gwb, hbufs["xh"][h]], w=[PSb[6]])
                        X("act", "activation", rt[:, c0:c0 + n], PS[6][:, 0:n], AF.Sigmoid, bias=vecs[:, c, 6:7],
                          r=[PSb[6], vecsb], w=[hbufs["rt"][h]])
                        X("pe", "matmul", PTf[:, 0:n], gw[:, 1, c, :], xh[:, c0:c0 + n], start=True, stop=True,
                          r=[gwb, hbufs["xh"][h]], w=[PThb[0]])
                        X("act", "activation", it[:, c0:c0 + n], PTf[:, 0:n], AF.Sigmoid, bias=vecs[:, c, 7:8],
                          r=[PThb[0], vecsb], w=[hbufs["it"][h]])

                def exps(h):
                    lo, hi = H2[h]["lo"], H2[h]["hi"]
                    rb, ab_ = hbufs["rt"][h], hbufs["At"][h]
                    X("act", "activation", At[:, lo:hi], rt[:, lo:hi], AF.Exp, scale=clt[:, c, 0:1], r=[rb, cltb], w=[ab_])
                    X("act", "activation", rt[:, lo:hi], rt[:, lo:hi], AF.Exp, scale=clt[:, c, 1:2], r=[rb, cltb], w=[rb])

                def itxc(h):
                    lo, hi = H2[h]["lo"], H2[h]["hi"]
                    X("dve", "tensor_tensor", it[:, lo:hi], it[:, lo:hi], xc[:, lo:hi], ALU.mult,
                      r=[hbufs["it"][h], hbufs["xc"][h]], w=[hbufs["it"][h]])
                    X("dve", "tensor_scalar", rt[:, lo:hi], rt[:, lo:hi], -1.0, 1.0, ALU.mult, ALU.add,
                      r=[hbufs["rt"][h]], w=[hbufs["rt"][h]])

                def sqrt_(h):
                    lo, hi = H2[h]["lo"], H2[h]["hi"]
                    X("act", "activation", rt[:, lo:hi], rt[:, lo:hi], AF.Sqrt, r=[hbufs["rt"][h]], w=[hbufs["rt"][h]])

                def scan(h):
                    lo, hi = H2[h]["lo"], H2[h]["hi"]
                    ab_, ib_, hb_ = hbufs["At"][h], hbufs["it"][h], hbufs["Ht"][h]
                    X("dve", "tensor_tensor", it[:, lo:hi], it[:, lo:hi], rt[:, lo:hi], ALU.mult, r=[ib_, hbufs["rt"][h]], w=[ib_])
                    if h == 0:
                        if isA:
                            X("dve", "tensor_tensor_scan", Ht[:, 0:512], At[:, 0:512], it[:, 0:512], 0.0, ALU.mult, ALU.add,
                              r=[ab_, ib_], w=[hb_])
                        else:
                            X("dve", "tensor_tensor_scan", Ht[:, 0:512], At[:, 0:512], it[:, 0:512], CH[:, c:c + 1],
                              ALU.mult, ALU.add, r=[ab_, ib_, CHb], w=[hb_])
                    else:
                        X("dve", "tensor_tensor_scan", Ht[:, 512:1024], At[:, 512:1024], it[:, 512:1024], Ht[:, 511:512],
                          ALU.mult, ALU.add, r=[ab_, ib_, hbufs["Ht"][0]], w=[hb_])
                        if isA:
                            for b in range(4):
                                s0 = 1024 + 32 * b
                                X("dve", "tensor_tensor_scan", Ht[:, s0:s0 + 32], At[:, s0:s0 + 32], it[:, s0:s0 + 32],
                                  shT[:, c, b:b + 1], ALU.mult, ALU.add, r=[ab_, ib_, shTb], w=[hb_])
                    X("dve", "tensor_tensor", mixv(c)[:, lo:hi], Ht[:, lo:hi], G[:, lo:hi], ALU.mult, r=[hb_, Gb], w=[mixb[c]])

                def states():
                    hb_ = hbufs["Ht"][1]
                    if isA:
                        X("act", "copy", CH[:, c:c + 1], Ht[:, 1023:1024], r=[hb_], w=[CHb])
                        X("act", "copy", STt[:, c, 12:16], Ht[:, 1024:1152].rearrange("p (b t) -> p b t", t=32)[:, :, 31],
                          r=[hb_], w=[STtb])
                    else:
                        X("act", "copy", STt[:, c, 3:4], Ht[:, 1023:1024], r=[hb_], w=[STtb])

                if part == 0:
                    conv(0); xhc(0); conv(1); xhc(1)
                else:
                    gates(0); gates(1)
                    exps(0); itxc(0); exps(1); sqrt_(0); itxc(1); sqrt_(1)
                    scan(0); scan(1)
                    states()

            for c in range(NCH):
                if c > 0:
                    post(c - 1, 0)
                pre(c)
                if c > 0:
                    post(c - 1, 1)
            post(NCH - 1, 0)
            post(NCH - 1, 1)
            R.fence(allh, [scrb[1], scrb[2], scrb[3], scrb[4], scrb[5], hb16b[0]])
            if isA:
                store_rows(STt, STtb, NCH, 16, [(0, 12, csn_o[j]), (12, 4, hsn_o[j])])
            else:
                store_rows(STt, STtb, NCH, 4, [(0, 3, cp_o[j]), (3, 1, hp_o[j:j + 1, :])])
            Sa, Sab = scr[1], scrb[1]
            Sb_, Sbb = scr[2], scrb[2]
            W = (15 + 1024 + 4 * 47) if isA else (15 + 1024)
            for g in range(4):
                wwin = 2 << g
                mbs = []
                for o in range(2):
                    cb = 2 * g + o
                    view, wb = load_w([wv[:, :, 4096 + cb * 128:4096 + (cb + 1) * 128],
                                       wv[:, :, 5120 + cb * 128:5120 + (cb + 1) * 128]], NCH)
                    G, Gb = Gs[o]
                    XB, XBb = XAs[o]
                    set_pads(XB, XBb, 15, (CP[:, cb, :], CPb), (spT[:, cb, :].rearrange("p (b k) -> p b k", k=15), spTb))
                    proj(view, 0, wb[0], NCH, rhsT, [hTb], tcs, [0, 1, 2], evac_padded(XB, XBb, 15))
                    proj(view, 1, wb[1], NCH, rhsT, [hTb], tcs, [3, 4, 5], evac_silu(G, Gb))
                    if isA:
                        X("act", "copy", CP[:, cb, :], XB[:, 1024:1039], r=[XBb], w=[CPb])
                        X("act", "copy", STp[:, cb, :].rearrange("p (b k) -> p b k", k=15),
                          XB[:, 1039:1227].rearrange("p (b t) -> p b t", t=47)[:, :, 32:47], r=[XBb], w=[STpb])
                    else:
                        X("act", "copy", STp[:, cb, 0:15], XB[:, 1024:1039], r=[XBb], w=[STpb])
                    src, srcb = XB, XBb
                    sh = 1
                    tgl = [(Sa, Sab), (Sb_, Sbb)]
                    ti = 0
                    lo = 0
                    while sh < wwin:
                        dstt, dstb = tgl[ti]
                        lo2 = lo + sh
                        X("dve", "tensor_tensor", dstt[:, lo2:W], src[:, lo2:W], src[:, lo:W - sh], ALU.add,
                          r=[srcb], w=[dstb])
                        src, srcb = dstt, dstb
                        lo = lo2
                        ti ^= 1
                        sh *= 2
                    mb = hb16[1 + o]; mbb = hb16b[1 + o]
                    for sv, xv, ov in zip(seg_views(src, 15, 15), seg_views(XB, 15, 15), out_views(mb)):
                        X("dve", "scalar_tensor_tensor", ov, sv, 1.0 / wwin, xv, ALU.mult, ALU.subtract,
                          r=[srcb, XBb], w=[mbb])
                    if isA:
                        nf = wwin - 1
                        tmp = tgl[ti][0]; tmpb = tgl[ti][1]
                        X("dve", "tensor_tensor", tmp[:, 0:nf], src[:, 15:15 + nf], invc[:, g * 16:g * 16 + nf], ALU.mult,
                          r=[srcb, invcb], w=[tmpb])
                        X("dve", "tensor_tensor", mb[:, 0:nf], tmp[:, 0:nf], XB[:, 15:15 + nf], ALU.subtract,
                          r=[tmpb, XBb], w=[mbb])
                    mbs.append((mb, mbb))
                    if os.environ.get("KDEBUG") == "2" and l == 0 and isA and g == 3 and o == 0:
                        d1 = nc.dram_tensor("dbg_s16", [128, SCW], F32, kind="ExternalOutput").ap()
                        d2 = nc.dram_tensor("dbg_xb", [128, SCW], F32, kind="ExternalOutput").ap()
                        d3 = nc.dram_tensor("dbg_mb", [128, NTA], BF16, kind="ExternalOutput").ap()
                        R.dma(cch_kv[0], d1, src[:, :], reads=[srcb])
                        R.dma(cch_kv[1], d2, XB[:, :], reads=[XBb])
                        R.dma(cch_kv[2], d3, mb[:, :], reads=[mbb])
                for o in range(2):
                    cbo = 2 * g + o
                    G, Gb = Gs[o]
                    for tci, (c0, n) in enumerate(tcs):
                        gbk, gbkb = gbanks[tci % 2]
                        for i in range(2):
                            X("pe", "matmul", gbk[:, 0:n], pw[:, g, i, o * 128:(o + 1) * 128], mbs[i][0][:, c0:c0 + n],
                              start=(i == 0), stop=(i == 1), r=[pwb, mbs[i][1]], w=[gbkb])
                        X("dve", "scalar_tensor_tensor", mixv(16 + cbo)[:, c0:c0 + n], gbk[:, 0:n], psc[:, cbo, 0:1],
                          G[:, c0:c0 + n], ALU.mult, ALU.mult, r=[gbkb, pscb, Gb], w=[mixb[16 + cbo]])
            if isA:
                store_rows(STp, STpb, 8, 60, [(0, 60, psn_o[j])])
            else:
                store_rows(STp, STpb, 8, 15, [(0, 15, pp_o[j])])
            if os.environ.get("KDEBUG") and l == 0 and isA:
                dbg = nc.dram_tensor("dbg_mix", [24, 128, NTA], BF16, kind="ExternalOutput").ap()
                for f in range(24):
                    R.dma(cch_kv[0], dbg[f], mixv(f), reads=[mixb[f]])
            phase3(l, P, w_out_rec[j], 24)

        def stA(T):
            zb = T["i"] % 2
            for (c0, nn, lhsT, rhs, rb, s_, e_) in T["qk"]:
                X("pe", "matmul", PS[zb][:, c0:c0 + nn], lhsT, rhs, start=s_, stop=e_, r=rb, w=[PSb[zb]])

        def stB(T):
            s = T["i"] % 2; n = T["n"]
            X("act", "activation", nbt[s][:, 0:n], PS[s][:, 0:n], AF.Sigmoid, scale=-1.0, r=[PSb[s]], w=[nbtb[s]])
            X("act", "activation", bt[s][:, 0:n], PS[s][:, 0:n], AF.Sigmoid, r=[PSb[s]], w=[btb[s]])

        def stC(T):
            s = T["i"] % 2; n = T["n"]
            pt_ = ptile[s]; ptb = ptileb[s]
            if T["first"]:
                X("dve", "memset", pt_[:, 0:1], 1.0, w=[ptb])
                init = 1.0; ib = []
            elif T["chain"] == "p":
                pp_ = ptile[1 - s]; npv = T["prev_n"]
                X("dve", "tensor_copy", pt_[:, 0:1], pp_[:, npv:npv + 1], r=[ptileb[1 - s]], w=[ptb])
                init = pp_[:, npv:npv + 1]; ib = [ptileb[1 - s]]
            else:
                X("dve", "tensor_copy", pt_[:, 0:1], scarry[:, 0:1], r=[scarryb], w=[ptb])
                init = scarry[:, 0:1]; ib = [scarryb]
            X("dve", "tensor_tensor_scan", pt_[:, 1:n + 1], nbt[s][:, 0:n], zeros_f[:, 0:n], init, ALU.mult, ALU.add,
              r=[nbtb[s], zerosb] + ib, w=[ptb])
            if T["chain"] == "s" and not T["lastc"]:
                X("dve", "tensor_copy", scarry[:, 0:1], pt_[:, n:n + 1], r=[ptb], w=[scarryb])
            X("dve", "tensor_tensor", Wn[s][:, 0:n][:, ::-1], bt[s][:, 0:n], pt_[:, 0:n], ALU.mult,
              r=[btb[s], ptb], w=[Wnb[s]])

        def stD(T):
            s = T["i"] % 2
            for bi, (c0, kk) in enumerate(T["tr"]):
                X("pe", "transpose", PTh[s][0:kk, bi * 128:(bi + 1) * 128], Wn[s][:, c0:c0 + kk], ident_b[:, :],
                  r=[Wnb[s], identbb], w=[PThb[s]])

        def stE(T):
            s = T["i"] % 2
            kk, ncl = T["ecopy"]
            X("act", "copy", WT[s][0:kk, 0:ncl], PTh[s][0:kk, 0:ncl], r=[PThb[s]], w=[WTb[s]])

        def stF(T):
            s = T["i"] % 2; pr = T["pr"]
            for (c0, nn, lhsT, lb, kk, wc0, s_, e_) in T["pv"]:
                X("pe", "matmul", po[pr][:, c0:c0 + nn], lhsT, WT[s][0:kk, wc0:wc0 + nn], start=s_, stop=e_,
                  r=[lb, WTb[s]], w=[pob[pr]])
            if T.get("fin"):
                T["fin"](pr)
            for h in T.get("postF", ()):
                h()

        def run_pipeline(tasks, pre=(), filler=()):
            n = len(tasks); ng = len(filler); gi = 0
            grp = -1
            for i, T in enumerate(tasks):
                T["i"] = i
                if T["chain"] == "s" or T["first"]:
                    grp += 1
                T["pr"] = grp % 2
            for h in pre:
                h()
            for t in range(n + 3):
                if 0 <= t - 3 < n:
                    stF(tasks[t - 3])
                if 0 <= t - 2 < n:
                    stD(tasks[t - 2]); stE(tasks[t - 2])
                if 0 <= t - 1 < n:
                    stC(tasks[t - 1])
                if t < n:
                    stA(tasks[t]); stB(tasks[t])
                    for h in tasks[t].get("post", ()):
                        h()
                    tgt = min(ng, ((t + 1) * ng + n - 1) // n)
                    while gi < tgt:
                        filler[gi](); gi += 1
            while gi < ng:
                filler[gi](); gi += 1

        pctr = [0]

        def prompt_tasks(hd, ql, isA, G, Gb):
            p = hd % 2
            qa = ql + (0 if isA else 8)
            s0 = (15 - qa) * 128
            q_ap = QT[p][:, ql * 128:(ql + 1) * 128]
            pr = pctr[0] % 2
            pctr[0] += 1
            spans = []
            s = s0
            while s < 2048:
                n = min(512, 2048 - s)
                spans.append((s, n, 2048 - s - n))
                s += n
            total = sum(n // 128 for _, n, _ in spans)
            bd = 0
            tasks = []
            qkb = [QTb[p], KTrb[p]]
            for ci, (s, n, k0) in enumerate(spans):
                if ci == 0:
                    qk = [(0, 128, q_ap, KTr[p][:, s:s + 128], qkb, True, False),
                          (0, 128, ident_b[:, :], dmask_b[:, :], [identbb, dmaskbb], False, True)]
                    if n > 128:
                        qk.append((128, n - 128, q_ap, KTr[p][:, s + 128:s + n], qkb, True, True))
                else:
                    qk = [(0, n, q_ap, KTr[p][:, s:s + n], qkb, True, True)]
                pv = []
                for bi in range(n // 128):
                    kb_ = k0 // 128 + bi
                    if isA:
                        vap, vbuf = Vb[p][:, kb_ * 128:(kb_ + 1) * 128], Vbb[p]
                    elif kb_ < 8:
                        vap, vbuf = Vbo[p][:, kb_ * 128:(kb_ + 1) * 128], Vbob[p]
                    else:
                        vap, vbuf = Vb[p][:, (kb_ - 8) * 128:(kb_ - 7) * 128], Vbb[p]
                    pv.append((0, 128, vap, vbuf, 128, bi * 128, bd == 0, bd == total - 1))
                    bd += 1
                tasks.append(dict(n=n, chain="p", first=(ci == 0), prev_n=(spans[ci - 1][1] if ci else None), lastc=False,
                                  pr=pr, qk=qk, tr=[(bi * 128, 128) for bi in range(n // 128)], ecopy=(128, n), pv=pv,
                                  post=[], postF=[]))

            def fin(pr, hd=hd, ql=ql, G=G, Gb=Gb):
                X("dve", "tensor_tensor", mixv(hd)[:, ql * 128:(ql + 1) * 128], po[pr][:, 0:128], G[:, ql * 128:(ql + 1) * 128],
                  ALU.mult, r=[pob[pr], Gb], w=[mixb[hd]])
            tasks[-1]["fin"] = fin
            return tasks

        def sample_tasks(hd, G, Gb):
            p = hd % 2
            tasks = []

            def mkfin(k):
                def fin(pr, hd=hd, G=G, Gb=Gb):
                    if k == 0:
                        X("dve", "tensor_copy", sacc[:, :], po[pr][:, 0:128], r=[pob[pr]], w=[saccb])
                    else:
                        X("dve", "tensor_tensor", sacc[:, :], sacc[:, :], po[pr][:, 0:128], ALU.add, r=[pob[pr], saccb], w=[saccb])
                    if k == 4:
                        X("dve", "tensor_tensor", mixv(hd)[:, 1024:1152], sacc[:, :], G[:, 1024:1152],
                          ALU.mult, r=[saccb, Gb], w=[mixb[hd]])
                return fin
            pr = pctr[0] % 2; pctr[0] += 1
            tasks.append(dict(n=32, chain="s", first=True, lastc=False, pr=pr,
                              qk=[(0, 32, Qpad[p][:, b * 128:(b + 1) * 128], KSr[p][:, 32 * b:32 * b + 32], [Qpadb[p], KSrb[p]],
                                   b == 0, False) for b in range(4)]
                                 + [(0, 32, ident_b[:, :], smask_b[:, :], [identbb, smaskbb], False, True)],
                              tr=[(0, 32)], ecopy=(32, 128),
                              pv=[(32 * b, 32, Vsn[p][0:32, b * 128:(b + 1) * 128], Vsnb[p], 32, 32 * b, b == 0, b == 3)
                                  for b in range(4)],
                              post=[], postF=[], fin=mkfin(0)))
            for cc in range(4):
                s = cc % 2
                pr = pctr[0] % 2; pctr[0] += 1
                tasks.append(dict(n=512, chain="s", first=False, lastc=(cc == 3), pr=pr,
                                  qk=[(0, 512, Qpad[p][:, b * 128:(b + 1) * 128], KcTc[s][:, b * 512:(b + 1) * 512],
                                       [Qpadb[p], KcTcb[s]], b == 0, b == 3) for b in range(4)],
                                  tr=[(bi * 128, 128) for bi in range(4)], ecopy=(128, 512),
                                  pv=[(32 * b, 32, vch[s][:, (b * 4 + blk) * 128:(b * 4 + blk + 1) * 128], vchb[s][b], 128,
                                       blk * 128 + 32 * b, (blk == 0 and b == 0), (blk == 3 and b == 3))
                                      for blk in range(4) for b in range(4)],
                                  post=[], postF=[], fin=mkfin(cc + 1)))
            return tasks

        def att_layer(l, j):
            load_rows_T_safe(vecs, vecsb, [(norm_att[j:j + 1, :], 1)], D)
            wv = w_in_att[j].rearrange("(k p) n -> p k n", p=128)
            R.fence(mixb[16:24] + rec_alias_bufs, att_alias_bufs)
            for P in PASSES:
                att_pass(l, j, P, wv)

        def att_pass(l, j, P, wv):
            NT = P["NT"]; isA = P["pi"] == 0; tcs = P["tcs"]; nblk = P["nblk"]
            phase1(l, P, lambda c: vecs[:, c, 0:1])
            R.fence([scrb[4], scrb[7]], scr47_att)
            rhsT = lambda k, c0, n: hT[:, k, c0:c0 + n]
            KT32, KT32b = scr[0], scrb[0]
            VT32, VT32b = scr[1], scrb[1]
            Gs = [(scr[2], scrb[2]), (scr[3], scrb[3])]
            tok0 = 0 if isA else 1024
            wviews = {}

            def load_qk(hd):
                wviews[(hd, 0)] = load_w([wv[:, :, hd * 128:(hd + 1) * 128], wv[:, :, 2048 + hd * 128:2048 + (hd + 1) * 128]], NCH)

            def load_vg(hd):
                wviews[(hd, 1)] = load_w([wv[:, :, 4096 + hd * 128:4096 + (hd + 1) * 128],
                                          wv[:, :, 6144 + hd * 128:6144 + (hd + 1) * 128]], NCH)

            def dma_K(hd, cc):
                s = cc % 2; k0 = 1536 - 512 * cc
                for b in range(4):
                    ev = R.dma(kcbch[s], kraw[s][:, b * 512:(b + 1) * 512].rearrange("p (k d) -> p k d", d=128),
                               ck_d[j, b, k0:k0 + 512, hd * 128:(hd + 1) * 128].rearrange("(k p) d -> p k d", p=128),
                               writes=[krawb[s][b]], eng="pool")
                for b in range(4):
                    krawb[s][b].w = ev

            def dma_V(hd, cc):
                s = cc % 2; k0 = 1536 - 512 * cc
                for b in range(4):
                    ev = R.dma(vcbch[s], vch[s][:, b * 512:(b + 1) * 512].rearrange("p (k d) -> p k d", d=128),
                               cv_d[j, b, k0:k0 + 512, hd * 128:(hd + 1) * 128].rearrange("(k p) d -> p k d", p=128),
                               writes=[vchb[s][b]], eng="pool")
                for b in range(4):
                    vchb[s][b].w = ev

            def prepT(cc):
                s = cc % 2
                for half in range(2):
                    for bb in range(2):
                        for blk in range(4):
                            idx = (2 * half + bb) * 4 + blk
                            X("pe", "transpose", PK[half][:, (bb * 4 + blk) * 128:(bb * 4 + blk + 1) * 128],
                              kraw[s][:, idx * 128:(idx + 1) * 128], ident_b[:, :], r=[krawb[s][2 * half + bb], identbb], w=[PKb[half]])
                    X("dve", "tensor_copy",
                      KcTc[s][:, 2 * half * 512:(2 * half + 2) * 512].rearrange("p (b k) -> p b k", k=512)[:, :, ::-1],
                      PK[half][:, 0:1024].rearrange("p (b k) -> p b k", k=512), r=[PKb[half]], w=[KcTcb[s]])

            def head_groups(hd):
                p = hd % 2
                G, Gb = Gs[p]
                groups = []

                def evq(tci, c0, n, bk):
                    if c0 < 1024:
                        X("act", "mul", QT[p][:, c0:c0 + n], PS[bk][:, 0:n], SCALE, r=[PSb[bk]], w=[QTb[p]])
                    else:
                        X("act", "mul", Qpad[p][:, 0:640].rearrange("p (b t) -> p b t", t=160)[:, :, 0:32],
                          PS[bk][:, 0:128].rearrange("p (b t) -> p b t", t=32), SCALE, r=[PSb[bk]], w=[Qpadb[p]])

                def evk(tci, c0, n, bk):
                    X("act", "copy", KT32[:, c0:c0 + n], PS[bk][:, 0:n], r=[PSb[bk]], w=[KT32b])
                    if c0 < 1024:
                        hi = 2047 - (tok0 + c0)
                        X("dve", "tensor_copy", KTr[p][:, hi - n + 1:hi + 1][:, ::-1], KT32[:, c0:c0 + n], r=[KT32b], w=[KTrb[p]])
                    else:
                        X("dve", "tensor_copy", KSr[p][:, 0:128].rearrange("p (b t) -> p b t", t=32)[:, :, ::-1],
                          KT32[:, 1024:1152].rearrange("p (b t) -> p b t", t=32), r=[KT32b], w=[KSrb[p]])

                def evv(tci, c0, n, bk):
                    X("act", "copy", VT32[:, c0:c0 + n], PS[bk][:, 0:n], r=[PSb[bk]], w=[VT32b])

                def evg(tci, c0, n, bk):
                    X("act", "activation", G[:, c0:c0 + n], PS[bk][:, 0:n], AF.Sigmoid, r=[PSb[bk]], w=[Gb])
                    X("dve", "tensor_tensor", G[:, c0:c0 + n], G[:, c0:c0 + n], PS[bk][:, 0:n], ALU.mult,
                      r=[PSb[bk], Gb], w=[Gb])

                if not isA:
                    def ld():
                        R.dma(cch_kv[0][p], KTr[p][:, 1024:2048], KTS[j][hd], reads=[KTSb[j][hd]], writes=[KTrb[p]])
                        R.dma(cch_kv[1][p], Vbo[p][:, :], VSS[j][hd], reads=[VSSb[j][hd]], writes=[Vbob[p]])
                    groups.append(ld)
                pctr_ = [0]
                for wi, pidx, ev in [(0, 0, evq), (0, 1, evk), (1, 0, evv), (1, 1, evg)]:
                    for tci, tc in enumerate(tcs):
                        def g(wi=wi, pidx=pidx, ev=ev, tci=tci, tc=tc):
                            view, wb = wviews[(hd, wi)]
                            bk = [4, 5][pctr_[0] % 2]; pctr_[0] += 1
                            proj(view, pidx, wb[pidx], NCH, rhsT, [hTb], [tc], [bk], lambda _t, c0, n, bk_: ev(tci, c0, n, bk_))
                        groups.append(g)
                    if (wi, pidx) == (0, 1) and hd + 1 < 16:
                        groups.append(lambda: load_qk(hd + 1))
                    if (wi, pidx) == (1, 1) and hd + 1 < 16:
                        groups.append(lambda: load_vg(hd + 1))
                for which, (src32, src32b, outp, outs) in enumerate([(KT32, KT32b, kp_o, ksn_o), (VT32, VT32b, vp_o, vsn_o)]):
                    kt = ktok[which]; ktb = ktokb[which]
                    for g0 in range(0, nblk, 4):
                        def g(g0=g0, src32=src32, src32b=src32b, kt=kt, ktb=ktb):
                            g1 = min(nblk, g0 + 4)
                            for i, tb in enumerate(range(g0, g1)):
                                X("pe", "transpose", PS[6][:, i * 128:(i + 1) * 128], src32[:, tb * 128:(tb + 1) * 128], ident_f[:, :],
                                  r=[src32b, identfb], w=[PSb[6]])
                            X("act", "copy", kt[:, g0:g1, :], PS[6][:, 0:(g1 - g0) * 128].rearrange("p (b d) -> p b d", d=128),
                              r=[PSb[6]], w=[ktb])
                        groups.append(g)

                    def st(which=which, kt=kt, ktb=ktb, outp=outp, outs=outs):
                        R.dma(ktokch[which], outp[j, tok0:tok0 + 1024, hd * 128:(hd + 1) * 128].rearrange("(b p) d -> p b d", p=128),
                              kt[:, 0:8, :], reads=[ktb])
                        if isA:
                            R.dma(ktokch[which], outs[j, :, hd * 128:(hd + 1) * 128], kt[:, 8, :], reads=[ktb])
                        if which == 1:
                            X("dve", "tensor_copy", Vb[p][:, 0:nblk * 128].rearrange("p (b d) -> p b d", d=128), kt[:, 0:nblk, :],
                              r=[ktb], w=[Vbb[p]])
                    groups.append(st)
                if isA:
                    def fin_a():
                        R.dma(cch_kv[2][p], KTS[j][hd], KTr[p][:, 1024:2048], reads=[KTrb[p]], writes=[KTSb[j][hd]])
                        R.dma(cch_kv[3][p], VSS[j][hd], Vb[p][:, 0:1024], reads=[Vbb[p]], writes=[VSSb[j][hd]])
                        for b in range(4):
                            X("pe", "transpose", PS[6][0:32, b * 128:(b + 1) * 128], VT32[:, 1024 + 32 * b:1056 + 32 * b], ident_f[:, :],
                              r=[VT32b, identfb], w=[PSb[6]])
                        X("act", "copy", Vsn[p][0:32, 0:512], PS[6][0:32, 0:512], r=[PSb[6]], w=[Vsnb[p]])
                    groups.append(fin_a)
                return groups

            X("dve", "memset", Qpad[1][:, :], 0.0, w=[Qpadb[1]])
            if isA:
                dma_K(0, 0); dma_V(0, 0); dma_K(0, 1); dma_V(0, 1)
            load_qk(0); load_vg(0)
            for g in head_groups(0):
                g()
            for hd in range(16):
                G, Gb = Gs[hd % 2]
                filler = head_groups(hd + 1) if hd + 1 < 16 else []
                Pq = [prompt_tasks(hd, ql, isA, G, Gb) for ql in range(8)]
                if isA:
                    S = sample_tasks(hd, G, Gb)
                    nxt = hd + 1 < 16
                    tasks = Pq[0] + [S[0]] + Pq[1] + [S[1]] + Pq[2] + Pq[3] + [S[2]] + Pq[4] + Pq[5] + [S[3]] + Pq[6] + Pq[7] + [S[4]]
                    pre = [lambda: prepT(0), lambda hd=hd: dma_K(hd, 2), lambda: prepT(1), lambda hd=hd: dma_K(hd, 3)]
                    S[1]["post"].append(lambda: prepT(2))
                    S[2]["post"].append(lambda: prepT(3))
                    S[1]["postF"].append(lambda hd=hd: dma_V(hd, 2))
                    S[2]["postF"].append(lambda hd=hd: dma_V(hd, 3))
                    if nxt:
                        S[1]["post"].append(lambda hd=hd: dma_K(hd + 1, 0))
                        S[2]["post"].append(lambda hd=hd: dma_K(hd + 1, 1))
                        S[3]["postF"].append(lambda hd=hd: dma_V(hd + 1, 0))
                        S[4]["postF"].append(lambda hd=hd: dma_V(hd + 1, 1))
                    run_pipeline(tasks, pre, filler)
                else:
                    run_pipeline([T for q in Pq for T in q], (), filler)
            R.fence(scr47_att, [scrb[4], scrb[7]])
            phase3(l, P, w_out_att[j], 16)

        cch_kv = [[R.chan(), R.chan()] for _ in range(4)]

        def final_phase(l):
            load_rows_T_safe(vecs, vecsb, [(norm_final, 1)], D)
            R.fence(att_alias_bufs, [xa2b])
            for P in PASSES:
                NT = P["NT"]; col0 = P["col0"]; pi = P["pi"]
                stats(l, P)
                rstd = scr[7]
                for tb in range(P["nblk"]):
                    for hf in range(2):
                        if tb % 2 == 0:
                            xb_ = scr[hf]; xbb = scrb[hf]; xch = scrch[hf]
                        else:
                            xb_, xbb, xch = [(scr[6], scrb[6], scrch[6]), (xa2, xa2b, xa2ch)][hf]
                        yn = scr[2 + hf]; ynb = scrb[2 + hf]
                        yt = scr[4 + hf]; ytb = scrb[4 + hf]
                        R.dma(xch, xb_[:, 0:1024].rearrange("p (c t) -> p c t", t=128),
                              XS[l][hf * 8:(hf + 1) * 8, :, col0 + tb * 128:col0 + (tb + 1) * 128].rearrange("c p t -> p c t"),
                              reads=[XSb[l][c][pi] for c in range(hf * 8, hf * 8 + 8)], writes=[xbb])
                        for ci in range(8):
                            c = hf * 8 + ci
                            X("dve", "scalar_tensor_tensor", yn[:, ci * 128:(ci + 1) * 128], xb_[:, ci * 128:(ci + 1) * 128],
                              vecs[:, c, 0:1], rstd[:, tb * 128:(tb + 1) * 128], ALU.mult, ALU.mult,
                              r=[xbb, vecsb, scrb[7]], w=[ynb])
                        for g in range(2):
                            bk = hf * 2 + g
                            for i in range(4):
                                ci = g * 4 + i
                                X("pe", "transpose", PS[bk][:, i * 128:(i + 1) * 128], yn[:, ci * 128:(ci + 1) * 128], ident_f[:, :],
                                  r=[ynb, identfb], w=[PSb[bk]])
                            X("act", "copy", yt[:, g * 512:(g + 1) * 512], PS[bk][:, :], r=[PSb[bk]], w=[ytb])
                        if pi == 0 and tb == 8:
                            dst = ys_o[:, hf * 1024:(hf + 1) * 1024]
                        else:
                            t0 = tb * 128 + (0 if pi == 0 else 1024)
                            dst = yp_o[t0:t0 + 128, hf * 1024:(hf + 1) * 1024]
                        R.dma(scrch[4 + hf], dst, yt[:, 0:1024], reads=[ytb])

        init_phase()
        if ltypes is None:
            ltypes = "rara"[:nlayers]
        nlayers = len(ltypes)
        cnt = {"r": 0, "a": 0}
        for l, t in enumerate(ltypes):
            if t == "r":
                rec_layer(l, cnt["r"])
            else:
                att_layer(l, cnt["a"])
            cnt[t] += 1
        final_phase(nlayers)
        R.emit()
        build.stats = R.stats
    return nc


_CACHE = {}


def _consts():
    ident = np.eye(128, dtype=np.float32)
    p = np.arange(128)[:, None]
    i = np.arange(512)[None, :]
    dmask = np.where((i < 128) & (i + p < 128), NEG, 0.0).astype(np.float32)
    t = (np.arange(128) % 32)[:, None]
    ii = np.arange(32)[None, :]
    smask = np.where(ii + t < 32, NEG, 0.0).astype(np.float32)
    invc = np.zeros((128, 64), np.float32)
    for g, w in enumerate((2, 4, 8, 16)):
        for tt in range(16):
            invc[:, g * 16 + tt] = 1.0 / min(w, tt + 1)
    return ident, dmask, smask, invc


def kernel(x_prompt, x_sample, cache_k, cache_v, state_h, state_conv, state_pool,
           norm_rec, w_in_rec, conv_w, conv_b, gate_r_w, gate_r_b, gate_i_w, gate_i_b, rg_lambda,
           pool_w, pool_scale, w_out_rec, norm_att, w_in_att, w_out_att, norm_final):
    f = lambda a: np.ascontiguousarray(np.asarray(a, dtype=np.float32))
    if "nc" not in _CACHE:
        _CACHE["nc"] = build()
    nc = _CACHE["nc"]
    ident, dmask, smask, invc = _consts()
    x_prompt = f(x_prompt); x_sample = f(x_sample); cache_k = f(cache_k); cache_v = f(cache_v)
    state_h = f(state_h); state_conv = f(state_conv); state_pool = f(state_pool)
    shared = dict(norm_rec=f(norm_rec), w_in_rec=f(w_in_rec), conv_w=f(conv_w), conv_b=f(conv_b),
                  gate_r_w=f(gate_r_w), gate_r_b=f(gate_r_b), gate_i_w=f(gate_i_w), gate_i_b=f(gate_i_b),
                  rg_lambda=f(rg_lambda), pool_w=f(pool_w), pool_scale=f(pool_scale), w_out_rec=f(w_out_rec),
                  norm_att=f(norm_att), w_in_att=f(w_in_att), w_out_att=f(w_out_att),
                  norm_final=f(norm_final).reshape(1, D), ident=ident, dmask=dmask, smask=smask, invc=invc)
    in_maps = []
    for c in range(8):
        sl = slice(4 * c, 4 * c + 4)
        m = dict(shared)
        m["xp"] = x_prompt[c % 4]
        m["xs"] = x_sample[sl].reshape(128, D)
        m["ck"] = cache_k[:, sl].reshape(2, 4, 2048, D)
        m["cv"] = cache_v[:, sl].reshape(2, 4, 2048, D)
        m["sh"] = state_h[:, sl]
        m["sc"] = state_conv[:, sl].reshape(2, 12, D)
        m["spl"] = state_pool[:, sl].reshape(2, 60, 1024)
        in_maps.append({k: np.ascontiguousarray(v) for k, v in m.items()})
    res = run_bass_kernel_spmd(nc, in_maps, core_ids=list(range(8)))
    rs = res.results
    y_prompt = np.stack([rs[c]["yp"] for c in range(4)])
    y_sample = np.concatenate([rs[c]["ys"].reshape(4, 32, D) for c in range(8)])
    k_prompt = np.stack([rs[c]["kp"] for c in range(4)], axis=1).reshape(2, 4, 2048, 16, 128)
    v_prompt = np.stack([rs[c]["vp"] for c in range(4)], axis=1).reshape(2, 4, 2048, 16, 128)
    h_prompt = np.stack([rs[c]["hp"] for c in range(4)], axis=1)
    conv_prompt = np.stack([rs[c]["cp"] for c in range(4)], axis=1)
    pool_prompt = np.stack([rs[c]["pp"] for c in range(4)], axis=1)
    k_sample = np.concatenate([rs[c]["ksn"].reshape(2, 4, 32, 16, 128) for c in range(8)], axis=1)
    v_sample = np.concatenate([rs[c]["vsn"].reshape(2, 4, 32, 16, 128) for c in range(8)], axis=1)
    h_sample = np.concatenate([rs[c]["hsn"] for c in range(8)], axis=1)
    conv_sample = np.concatenate([rs[c]["csn"].reshape(2, 4, 3, D) for c in range(8)], axis=1)
    pool_sample = np.concatenate([rs[c]["psn"].reshape(2, 4, 15, 1024) for c in range(8)], axis=1)
    outs = (y_prompt, y_sample, k_prompt, v_prompt, h_prompt, conv_prompt, pool_prompt,
            k_sample, v_sample, h_sample, conv_sample, pool_sample)
    return tuple(np.ascontiguousarray(o, dtype=np.float32) for o in outs)
```

```python
import contextlib
import os
import numpy as np
import concourse.bass as bass
import concourse.mybir as mybir
from concourse.bass_utils import run_bass_kernel_spmd

F32 = mybir.dt.float32
BF16 = mybir.dt.bfloat16
ALU = mybir.AluOpType
AF = mybir.ActivationFunctionType

ENGS = ("pe", "act", "dve", "pool", "sp")
NEG = -10000.0
EPS = 1e-6
D = 2048
NCH = 16
NTA = 1152
NTB = 1024
NTT = NTA + NTB
SCALE = 128 ** -0.5
DEPTH = 4
KMODE = 31


class Buf:
    __slots__ = ("name", "w", "r")

    def __init__(self, name=""):
        self.name = name
        self.w = None
        self.r = {}


class Chan:
    def __init__(self, sem):
        self.sem = sem
        self.cnt = 0


class Rec:
    def __init__(self, nc, stack):
        self.nc = nc
        self.stack = stack
        self.ops = {e: [] for e in ENGS}
        self.esem = {e: stack.enter_context(nc.semaphore("es_" + e)) for e in ENGS}
        self.chans = []

    def chan(self, name=None):
        c = Chan(self.stack.enter_context(self.nc.semaphore(name or ("ch%d" % len(self.chans)))))
        self.chans.append(c)
        return c

    def _add(self, eng, fn, reads, writes, chan=None):
        deps = []
        for b in reads:
            if b.w is not None:
                deps.append(b.w)
        for b in writes:
            if b.w is not None:
                deps.append(b.w)
            deps.extend(b.r.values())
        ops = self.ops[eng]
        idx = len(ops)
        if chan is not None:
            chan.cnt += 16
            ev = ("d", chan, chan.cnt)
            key = ("d", id(chan))
        else:
            ev = ("e", eng, idx)
            key = ("e", eng)
        ops.append({"fn": fn, "deps": deps, "chan": chan, "inc": False})
        for b in reads:
            b.r[key] = ev
        for b in writes:
            b.w = ev
            b.r = {}
        return ev

    def op(self, eng, fn, reads=(), writes=()):
        return self._add(eng, fn, reads, writes)

    def dma(self, chan, out, in_, reads=(), writes=(), eng="sp"):
        def fn(e, out=out, in_=in_):
            return e.dma_start(out=out, in_=in_)
        return self._add(eng, fn, reads, writes, chan=chan)

    def fence(self, srcs, dsts):
        snap = [(id(s), s.w, list(s.r.items())) for s in srcs]
        for d in dsts:
            for sid, w, items in snap:
                if w is not None:
                    d.r[("w", sid)] = w
                for k, ev in items:
                    d.r[(k, sid)] = ev

    def emit(self):
        nc = self.nc
        for eng in ENGS:
            for o in self.ops[eng]:
                nd = []
                for d in o["deps"]:
                    if d[0] == "e":
                        if d[1] == "pe" and eng == "pe":
                            continue
                        self.ops[d[1]][d[2]]["inc"] = True
                    nd.append(d)
                o["deps"] = nd
        val = {}
        for eng in ENGS:
            c = 0
            v = []
            for o in self.ops[eng]:
                if o["inc"] and o["chan"] is None:
                    c += 1
                v.append(c)
            val[eng] = v
        self.stats = {e: len(self.ops[e]) for e in ENGS}
        engobj = {"pe": "tensor", "act": "scalar", "dve": "vector", "pool": "gpsimd", "sp": "sync"}
        with nc.Block() as block:
            for eng in ENGS:
                ops = self.ops[eng]
                if not ops and eng != "sp":
                    continue

                def body(e, eng=eng, ops=ops):
                    known = {}
                    for o in ops:
                        need = {}
                        for d in o["deps"]:
                            if d[0] == "e":
                                sem = self.esem[d[1]]
                                v = val[d[1]][d[2]]
                            else:
                                sem = d[1].sem
                                v = d[2]
                            k = id(sem)
                            if known.get(k, 0) >= v:
                                continue
                            if k not in need or need[k][1] < v:
                                need[k] = (sem, v)
                        for k, (sem, v) in need.items():
                            e.wait_ge(sem, v)
                            known[k] = v
                        ins = o["fn"](e)
                        if o["chan"] is not None:
                            ins.then_inc(o["chan"].sem, 16)
                        elif o["inc"]:
                            ins.then_inc(self.esem[eng], 1)
                    if eng == "sp":
                        for c in self.chans:
                            if c.cnt > 0:
                                e.wait_ge(c.sem, c.cnt)
                getattr(block, engobj[eng])(body)


def build(nlayers=DEPTH, ltypes=None):
    nc = bass.Bass("TRN2", target_bir_lowering=False)

    def din(name, shape, dt=F32):
        return nc.dram_tensor(name, list(shape), dt, kind="ExternalInput").ap()

    def dout(name, shape):
        return nc.dram_tensor(name, list(shape), F32, kind="ExternalOutput").ap()

    def dscr(name, shape, dt):
        return nc.dram_tensor(name, list(shape), dt, kind="Internal").ap()

    xp_d = din("xp", [2048, D]); xs_d = din("xs", [128, D])
    ck_d = din("ck", [2, 4, 2048, D]); cv_d = din("cv", [2, 4, 2048, D])
    sh_d = din("sh", [2, 4, D]); sc_d = din("sc", [2, 12, D]); spl_d = din("spl", [2, 60, 1024])
    norm_rec = din("norm_rec", [2, D]); w_in_rec = din("w_in_rec", [2, D, 6144])
    conv_w = din("conv_w", [2, 4, D]); conv_b = din("conv_b", [2, D])
    gate_r_w = din("gate_r_w", [2, 16, 128, 128]); gate_r_b = din("gate_r_b", [2, D])
    gate_i_w = din("gate_i_w", [2, 16, 128, 128]); gate_i_b = din("gate_i_b", [2, D])
    rg_lambda = din("rg_lambda", [2, D]); pool_w = din("pool_w", [2, 4, 256, 256])
    pool_scale = din("pool_scale", [2, 1024]); w_out_rec = din("w_out_rec", [2, 3072, D])
    norm_att = din("norm_att", [2, D]); w_in_att = din("w_in_att", [2, D, 8192])
    w_out_att = din("w_out_att", [2, D, D]); norm_final = din("norm_final", [1, D])
    ident_d = din("ident", [128, 128]); dmask_d = din("dmask", [128, 512]); smask_d = din("smask", [128, 32])
    invc_d = din("invc", [128, 64])

    yp_o = dout("yp", [2048, D]); ys_o = dout("ys", [128, D])
    kp_o = dout("kp", [2, 2048, D]); vp_o = dout("vp", [2, 2048, D])
    hp_o = dout("hp", [2, D]); cp_o = dout("cp", [2, 3, D]); pp_o = dout("pp", [2, 15, 1024])
    ksn_o = dout("ksn", [2, 128, D]); vsn_o = dout("vsn", [2, 128, D])
    hsn_o = dout("hsn", [2, 4, D]); csn_o = dout("csn", [2, 12, D]); psn_o = dout("psn", [2, 60, 1024])

    XS = [dscr("xscr%d" % l, [NCH, 128, NTT], F32) for l in range(DEPTH + 1)]
    XSb = [[[Buf() for _ in range(2)] for _ in range(NCH)] for _ in range(DEPTH + 1)]
    KTS = [dscr("kts%d" % j, [16, 128, 1024], BF16) for j in range(2)]
    VSS = [dscr("vss%d" % j, [16, 128, 1024], BF16) for j in range(2)]
    KTSb = [[Buf() for _ in range(16)] for _ in range(2)]
    VSSb = [[Buf() for _ in range(16)] for _ in range(2)]

    PASSES = [dict(name="A", NT=NTA, col0=0, tcs=[(0, 512), (512, 512), (1024, 128)], nblk=9, pi=0),
              dict(name="B", NT=NTB, col0=NTA, tcs=[(0, 512), (512, 512)], nblk=8, pi=1)]

    with contextlib.ExitStack() as st:
        R = Rec(nc, st)

        def SB(name, shape, dt=F32):
            return st.enter_context(nc.sbuf_tensor("s_" + name, list(shape), dt))

        def X(eng, meth, *args, r=(), w=(), **kw):
            R.op(eng, lambda e: getattr(e, meth)(*args, **kw), reads=r, writes=w)

        PS = [st.enter_context(nc.psum_tensor("ps%d" % i, [128, 512], F32)) for i in range(7)]
        PSb = [Buf("ps%d" % i) for i in range(7)]
        PT = st.enter_context(nc.psum_tensor("pt", [128, 1024], BF16))
        PTh = [PT[:, 0:512], PT[:, 0:512]]
        PThb = [Buf("pt0")] * 2
        PTf = PT[:, :].bitcast(F32)
        PK = [PS[6][:, :].bitcast(BF16), PS[6][:, :].bitcast(BF16)]
        PKb = [PSb[6], PSb[6]]
        po = [PS[2][:, 0:128], PS[3][:, 0:128]]
        pob = [PSb[2], PSb[3]]

        hT = SB("hT", [128, NCH, NTA], BF16); hTb = Buf("hT")
        mixT = SB("mixT", [128, 24 * NTA], BF16)
        mixb = [Buf("mix%d" % i) for i in range(24)]

        def mixv(f):
            return mixT[:, f * NTA:(f + 1) * NTA]

        NSCR = 8
        SCW = 1232
        xa2 = SB("xa2", [128, SCW]); xa2b = Buf("xa2"); xa2ch = R.chan()
        scr = [SB("scr%d" % i, [128, SCW]) for i in range(NSCR)]
        scrb = [Buf("scr%d" % i) for i in range(NSCR)]
        scrch = [R.chan() for _ in range(NSCR)]
        ab = 16 * NTA
        def alias(off, n):
            return mixT[:, ab + off: ab + off + n]
        QT = alias(0, 1152); KTr = alias(1152, 2048); Vb = alias(3200, 1152); Vbo = alias(4352, 1024)
        KcT0 = alias(5376, 2048); Wn0 = alias(7424, 512); WT0 = alias(7936, 512); Vsn = alias(8448, 512)
        KSr = alias(8960, 128)
        QT = [QT[:, 0:1024], xa2[:, 0:512].bitcast(BF16)]; QTb = [Buf(), Buf()]
        KTr = [KTr, scr[4][:, 0:1024].bitcast(BF16)]; KTrb = [Buf(), Buf()]
        Vb = [Vb, scr[7][:, 0:576].bitcast(BF16)]; Vbb = [Buf(), Buf()]
        Vbo = [Vbo, scr[7][:, 576:1088].bitcast(BF16)]; Vbob = [Buf(), Buf()]
        Vsn = [Vsn, xa2[:, 832:1088].bitcast(BF16)]; Vsnb = [Buf(), Buf()]
        KSr = [KSr, alias(9088, 128)]; KSrb = [Buf(), Buf()]
        sacc = xa2[:, 1088:1216]; saccb = Buf()
        U = SB("U", [128, 10240], BF16)
        kraw = [U[:, i * 2048:(i + 1) * 2048] for i in range(2)]; krawb = [[Buf() for _ in range(4)] for _ in range(2)]
        vch = [U[:, 4096 + i * 2048:4096 + (i + 1) * 2048] for i in range(2)]; vchb = [[Buf() for _ in range(4)] for _ in range(2)]
        KcTc = [KcT0, U[:, 8192:10240]]; KcTcb = [Buf(), Buf()]
        Wn = [Wn0, SB("Wn1", [128, 512], BF16)]; Wnb = [Buf(), Buf()]
        WT = [WT0, SB("WT1", [128, 512], BF16)]; WTb = [Buf(), Buf()]
        Qpad = [SB("Qpad", [128, 640], BF16), xa2[:, 512:832].bitcast(BF16)]; Qpadb = [Buf(), Buf()]
        att_alias_bufs = ([QTb[0], KTrb[0], Vbb[0], Vbob[0], Vsnb[0]] + KSrb + krawb[0] + krawb[1] + vchb[0] + vchb[1] + KcTcb + Wnb + WTb
                          + [QTb[1], Vsnb[1], Qpadb[1], saccb])
        scr47_att = [KTrb[1], Vbb[1], Vbob[1]]

        hb16 = [U[:, 6144 + i * NTA:6144 + (i + 1) * NTA] for i in range(3)]
        hb16b = [Buf() for _ in range(3)]
        wslot = [SB("wslot%d" % i, [128, 4096], BF16) for i in range(2)]
        wslotb = [[Buf(), Buf()] for _ in range(2)]
        wslotch = [[R.chan(), R.chan()] for _ in range(2)]
        wctr = [0]
        gw = U[:, 0:4096].rearrange("p (a n d) -> p a n d", a=2, n=16); gwb = Buf(); gwch = R.chan()
        pw = U[:, 4096:6144].rearrange("p (g i d) -> p g i d", g=4, i=2); pwb = Buf(); pwch = R.chan()
        rec_alias_bufs = [gwb, pwb] + hb16b + [xa2b]
        ktok = [scr[5][:, 0:1152].rearrange("p (b d) -> p b d", d=128), scr[6][:, 0:1152].rearrange("p (b d) -> p b d", d=128)]
        ktokb = [scrb[5], scrb[6]]
        ktokch = [scrch[5], scrch[6]]
        kcbch = [R.chan(), R.chan()]; vcbch = [R.chan(), R.chan()]
        nbt = [SB("nbt%d" % i, [128, 512]) for i in range(2)]; nbtb = [Buf(), Buf()]
        bt = [SB("bt%d" % i, [128, 512]) for i in range(2)]; btb = [Buf(), Buf()]
        scarry = SB("scarry", [128, 1]); scarryb = Buf()
        dmask_b = SB("dmask_b", [128, 128], BF16); dmaskbb = Buf()
        smask_b = SB("smask_b", [128, 32], BF16); smaskbb = Buf()
        ptile = [SB("ptile%d" % i, [128, 513]) for i in range(2)]
        ptileb = [Buf(), Buf()]
        zeros_f = SB("zeros_f", [128, 512]); zerosb = Buf()
        ident_f = SB("ident_f", [128, 128]); identfb = Buf()
        ident_b = SB("ident_b", [128, 128], BF16); identbb = Buf()
        ones_f = SB("ones_f", [128, 128]); onesb = Buf()
        dmask = SB("dmask", [128, 128]); dmaskb = Buf()
        smask = SB("smask", [128, 32]); smaskb = Buf()
        invc = SB("invc", [128, 64]); invcb = Buf()
        eps_t = SB("eps_t", [128, 1]); epsb = Buf()
        stg = SB("stg", [64, 1024]); stgch = R.chan()
        ost = SB("ost", [64, 1024]); ostb = Buf(); ostch = R.chan()
        vecs = SB("vecs", [128, NCH, 12]); vecsb = Buf()
        psc = SB("psc", [128, 8, 1]); pscb = Buf()
        clt = SB("clt", [128, NCH, 2]); cltb = Buf()
        shT = SB("shT", [128, NCH, 4]); shTb = Buf()
        scT = SB("scT", [128, NCH, 12]); scTb = Buf()
        spT = SB("spT", [128, 8, 60]); spTb = Buf()
        CT = SB("CT", [128, NCH, 3]); CTb = Buf()
        CH = SB("CH", [128, NCH]); CHb = Buf()
        CP = SB("CP", [128, 8, 15]); CPb = Buf()
        STt = SB("STt", [128, NCH, 16]); STtb = Buf()
        STp = SB("STp", [128, 8, 60]); STpb = Buf()
        cch = R.chan()

        R.dma(cch, ident_f[:], ident_d, writes=[identfb])
        R.dma(cch, dmask[:], dmask_d[:, 0:128], writes=[dmaskb])
        R.dma(cch, smask[:], smask_d, writes=[smaskb])
        evc = R.dma(cch, invc[:], invc_d, writes=[invcb])
        for b_ in (identfb, dmaskb, smaskb, invcb):
            b_.w = evc
        X("dve", "tensor_copy", ident_b[:], ident_f[:], r=[identfb], w=[identbb])
        X("dve", "memset", ones_f[:], 1.0, w=[onesb])
        X("dve", "memset", zeros_f[:], 0.0, w=[zerosb])
        X("dve", "memset", eps_t[:], EPS, w=[epsb])
        X("dve", "memset", Qpad[0][:], 0.0, w=[Qpadb[0]])
        X("dve", "tensor_copy", dmask_b[:], dmask[:], r=[dmaskb], w=[dmaskbb])
        X("dve", "tensor_copy", smask_b[:], smask[:], r=[smaskb], w=[smaskbb])

        stg_prev = []

        def load_rows_T_safe(dst, dstb, srcs, ncols):
            nonlocal stg_prev
            for h0 in range(0, ncols, 1024):
                bufs = []
                r0 = 0
                for ap, nr in srcs:
                    b = Buf()
                    R.fence(stg_prev, [b])
                    R.dma(stgch, stg[r0:r0 + nr, 0:1024], ap[:, h0:h0 + 1024], writes=[b])
                    bufs.append(b)
                    r0 += nr
                nrows = r0
                cb0 = h0 // 128
                for i in range(8):
                    X("pe", "transpose", PS[6][0:128, i * nrows:(i + 1) * nrows], stg[0:nrows, i * 128:(i + 1) * 128],
                      ident_f[0:nrows, 0:nrows], r=bufs + [identfb], w=[PSb[6]])
                X("act", "copy", dst[:, cb0:cb0 + 8, 0:nrows], PS[6][:, 0:8 * nrows].rearrange("p (g r) -> p g r", r=nrows),
                  r=[PSb[6]], w=[dstb])
                stg_prev = bufs

        def store_rows(src, srcb, nch, ncol, dests):
            for h0 in range(0, nch, 8):
                for g0 in range(h0, h0 + 8, 4):
                    for i in range(4):
                        X("pe", "transpose", PS[6][0:ncol, i * 128:(i + 1) * 128], src[:, g0 + i, 0:ncol], ident_f[:, :],
                          r=[srcb, identfb], w=[PSb[6]])
                    X("act", "copy", ost[0:ncol, (g0 - h0) * 128:(g0 - h0 + 4) * 128], PS[6][0:ncol, 0:512], r=[PSb[6]], w=[ostb])
                for r0, nr, ap in dests:
                    R.dma(ostch, ap[:, h0 * 128:(h0 + 8) * 128], ost[r0:r0 + nr, 0:1024], reads=[ostb])

        def load_w(panels, K):
            s = wctr[0] % 2
            wctr[0] += 1
            flat = wslot[s]
            np_ = len(panels)
            view = flat[:, 0:K * 128 * np_].rearrange("p (k n) -> p k n", n=128 * np_)
            R.fence(wslotb[s], wslotb[s])
            for i, ap in enumerate(panels):
                R.dma(wslotch[s][i], view[:, :, i * 128:(i + 1) * 128], ap, writes=[wslotb[s][i]], eng="pool")
            return view, wslotb[s]

        def proj(view, pidx, wbuf, K, rhs_fn, rhs_bufs, tcs, banks, evac):
            for tci, (c0, n) in enumerate(tcs):
                bk = banks[tci % len(banks)]
                for k in range(K):
                    X("pe", "matmul", PS[bk][:, 0:n], view[:, k, pidx * 128:(pidx + 1) * 128], rhs_fn(k, c0, n),
                      start=(k == 0), stop=(k == K - 1), r=[wbuf] + rhs_bufs, w=[PSb[bk]])
                evac(tci, c0, n, bk)

        def init_phase():
            for tb in range(17):
                if tb < 8:
                    pi, col = 0, tb * 128
                elif tb < 16:
                    pi, col = 1, NTA + (tb - 8) * 128
                else:
                    pi, col = 0, 1024
                for hf in range(2):
                    src = (xp_d[tb * 128:(tb + 1) * 128, hf * 1024:(hf + 1) * 1024] if tb < 16
                           else xs_d[:, hf * 1024:(hf + 1) * 1024])
                    ii = (tb % 2) * 2 + hf; oi = 4 + ii
                    xin = scr[ii]; xinb = scrb[ii]
                    xo_ = scr[oi]; xob = scrb[oi]
                    R.dma(scrch[ii], xin[:, 0:1024], src, writes=[xinb])
                    for g in range(2):
                        bk = hf * 2 + g
                        for i in range(4):
                            c = g * 4 + i
                            X("pe", "transpose", PS[bk][:, i * 128:(i + 1) * 128], xin[:, c * 128:(c + 1) * 128], ident_f[:, :],
                              r=[xinb, identfb], w=[PSb[bk]])
                        if g == 0:
                            X("act", "copy", xo_[:, 0:512], PS[bk][:, :], r=[PSb[bk]], w=[xob])
                        else:
                            X("dve", "tensor_copy", xo_[:, 512:1024], PS[bk][:, :], r=[PSb[bk]], w=[xob])
                    dst = XS[0][hf * 8:(hf + 1) * 8, :, col:col + 128].rearrange("c p t -> p c t")
                    wb = [XSb[0][c][pi] for c in range(hf * 8, hf * 8 + 8)]
                    R.dma(scrch[oi], dst, xo_[:, 0:1024].rearrange("p (c t) -> p c t", t=128), reads=[xob], writes=wb)

        def stats(l, P):
            NT = P["NT"]; col0 = P["col0"]; pi = P["pi"]
            acc = scr[4]; accb = scrb[4]
            for c in range(NCH):
                s = c % 2
                R.dma(scrch[s], scr[s][:, 0:NT], XS[l][c, :, col0:col0 + NT], reads=[XSb[l][c][pi]], writes=[scrb[s]])
                if c == 0:
                    X("act", "activation", acc[:, 0:NT], scr[s][:, 0:NT], AF.Square, r=[scrb[s]], w=[accb])
                else:
                    X("act", "activation", scr[2 + s][:, 0:NT], scr[s][:, 0:NT], AF.Square, r=[scrb[s]], w=[scrb[2 + s]])
                    X("dve", "tensor_tensor", acc[:, 0:NT], acc[:, 0:NT], scr[2 + s][:, 0:NT], ALU.add,
                      r=[accb, scrb[2 + s]], w=[accb])
            rstd = scr[7]; rstdb = scrb[7]
            for tci, (c0, n) in enumerate(P["tcs"]):
                X("pe", "matmul", PS[tci][:, 0:n], ones_f[:, :], acc[:, c0:c0 + n], start=True, stop=True,
                  r=[onesb, accb], w=[PSb[tci]])
                X("act", "activation", rstd[:, c0:c0 + n], PS[tci][:, 0:n], AF.Sqrt, bias=eps_t[:, 0:1], scale=1.0 / D,
                  r=[PSb[tci], epsb], w=[rstdb])
            X("dve", "reciprocal", rstd[:, 0:NT], rstd[:, 0:NT], r=[rstdb], w=[rstdb])

        def phase1(l, P, gcol):
            NT = P["NT"]; col0 = P["col0"]; pi = P["pi"]
            stats(l, P)
            for c in range(NCH):
                s = c % 2
                R.dma(scrch[s], scr[s][:, 0:NT], XS[l][c, :, col0:col0 + NT], reads=[XSb[l][c][pi]], writes=[scrb[s]])
                X("dve", "scalar_tensor_tensor", hT[:, c, 0:NT], scr[s][:, 0:NT], gcol(c), scr[7][:, 0:NT], ALU.mult, ALU.mult,
                  r=[scrb[s], scrb[7], vecsb], w=[hTb])

        def phase3(l, P, wout, F):
            NT = P["NT"]; col0 = P["col0"]; pi = P["pi"]
            wv = wout.rearrange("(f p) n -> p f n", p=128)
            for j in range(NCH):
                view, wb = load_w([wv[:, :, j * 128:(j + 1) * 128]], F)
                s = j % 2
                R.dma(scrch[s], scr[s][:, 0:NT], XS[l][j, :, col0:col0 + NT], reads=[XSb[l][j][pi]], writes=[scrb[s]])

                def evac(tci, c0, n, bk, s=s):
                    X("dve", "tensor_tensor", scr[2 + s][:, c0:c0 + n], PS[bk][:, 0:n], scr[s][:, c0:c0 + n], ALU.add,
                      r=[PSb[bk], scrb[s]], w=[scrb[2 + s]])
                proj(view, 0, wb[0], F, lambda k, c0, n: mixv(k)[:, c0:c0 + n], [mixb[k] for k in range(F)], P["tcs"],
                     [0, 1, 2], evac)
                R.dma(scrch[2 + s], XS[l + 1][j, :, col0:col0 + NT], scr[2 + s][:, 0:NT], reads=[scrb[2 + s]],
                      writes=[XSb[l + 1][j][pi]])

        def rec_layer(l, j):
            load_rows_T_safe(vecs, vecsb, [(norm_rec[j:j + 1, :], 1), (conv_w[j], 4), (conv_b[j:j + 1, :], 1),
                                           (gate_r_b[j:j + 1, :], 1), (gate_i_b[j:j + 1, :], 1), (rg_lambda[j:j + 1, :], 1)], D)
            load_rows_T_safe(psc, pscb, [(pool_scale[j:j + 1, :], 1)], 1024)
            load_rows_T_safe(shT, shTb, [(sh_d[j], 4)], D)
            load_rows_T_safe(scT, scTb, [(sc_d[j], 12)], D)
            load_rows_T_safe(spT, spTb, [(spl_d[j], 60)], 1024)
            lam = vecs[:, :, 8]
            X("act", "activation", clt[:, :, 0], lam, AF.Exp, scale=-1.0, r=[vecsb], w=[cltb])
            X("act", "activation", clt[:, :, 0], clt[:, :, 0], AF.Ln, bias=ones_f[:, 0:1], r=[cltb, onesb], w=[cltb])
            X("dve", "tensor_scalar", clt[:, :, 1], clt[:, :, 0], -16.0, None, ALU.mult, r=[cltb], w=[cltb])
            X("dve", "tensor_scalar", clt[:, :, 0], clt[:, :, 0], -8.0, None, ALU.mult, r=[cltb], w=[cltb])
            R.dma(gwch, gw[:, 0], gate_r_w[j].rearrange("n c d -> c n d"), writes=[gwb], eng="pool")
            R.dma(gwch, gw[:, 1], gate_i_w[j].rearrange("n c d -> c n d"), writes=[gwb], eng="pool")
            R.dma(pwch, pw[:], pool_w[j].rearrange("g (i p) d -> p g i d", p=128), writes=[pwb], eng="pool")
            wv = w_in_rec[j].rearrange("(k p) n -> p k n", p=128)
            R.fence(att_alias_bufs, mixb[16:24] + rec_alias_bufs)
            for P in PASSES:
                rec_pass(l, j, P, wv)

        def rec_pass(l, j, P, wv):
            NT = P["NT"]; isA = P["pi"] == 0
            phase1(l, P, lambda c: vecs[:, c, 0:1])
            tcs = P["tcs"]
            xc, xcb_ = scr[1], scrb[1]
            rt, rtb = scr[2], scrb[2]
            it, itb = scr[3], scrb[3]
            At, Atb = scr[4], scrb[4]
            Ht, Htb = scr[5], scrb[5]
            Gs = [(scr[6], scrb[6]), (scr[7], scrb[7])]
            PADW = 3
            def seg_views(tile, pad, off):
                v = [tile[:, off:off + 1024]]
                if isA:
                    base = pad + 1024
                    v.append(tile[:, base:base + 4 * (32 + pad)].rearrange("p (b t) -> p b t", t=32 + pad)[:, :, off:off + 32])
                return v

            def out_views(tile):
                v = [tile[:, 0:1024]]
                if isA:
                    v.append(tile[:, 1024:1152].rearrange("p (b t) -> p b t", t=32))
                return v

            def evac_padded(dst, dstb, pad, eng="act"):
                def ev(tci, c0, n, bk):
                    if c0 < 1024:
                        X(eng, "copy" if eng == "act" else "tensor_copy", dst[:, pad + c0:pad + c0 + n], PS[bk][:, 0:n],
                          r=[PSb[bk]], w=[dstb])
                    else:
                        base = pad + 1024
                        X(eng, "copy" if eng == "act" else "tensor_copy",
                          dst[:, base:base + 4 * (32 + pad)].rearrange("p (b t) -> p b t", t=32 + pad)[:, :, pad:pad + 32],
                          PS[bk][:, 0:128].rearrange("p (b t) -> p b t", t=32), r=[PSb[bk]], w=[dstb])
                return ev

            def evac_silu(G, Gb):
                def ev(tci, c0, n, bk):
                    X("act", "activation", G[:, c0:c0 + n], PS[bk][:, 0:n], AF.Sigmoid, r=[PSb[bk]], w=[Gb])
                    X("dve", "tensor_tensor", G[:, c0:c0 + n], G[:, c0:c0 + n], PS[bk][:, 0:n], ALU.mult,
                      r=[PSb[bk], Gb], w=[Gb])
                return ev

            def set_pads(dst, dstb, pad, carry_ap, state_view):
                if isA:
                    X("dve", "memset", dst[:, 0:pad], 0.0, w=[dstb])
                    base = pad + 1024
                    X("dve", "tensor_copy",
                      dst[:, base:base + 4 * (32 + pad)].rearrange("p (b t) -> p b t", t=32 + pad)[:, :, 0:pad],
                      state_view[0], r=[state_view[1]], w=[dstb])
                else:
                    X("dve", "tensor_copy", dst[:, 0:pad], carry_ap[0], r=[carry_ap[1]], w=[dstb])

            rhsT = lambda k, c0, n: hT[:, k, c0:c0 + n]
            XAs = [(scr[0], scrb[0]), (xa2, xa2b)]
            gbanks = [(PS[6], PSb[6]), (PTf, PThb[0])]

            def pre(c):
                XA, XAb = XAs[c % 2]
                view, wb = load_w([wv[:, :, c * 128:(c + 1) * 128], wv[:, :, 2048 + c * 128:2048 + (c + 1) * 128]], NCH)
                G, Gb = Gs[c % 2]
                set_pads(XA, XAb, 3, (CT[:, c, :], CTb), (scT[:, c, :].rearrange("p (b k) -> p b k", k=3), scTb))
                proj(view, 0, wb[0], NCH, rhsT, [hTb], tcs, [0, 1, 2], evac_padded(XA, XAb, 3))
                proj(view, 1, wb[1], NCH, rhsT, [hTb], tcs, [3, 4, 5], evac_silu(G, Gb))

            H2 = [dict(lo=0, hi=512, tcis=[0]), dict(lo=512, hi=NT, tcis=list(range(1, len(tcs))))]
            hbufs = {k: [Buf(), Buf()] for k in ("xc", "rt", "it", "At", "Ht", "xh")}
            allh = [b_ for v in hbufs.values() for b_ in v]
            R.fence([scrb[1], scrb[2], scrb[3], scrb[4], scrb[5], hb16b[0]], allh)

            def post(c, part):
                XA, XAb = XAs[c % 2]
                G, Gb = Gs[c % 2]
                xh = hb16[0]
                cw = lambda k: vecs[:, c, 1 + k:2 + k]

                def conv(h):
                    lo, hi = H2[h]["lo"], min(H2[h]["hi"], 1024)
                    xb_ = hbufs["xc"][h]
                    segs = [(lambda k: XA[:, k + lo:k + hi], xc[:, lo:hi])]
                    if h == 1 and isA:
                        segs.append((lambda k: XA[:, 1027:1167].rearrange("p (b t) -> p b t", t=35)[:, :, k:k + 32],
                                     xc[:, 1024:1152].rearrange("p (b t) -> p b t", t=32)))
                    for iv, ov in segs:
                        X("dve", "tensor_scalar", ov, iv(0), cw(0), vecs[:, c, 5:6], ALU.mult, ALU.add, r=[XAb, vecsb], w=[xb_])
                    for k in range(1, 4):
                        for iv, ov in segs:
                            X("dve", "scalar_tensor_tensor", ov, iv(k), cw(k), ov, ALU.mult, ALU.add, r=[XAb, vecsb, xb_], w=[xb_])

                def xhc(h):
                    lo, hi = H2[h]["lo"], H2[h]["hi"]
                    X("act", "copy", xh[:, lo:hi], xc[:, lo:hi], r=[hbufs["xc"][h]], w=[hbufs["xh"][h]])
                    if h == 1:
                        if isA:
                            X("act", "copy", CT[:, c, :], XA[:, 1024:1027], r=[XAb], w=[CTb])
                            X("act", "copy", STt[:, c, 0:12].rearrange("p (b k) -> p b k", k=3),
                              XA[:, 1027:1167].rearrange("p (b t) -> p b t", t=35)[:, :, 32:35], r=[XAb], w=[STtb])
                        else:
                            X("act", "copy", STt[:, c, 0:3], XA[:, 1024:1027], r=[XAb], w=[STtb])

                def gates(h):
                    for tci in H2[h]["tcis"]:
                        c0, n = tcs[tci]
                        X("pe", "matmul", PS[6][:, 0:n], gw[:, 0, c, :], xh[:, c0:c0 + n], start=True, stop=True,
                          r=[gwb, hbufs["xh"][h]], w=[PSb[6]])
                        X("act", "activation", rt[:, c0:c0 + n], PS[6][:, 0:n], AF.Sigmoid, bias=vecs[:, c, 6:7],
                          r=[PSb[6], vecsb], w=[hbufs["rt"][h]])
                        X("pe", "matmul", PTf[:, 0:n], gw[:, 1, c, :], xh[:, c0:c0 + n], start=True, stop=True,
                          r=[gwb, hbufs["xh"][h]], w=[PThb[0]])
                        X("act", "activation", it[:, c0:c0 + n], PTf[:, 0:n], AF.Sigmoid, bias=vecs[:, c, 7:8],
                          r=[PThb[0], vecsb], w=[hbufs["it"][h]])

                def exps(h):
                    lo, hi = H2[h]["lo"], H2[h]["hi"]
                    rb, ab_ = hbufs["rt"][h], hbufs["At"][h]
                    X("act", "activation", At[:, lo:hi], rt[:, lo:hi], AF.Exp, scale=clt[:, c, 0:1], r=[rb, cltb], w=[ab_])
                    X("act", "activation", rt[:, lo:hi], rt[:, lo:hi], AF.Exp, scale=clt[:, c, 1:2], r=[rb, cltb], w=[rb])

                def itxc(h):
                    lo, hi = H2[h]["lo"], H2[h]["hi"]
                    X("dve", "tensor_tensor", it[:, lo:hi], it[:, lo:hi], xc[:, lo:hi], ALU.mult,
                      r=[hbufs["it"][h], hbufs["xc"][h]], w=[hbufs["it"][h]])
                    X("dve", "tensor_scalar", rt[:, lo:hi], rt[:, lo:hi], -1.0, 1.0, ALU.mult, ALU.add,
                      r=[hbufs["rt"][h]], w=[hbufs["rt"][h]])

                def sqrt_(h):
                    lo, hi = H2[h]["lo"], H2[h]["hi"]
                    X("act", "activation", rt[:, lo:hi], rt[:, lo:hi], AF.Sqrt, r=[hbufs["rt"][h]], w=[hbufs["rt"][h]])

                def scan(h):
                    lo, hi = H2[h]["lo"], H2[h]["hi"]
                    ab_, ib_, hb_ = hbufs["At"][h], hbufs["it"][h], hbufs["Ht"][h]
                    X("dve", "tensor_tensor", it[:, lo:hi], it[:, lo:hi], rt[:, lo:hi], ALU.mult, r=[ib_, hbufs["rt"][h]], w=[ib_])
                    if h == 0:
                        if isA:
                            X("dve", "tensor_tensor_scan", Ht[:, 0:512], At[:, 0:512], it[:, 0:512], 0.0, ALU.mult, ALU.add,
                              r=[ab_, ib_], w=[hb_])
                        else:
                            X("dve", "tensor_tensor_scan", Ht[:, 0:512], At[:, 0:512], it[:, 0:512], CH[:, c:c + 1],
                              ALU.mult, ALU.add, r=[ab_, ib_, CHb], w=[hb_])
                    else:
                        X("dve", "tensor_tensor_scan", Ht[:, 512:1024], At[:, 512:1024], it[:, 512:1024], Ht[:, 511:512],
                          ALU.mult, ALU.add, r=[ab_, ib_, hbufs["Ht"][0]], w=[hb_])
                        if isA:
                            for b in range(4):
                                s0 = 1024 + 32 * b
                                X("dve", "tensor_tensor_scan", Ht[:, s0:s0 + 32], At[:, s0:s0 + 32], it[:, s0:s0 + 32],
                                  shT[:, c, b:b + 1], ALU.mult, ALU.add, r=[ab_, ib_, shTb], w=[hb_])
                    X("dve", "tensor_tensor", mixv(c)[:, lo:hi], Ht[:, lo:hi], G[:, lo:hi], ALU.mult, r=[hb_, Gb], w=[mixb[c]])

                def states():
                    hb_ = hbufs["Ht"][1]
                    if isA:
                        X("act", "copy", CH[:, c:c + 1], Ht[:, 1023:1024], r=[hb_], w=[CHb])
                        X("act", "copy", STt[:, c, 12:16], Ht[:, 1024:1152].rearrange("p (b t) -> p b t", t=32)[:, :, 31],
                          r=[hb_], w=[STtb])
                    else:
                        X("act", "copy", STt[:, c, 3:4], Ht[:, 1023:1024], r=[hb_], w=[STtb])

                if part == 0:
                    conv(0); xhc(0); conv(1); xhc(1)
                else:
                    gates(0); gates(1)
                    exps(0); itxc(0); exps(1); sqrt_(0); itxc(1); sqrt_(1)
                    scan(0); scan(1)
                    states()

            for c in range(NCH):
                if c > 0:
                    post(c - 1, 0)
                pre(c)
                if c > 0:
                    post(c - 1, 1)
            post(NCH - 1, 0)
            post(NCH - 1, 1)
            R.fence(allh, [scrb[1], scrb[2], scrb[3], scrb[4], scrb[5], hb16b[0]])
            if isA:
                store_rows(STt, STtb, NCH, 16, [(0, 12, csn_o[j]), (12, 4, hsn_o[j])])
            else:
                store_rows(STt, STtb, NCH, 4, [(0, 3, cp_o[j]), (3, 1, hp_o[j:j + 1, :])])
            Sa, Sab = scr[1], scrb[1]
            Sb_, Sbb = scr[2], scrb[2]
            W = (15 + 1024 + 4 * 47) if isA else (15 + 1024)
            for g in range(4):
                wwin = 2 << g
                mbs = []
                for o in range(2):
                    cb = 2 * g + o
                    view, wb = load_w([wv[:, :, 4096 + cb * 128:4096 + (cb + 1) * 128],
                                       wv[:, :, 5120 + cb * 128:5120 + (cb + 1) * 128]], NCH)
                    G, Gb = Gs[o]
                    XB, XBb = XAs[o]
                    set_pads(XB, XBb, 15, (CP[:, cb, :], CPb), (spT[:, cb, :].rearrange("p (b k) -> p b k", k=15), spTb))
                    proj(view, 0, wb[0], NCH, rhsT, [hTb], tcs, [0, 1, 2], evac_padded(XB, XBb, 15))
                    proj(view, 1, wb[1], NCH, rhsT, [hTb], tcs, [3, 4, 5], evac_silu(G, Gb))
                    if isA:
                        X("act", "copy", CP[:, cb, :], XB[:, 1024:1039], r=[XBb], w=[CPb])
                        X("act", "copy", STp[:, cb, :].rearrange("p (b k) -> p b k", k=15),
                          XB[:, 1039:1227].rearrange("p (b t) -> p b t", t=47)[:, :, 32:47], r=[XBb], w=[STpb])
                    else:
                        X("act", "copy", STp[:, cb, 0:15], XB[:, 1024:1039], r=[XBb], w=[STpb])
                    src, srcb = XB, XBb
                    sh = 1
                    tgl = [(Sa, Sab), (Sb_, Sbb)]
                    ti = 0
                    lo = 0
                    while sh < wwin:
                        dstt, dstb = tgl[ti]
                        lo2 = lo + sh
                        X("dve", "tensor_tensor", dstt[:, lo2:W], src[:, lo2:W], src[:, lo:W - sh], ALU.add,
                          r=[srcb], w=[dstb])
                        src, srcb = dstt, dstb
                        lo = lo2
                        ti ^= 1
                        sh *= 2
                    mb = hb16[1 + o]; mbb = hb16b[1 + o]
                    for sv, xv, ov in zip(seg_views(src, 15, 15), seg_views(XB, 15, 15), out_views(mb)):
                        X("dve", "scalar_tensor_tensor", ov, sv, 1.0 / wwin, xv, ALU.mult, ALU.subtract,
                          r=[srcb, XBb], w=[mbb])
                    if isA:
                        nf = wwin - 1
                        tmp = tgl[ti][0]; tmpb = tgl[ti][1]
                        X("dve", "tensor_tensor", tmp[:, 0:nf], src[:, 15:15 + nf], invc[:, g * 16:g * 16 + nf], ALU.mult,
                          r=[srcb, invcb], w=[tmpb])
                        X("dve", "tensor_tensor", mb[:, 0:nf], tmp[:, 0:nf], XB[:, 15:15 + nf], ALU.subtract,
                          r=[tmpb, XBb], w=[mbb])
                    mbs.append((mb, mbb))
                    if os.environ.get("KDEBUG") == "2" and l == 0 and isA and g == 3 and o == 0:
                        d1 = nc.dram_tensor("dbg_s16", [128, SCW], F32, kind="ExternalOutput").ap()
                        d2 = nc.dram_tensor("dbg_xb", [128, SCW], F32, kind="ExternalOutput").ap()
                        d3 = nc.dram_tensor("dbg_mb", [128, NTA], BF16, kind="ExternalOutput").ap()
                        R.dma(cch_kv[0], d1, src[:, :], reads=[srcb])
                        R.dma(cch_kv[1], d2, XB[:, :], reads=[XBb])
                        R.dma(cch_kv[2], d3, mb[:, :], reads=[mbb])
                for o in range(2):
                    cbo = 2 * g + o
                    G, Gb = Gs[o]
                    for tci, (c0, n) in enumerate(tcs):
                        gbk, gbkb = gbanks[tci % 2]
                        for i in range(2):
                            X("pe", "matmul", gbk[:, 0:n], pw[:, g, i, o * 128:(o + 1) * 128], mbs[i][0][:, c0:c0 + n],
                              start=(i == 0), stop=(i == 1), r=[pwb, mbs[i][1]], w=[gbkb])
                        X("dve", "scalar_tensor_tensor", mixv(16 + cbo)[:, c0:c0 + n], gbk[:, 0:n], psc[:, cbo, 0:1],
                          G[:, c0:c0 + n], ALU.mult, ALU.mult, r=[gbkb, pscb, Gb], w=[mixb[16 + cbo]])
            if isA:
                store_rows(STp, STpb, 8, 60, [(0, 60, psn_o[j])])
            else:
                store_rows(STp, STpb, 8, 15, [(0, 15, pp_o[j])])
            if os.environ.get("KDEBUG") and l == 0 and isA:
                dbg = nc.dram_tensor("dbg_mix", [24, 128, NTA], BF16, kind="ExternalOutput").ap()
                for f in range(24):
                    R.dma(cch_kv[0], dbg[f], mixv(f), reads=[mixb[f]])
            phase3(l, P, w_out_rec[j], 24)

        def stA(T):
            zb = T["i"] % 2
            for (c0, nn, lhsT, rhs, rb, s_, e_) in T["qk"]:
                X("pe", "matmul", PS[zb][:, c0:c0 + nn], lhsT, rhs, start=s_, stop=e_, r=rb, w=[PSb[zb]])

        def stB(T):
            s = T["i"] % 2; n = T["n"]
            X("act", "activation", nbt[s][:, 0:n], PS[s][:, 0:n], AF.Sigmoid, scale=-1.0, r=[PSb[s]], w=[nbtb[s]])
            X("act", "activation", bt[s][:, 0:n], PS[s][:, 0:n], AF.Sigmoid, r=[PSb[s]], w=[btb[s]])

        def stC(T):
            s = T["i"] % 2; n = T["n"]
            pt_ = ptile[s]; ptb = ptileb[s]
            if T["first"]:
                X("dve", "memset", pt_[:, 0:1], 1.0, w=[ptb])
                init = 1.0; ib = []
            elif T["chain"] == "p":
                pp_ = ptile[1 - s]; npv = T["prev_n"]
                X("dve", "tensor_copy", pt_[:, 0:1], pp_[:, npv:npv + 1], r=[ptileb[1 - s]], w=[ptb])
                init = pp_[:, npv:npv + 1]; ib = [ptileb[1 - s]]
            else:
                X("dve", "tensor_copy", pt_[:, 0:1], scarry[:, 0:1], r=[scarryb], w=[ptb])
                init = scarry[:, 0:1]; ib = [scarryb]
            X("dve", "tensor_tensor_scan", pt_[:, 1:n + 1], nbt[s][:, 0:n], zeros_f[:, 0:n], init, ALU.mult, ALU.add,
              r=[nbtb[s], zerosb] + ib, w=[ptb])
            if T["chain"] == "s" and not T["lastc"]:
                X("dve", "tensor_copy", scarry[:, 0:1], pt_[:, n:n + 1], r=[ptb], w=[scarryb])
            X("dve", "tensor_tensor", Wn[s][:, 0:n][:, ::-1], bt[s][:, 0:n], pt_[:, 0:n], ALU.mult,
              r=[btb[s], ptb], w=[Wnb[s]])

        def stD(T):
            s = T["i"] % 2
            for bi, (c0, kk) in enumerate(T["tr"]):
                X("pe", "transpose", PTh[s][0:kk, bi * 128:(bi + 1) * 128], Wn[s][:, c0:c0 + kk], ident_b[:, :],
                  r=[Wnb[s], identbb], w=[PThb[s]])

        def stE(T):
            s = T["i"] % 2
            kk, ncl = T["ecopy"]
            X("act", "copy", WT[s][0:kk, 0:ncl], PTh[s][0:kk, 0:ncl], r=[PThb[s]], w=[WTb[s]])

        def stF(T):
            s = T["i"] % 2; pr = T["pr"]
            for (c0, nn, lhsT, lb, kk, wc0, s_, e_) in T["pv"]:
                X("pe", "matmul", po[pr][:, c0:c0 + nn], lhsT, WT[s][0:kk, wc0:wc0 + nn], start=s_, stop=e_,
                  r=[lb, WTb[s]], w=[pob[pr]])
            if T.get("fin"):
                T["fin"](pr)
            for h in T.get("postF", ()):
                h()

        def run_pipeline(tasks, pre=(), filler=()):
            n = len(tasks); ng = len(filler); gi = 0
            grp = -1
            for i, T in enumerate(tasks):
                T["i"] = i
                if T["chain"] == "s" or T["first"]:
                    grp += 1
                T["pr"] = grp % 2
            for h in pre:
                h()
            for t in range(n + 3):
                if 0 <= t - 3 < n:
                    stF(tasks[t - 3])
                if 0 <= t - 2 < n:
                    stD(tasks[t - 2]); stE(tasks[t - 2])
                if 0 <= t - 1 < n:
                    stC(tasks[t - 1])
                if t < n:
                    stA(tasks[t]); stB(tasks[t])
                    for h in tasks[t].get("post", ()):
                        h()
                    tgt = min(ng, ((t + 1) * ng + n - 1) // n)
                    while gi < tgt:
                        filler[gi](); gi += 1
            while gi < ng:
                filler[gi](); gi += 1

        pctr = [0]

        def prompt_tasks(hd, ql, isA, G, Gb):
            p = hd % 2
            qa = ql + (0 if isA else 8)
            s0 = (15 - qa) * 128
            q_ap = QT[p][:, ql * 128:(ql + 1) * 128]
            pr = pctr[0] % 2
            pctr[0] += 1
            spans = []
            s = s0
            while s < 2048:
                n = min(512, 2048 - s)
                spans.append((s, n, 2048 - s - n))
                s += n
            total = sum(n // 128 for _, n, _ in spans)
            bd = 0
            tasks = []
            qkb = [QTb[p], KTrb[p]]
            for ci, (s, n, k0) in enumerate(spans):
                if ci == 0:
                    qk = [(0, 128, q_ap, KTr[p][:, s:s + 128], qkb, True, False),
                          (0, 128, ident_b[:, :], dmask_b[:, :], [identbb, dmaskbb], False, True)]
                    if n > 128:
                        qk.append((128, n - 128, q_ap, KTr[p][:, s + 128:s + n], qkb, True, True))
                else:
                    qk = [(0, n, q_ap, KTr[p][:, s:s + n], qkb, True, True)]
                pv = []
                for bi in range(n // 128):
                    kb_ = k0 // 128 + bi
                    if isA:
                        vap, vbuf = Vb[p][:, kb_ * 128:(kb_ + 1) * 128], Vbb[p]
                    elif kb_ < 8:
                        vap, vbuf = Vbo[p][:, kb_ * 128:(kb_ + 1) * 128], Vbob[p]
                    else:
                        vap, vbuf = Vb[p][:, (kb_ - 8) * 128:(kb_ - 7) * 128], Vbb[p]
                    pv.append((0, 128, vap, vbuf, 128, bi * 128, bd == 0, bd == total - 1))
                    bd += 1
                tasks.append(dict(n=n, chain="p", first=(ci == 0), prev_n=(spans[ci - 1][1] if ci else None), lastc=False,
                                  pr=pr, qk=qk, tr=[(bi * 128, 128) for bi in range(n // 128)], ecopy=(128, n), pv=pv,
                                  post=[], postF=[]))

            def fin(pr, hd=hd, ql=ql, G=G, Gb=Gb):
                X("dve", "tensor_tensor", mixv(hd)[:, ql * 128:(ql + 1) * 128], po[pr][:, 0:128], G[:, ql * 128:(ql + 1) * 128],
                  ALU.mult, r=[pob[pr], Gb], w=[mixb[hd]])
            tasks[-1]["fin"] = fin
            return tasks

        def sample_tasks(hd, G, Gb):
            p = hd % 2
            tasks = []

            def mkfin(k):
                def fin(pr, hd=hd, G=G, Gb=Gb):
                    if k == 0:
                        X("dve", "tensor_copy", sacc[:, :], po[pr][:, 0:128], r=[pob[pr]], w=[saccb])
                    else:
                        X("dve", "tensor_tensor", sacc[:, :], sacc[:, :], po[pr][:, 0:128], ALU.add, r=[pob[pr], saccb], w=[saccb])
                    if k == 4:
                        X("dve", "tensor_tensor", mixv(hd)[:, 1024:1152], sacc[:, :], G[:, 1024:1152],
                          ALU.mult, r=[saccb, Gb], w=[mixb[hd]])
                return fin
            pr = pctr[0] % 2; pctr[0] += 1
            tasks.append(dict(n=32, chain="s", first=True, lastc=False, pr=pr,
                              qk=[(0, 32, Qpad[p][:, b * 128:(b + 1) * 128], KSr[p][:, 32 * b:32 * b + 32], [Qpadb[p], KSrb[p]],
                                   b == 0, False) for b in range(4)]
                                 + [(0, 32, ident_b[:, :], smask_b[:, :], [identbb, smaskbb], False, True)],
                              tr=[(0, 32)], ecopy=(32, 128),
                              pv=[(32 * b, 32, Vsn[p][0:32, b * 128:(b + 1) * 128], Vsnb[p], 32, 32 * b, b == 0, b == 3)
                                  for b in range(4)],
                              post=[], postF=[], fin=mkfin(0)))
            for cc in range(4):
                s = cc % 2
                pr = pctr[0] % 2; pctr[0] += 1
                tasks.append(dict(n=512, chain="s", first=False, lastc=(cc == 3), pr=pr,
                                  qk=[(0, 512, Qpad[p][:, b * 128:(b + 1) * 128], KcTc[s][:, b * 512:(b + 1) * 512],
                                       [Qpadb[p], KcTcb[s]], b == 0, b == 3) for b in range(4)],
                                  tr=[(bi * 128, 128) for bi in range(4)], ecopy=(128, 512),
                                  pv=[(32 * b, 32, vch[s][:, (b * 4 + blk) * 128:(b * 4 + blk + 1) * 128], vchb[s][b], 128,
                                       blk * 128 + 32 * b, (blk == 0 and b == 0), (blk == 3 and b == 3))
                                      for blk in range(4) for b in range(4)],
                                  post=[], postF=[], fin=mkfin(cc + 1)))
            return tasks

        def att_layer(l, j):
            load_rows_T_safe(vecs, vecsb, [(norm_att[j:j + 1, :], 1)], D)
            wv = w_in_att[j].rearrange("(k p) n -> p k n", p=128)
            R.fence(mixb[16:24] + rec_alias_bufs, att_alias_bufs)
            for P in PASSES:
                att_pass(l, j, P, wv)

        def att_pass(l, j, P, wv):
            NT = P["NT"]; isA = P["pi"] == 0; tcs = P["tcs"]; nblk = P["nblk"]
            phase1(l, P, lambda c: vecs[:, c, 0:1])
            R.fence([scrb[4], scrb[7]], scr47_att)
            rhsT = lambda k, c0, n: hT[:, k, c0:c0 + n]
            KT32, KT32b = scr[0], scrb[0]
            VT32, VT32b = scr[1], scrb[1]
            Gs = [(scr[2], scrb[2]), (scr[3], scrb[3])]
            tok0 = 0 if isA else 1024
            wviews = {}

            def load_qk(hd):
                wviews[(hd, 0)] = load_w([wv[:, :, hd * 128:(hd + 1) * 128], wv[:, :, 2048 + hd * 128:2048 + (hd + 1) * 128]], NCH)

            def load_vg(hd):
                wviews[(hd, 1)] = load_w([wv[:, :, 4096 + hd * 128:4096 + (hd + 1) * 128],
                                          wv[:, :, 6144 + hd * 128:6144 + (hd + 1) * 128]], NCH)

            def dma_K(hd, cc):
                s = cc % 2; k0 = 1536 - 512 * cc
                for b in range(4):
                    ev = R.dma(kcbch[s], kraw[s][:, b * 512:(b + 1) * 512].rearrange("p (k d) -> p k d", d=128),
                               ck_d[j, b, k0:k0 + 512, hd * 128:(hd + 1) * 128].rearrange("(k p) d -> p k d", p=128),
                               writes=[krawb[s][b]], eng="pool")
                for b in range(4):
                    krawb[s][b].w = ev

            def dma_V(hd, cc):
                s = cc % 2; k0 = 1536 - 512 * cc
                for b in range(4):
                    ev = R.dma(vcbch[s], vch[s][:, b * 512:(b + 1) * 512].rearrange("p (k d) -> p k d", d=128),
                               cv_d[j, b, k0:k0 + 512, hd * 128:(hd + 1) * 128].rearrange("(k p) d -> p k d", p=128),
                               writes=[vchb[s][b]], eng="pool")
                for b in range(4):
                    vchb[s][b].w = ev

            def prepT(cc):
                s = cc % 2
                for half in range(2):
                    for bb in range(2):
                        for blk in range(4):
                            idx = (2 * half + bb) * 4 + blk
                            X("pe", "transpose", PK[half][:, (bb * 4 + blk) * 128:(bb * 4 + blk + 1) * 128],
                              kraw[s][:, idx * 128:(idx + 1) * 128], ident_b[:, :], r=[krawb[s][2 * half + bb], identbb], w=[PKb[half]])
                    X("dve", "tensor_copy",
                      KcTc[s][:, 2 * half * 512:(2 * half + 2) * 512].rearrange("p (b k) -> p b k", k=512)[:, :, ::-1],
                      PK[half][:, 0:1024].rearrange("p (b k) -> p b k", k=512), r=[PKb[half]], w=[KcTcb[s]])

            def head_groups(hd):
                p = hd % 2
                G, Gb = Gs[p]
                groups = []

                def evq(tci, c0, n, bk):
                    if c0 < 1024:
                        X("act", "mul", QT[p][:, c0:c0 + n], PS[bk][:, 0:n], SCALE, r=[PSb[bk]], w=[QTb[p]])
                    else:
                        X("act", "mul", Qpad[p][:, 0:640].rearrange("p (b t) -> p b t", t=160)[:, :, 0:32],
                          PS[bk][:, 0:128].rearrange("p (b t) -> p b t", t=32), SCALE, r=[PSb[bk]], w=[Qpadb[p]])

                def evk(tci, c0, n, bk):
                    X("act", "copy", KT32[:, c0:c0 + n], PS[bk][:, 0:n], r=[PSb[bk]], w=[KT32b])
                    if c0 < 1024:
                        hi = 2047 - (tok0 + c0)
                        X("dve", "tensor_copy", KTr[p][:, hi - n + 1:hi + 1][:, ::-1], KT32[:, c0:c0 + n], r=[KT32b], w=[KTrb[p]])
                    else:
                        X("dve", "tensor_copy", KSr[p][:, 0:128].rearrange("p (b t) -> p b t", t=32)[:, :, ::-1],
                          KT32[:, 1024:1152].rearrange("p (b t) -> p b t", t=32), r=[KT32b], w=[KSrb[p]])

                def evv(tci, c0, n, bk):
                    X("act", "copy", VT32[:, c0:c0 + n], PS[bk][:, 0:n], r=[PSb[bk]], w=[VT32b])

                def evg(tci, c0, n, bk):
                    X("act", "activation", G[:, c0:c0 + n], PS[bk][:, 0:n], AF.Sigmoid, r=[PSb[bk]], w=[Gb])
                    X("dve", "tensor_tensor", G[:, c0:c0 + n], G[:, c0:c0 + n], PS[bk][:, 0:n], ALU.mult,
                      r=[PSb[bk], Gb], w=[Gb])

                if not isA:
                    def ld():
                        R.dma(cch_kv[0][p], KTr[p][:, 1024:2048], KTS[j][hd], reads=[KTSb[j][hd]], writes=[KTrb[p]])
                        R.dma(cch_kv[1][p], Vbo[p][:, :], VSS[j][hd], reads=[VSSb[j][hd]], writes=[Vbob[p]])
                    groups.append(ld)
                gctr = 0
                for wi, pidx, ev in [(0, 0, evq), (0, 1, evk), (1, 0, evv), (1, 1, evg)]:
                    for tci, tc in enumerate(tcs):
                        bk = [4, 5][gctr % 2]; gctr += 1
                        for half in range(2):
                            def g(wi=wi, pidx=pidx, ev=ev, tci=tci, tc=tc, bk=bk, half=half):
                                view, wb = wviews[(hd, wi)]
                                c0, n = tc
                                for k in range(8 * half, 8 * half + 8):
                                    X("pe", "matmul", PS[bk][:, 0:n], view[:, k, pidx * 128:(pidx + 1) * 128], hT[:, k, c0:c0 + n],
                                      start=(k == 0), stop=(k == NCH - 1), r=[wb[pidx], hTb], w=[PSb[bk]])
                                if half == 1:
                                    ev(tci, c0, n, bk)
                            groups.append(g)
                    if (wi, pidx) == (0, 1) and hd + 1 < 16:
                        groups.append(lambda: load_qk(hd + 1))
                    if (wi, pidx) == (1, 1) and hd + 1 < 16:
                        groups.append(lambda: load_vg(hd + 1))
                for which, (src32, src32b, outp, outs) in enumerate([(KT32, KT32b, kp_o, ksn_o), (VT32, VT32b, vp_o, vsn_o)]):
                    kt = ktok[which]; ktb = ktokb[which]
                    for g0 in range(0, nblk, 4):
                        def g(g0=g0, src32=src32, src32b=src32b, kt=kt, ktb=ktb):
                            g1 = min(nblk, g0 + 4)
                            for i, tb in enumerate(range(g0, g1)):
                                X("pe", "transpose", PS[6][:, i * 128:(i + 1) * 128], src32[:, tb * 128:(tb + 1) * 128], ident_f[:, :],
                                  r=[src32b, identfb], w=[PSb[6]])
                            X("act", "copy", kt[:, g0:g1, :], PS[6][:, 0:(g1 - g0) * 128].rearrange("p (b d) -> p b d", d=128),
                              r=[PSb[6]], w=[ktb])
                        groups.append(g)

                    def st(which=which, kt=kt, ktb=ktb, outp=outp, outs=outs):
                        R.dma(ktokch[which], outp[j, tok0:tok0 + 1024, hd * 128:(hd + 1) * 128].rearrange("(b p) d -> p b d", p=128),
                              kt[:, 0:8, :], reads=[ktb])
                        if isA:
                            R.dma(ktokch[which], outs[j, :, hd * 128:(hd + 1) * 128], kt[:, 8, :], reads=[ktb])
                        if which == 1:
                            X("dve", "tensor_copy", Vb[p][:, 0:nblk * 128].rearrange("p (b d) -> p b d", d=128), kt[:, 0:nblk, :],
                              r=[ktb], w=[Vbb[p]])
                    groups.append(st)
                if isA:
                    def fin_a():
                        R.dma(cch_kv[2][p], KTS[j][hd], KTr[p][:, 1024:2048], reads=[KTrb[p]], writes=[KTSb[j][hd]])
                        R.dma(cch_kv[3][p], VSS[j][hd], Vb[p][:, 0:1024], reads=[Vbb[p]], writes=[VSSb[j][hd]])
                        for b in range(4):
                            X("pe", "transpose", PS[6][0:32, b * 128:(b + 1) * 128], VT32[:, 1024 + 32 * b:1056 + 32 * b], ident_f[:, :],
                              r=[VT32b, identfb], w=[PSb[6]])
                        X("act", "copy", Vsn[p][0:32, 0:512], PS[6][0:32, 0:512], r=[PSb[6]], w=[Vsnb[p]])
                    groups.append(fin_a)
                return groups

            X("dve", "memset", Qpad[1][:, :], 0.0, w=[Qpadb[1]])
            if isA:
                dma_K(0, 0); dma_V(0, 0); dma_K(0, 1); dma_V(0, 1)
            load_qk(0); load_vg(0)
            for g in head_groups(0):
                g()
            for hd in range(16):
                G, Gb = Gs[hd % 2]
                filler = head_groups(hd + 1) if hd + 1 < 16 else []
                Pq = [prompt_tasks(hd, ql, isA, G, Gb) for ql in range(8)]
                if isA:
                    S = sample_tasks(hd, G, Gb)
                    nxt = hd + 1 < 16
                    tasks = Pq[0] + [S[0]] + Pq[1] + [S[1]] + Pq[2] + Pq[3] + [S[2]] + Pq[4] + Pq[5] + [S[3]] + Pq[6] + Pq[7] + [S[4]]
                    pre = [lambda: prepT(0), lambda hd=hd: dma_K(hd, 2), lambda: prepT(1), lambda hd=hd: dma_K(hd, 3)]
                    S[1]["post"].append(lambda: prepT(2))
                    S[2]["post"].append(lambda: prepT(3))
                    S[1]["postF"].append(lambda hd=hd: dma_V(hd, 2))
                    S[2]["postF"].append(lambda hd=hd: dma_V(hd, 3))
                    if nxt:
                        S[1]["post"].append(lambda hd=hd: dma_K(hd + 1, 0))
                        S[2]["post"].append(lambda hd=hd: dma_K(hd + 1, 1))
                        S[3]["postF"].append(lambda hd=hd: dma_V(hd + 1, 0))
                        S[4]["postF"].append(lambda hd=hd: dma_V(hd + 1, 1))
                    run_pipeline(tasks, pre, filler)
                else:
                    run_pipeline([T for q in Pq for T in q], (), filler)
            R.fence(scr47_att, [scrb[4], scrb[7]])
            phase3(l, P, w_out_att[j], 16)

        cch_kv = [[R.chan(), R.chan()] for _ in range(4)]

        def final_phase(l):
            load_rows_T_safe(vecs, vecsb, [(norm_final, 1)], D)
            R.fence(att_alias_bufs, [xa2b])
            for P in PASSES:
                NT = P["NT"]; col0 = P["col0"]; pi = P["pi"]
                stats(l, P)
                rstd = scr[7]
                for tb in range(P["nblk"]):
                    for hf in range(2):
                        if tb % 2 == 0:
                            xb_ = scr[hf]; xbb = scrb[hf]; xch = scrch[hf]
                        else:
                            xb_, xbb, xch = [(scr[6], scrb[6], scrch[6]), (xa2, xa2b, xa2ch)][hf]
                        yn = scr[2 + hf]; ynb = scrb[2 + hf]
                        yt = scr[4 + hf]; ytb = scrb[4 + hf]
                        R.dma(xch, xb_[:, 0:1024].rearrange("p (c t) -> p c t", t=128),
                              XS[l][hf * 8:(hf + 1) * 8, :, col0 + tb * 128:col0 + (tb + 1) * 128].rearrange("c p t -> p c t"),
                              reads=[XSb[l][c][pi] for c in range(hf * 8, hf * 8 + 8)], writes=[xbb])
                        for ci in range(8):
                            c = hf * 8 + ci
                            X("dve", "scalar_tensor_tensor", yn[:, ci * 128:(ci + 1) * 128], xb_[:, ci * 128:(ci + 1) * 128],
                              vecs[:, c, 0:1], rstd[:, tb * 128:(tb + 1) * 128], ALU.mult, ALU.mult,
                              r=[xbb, vecsb, scrb[7]], w=[ynb])
                        for g in range(2):
                            bk = hf * 2 + g
                            for i in range(4):
                                ci = g * 4 + i
                                X("pe", "transpose", PS[bk][:, i * 128:(i + 1) * 128], yn[:, ci * 128:(ci + 1) * 128], ident_f[:, :],
                                  r=[ynb, identfb], w=[PSb[bk]])
                            X("act", "copy", yt[:, g * 512:(g + 1) * 512], PS[bk][:, :], r=[PSb[bk]], w=[ytb])
                        if pi == 0 and tb == 8:
                            dst = ys_o[:, hf * 1024:(hf + 1) * 1024]
                        else:
                            t0 = tb * 128 + (0 if pi == 0 else 1024)
                            dst = yp_o[t0:t0 + 128, hf * 1024:(hf + 1) * 1024]
                        R.dma(scrch[4 + hf], dst, yt[:, 0:1024], reads=[ytb])

        init_phase()
        if ltypes is None:
            ltypes = "rara"[:nlayers]
        nlayers = len(ltypes)
        cnt = {"r": 0, "a": 0}
        for l, t in enumerate(ltypes):
            if t == "r":
                rec_layer(l, cnt["r"])
            else:
                att_layer(l, cnt["a"])
            cnt[t] += 1
        final_phase(nlayers)
        R.emit()
        build.stats = R.stats
    return nc


_CACHE = {}


def _consts():
    ident = np.eye(128, dtype=np.float32)
    p = np.arange(128)[:, None]
    i = np.arange(512)[None, :]
    dmask = np.where((i < 128) & (i + p < 128), NEG, 0.0).astype(np.float32)
    t = (np.arange(128) % 32)[:, None]
    ii = np.arange(32)[None, :]
    smask = np.where(ii + t < 32, NEG, 0.0).astype(np.float32)
    invc = np.zeros((128, 64), np.float32)
    for g, w in enumerate((2, 4, 8, 16)):
        for tt in range(16):
            invc[:, g * 16 + tt] = 1.0 / min(w, tt + 1)
    return ident, dmask, smask, invc


def kernel(x_prompt, x_sample, cache_k, cache_v, state_h, state_conv, state_pool,
           norm_rec, w_in_rec, conv_w, conv_b, gate_r_w, gate_r_b, gate_i_w, gate_i_b, rg_lambda,
           pool_w, pool_scale, w_out_rec, norm_att, w_in_att, w_out_att, norm_final):
    f = lambda a: np.ascontiguousarray(np.asarray(a, dtype=np.float32))
    if "nc" not in _CACHE:
        _CACHE["nc"] = build()
    nc = _CACHE["nc"]
    ident, dmask, smask, invc = _consts()
    x_prompt = f(x_prompt); x_sample = f(x_sample); cache_k = f(cache_k); cache_v = f(cache_v)
    state_h = f(state_h); state_conv = f(state_conv); state_pool = f(state_pool)
    shared = dict(norm_rec=f(norm_rec), w_in_rec=f(w_in_rec), conv_w=f(conv_w), conv_b=f(conv_b),
                  gate_r_w=f(gate_r_w), gate_r_b=f(gate_r_b), gate_i_w=f(gate_i_w), gate_i_b=f(gate_i_b),
                  rg_lambda=f(rg_lambda), pool_w=f(pool_w), pool_scale=f(pool_scale), w_out_rec=f(w_out_rec),
                  norm_att=f(norm_att), w_in_att=f(w_in_att), w_out_att=f(w_out_att),
                  norm_final=f(norm_final).reshape(1, D), ident=ident, dmask=dmask, smask=smask, invc=invc)
    in_maps = []
    for c in range(8):
        sl = slice(4 * c, 4 * c + 4)
        m = dict(shared)
        m["xp"] = x_prompt[c % 4]
        m["xs"] = x_sample[sl].reshape(128, D)
        m["ck"] = cache_k[:, sl].reshape(2, 4, 2048, D)
        m["cv"] = cache_v[:, sl].reshape(2, 4, 2048, D)
        m["sh"] = state_h[:, sl]
        m["sc"] = state_conv[:, sl].reshape(2, 12, D)
        m["spl"] = state_pool[:, sl].reshape(2, 60, 1024)
        in_maps.append({k: np.ascontiguousarray(v) for k, v in m.items()})
    res = run_bass_kernel_spmd(nc, in_maps, core_ids=list(range(8)))
    rs = res.results
    y_prompt = np.stack([rs[c]["yp"] for c in range(4)])
    y_sample = np.concatenate([rs[c]["ys"].reshape(4, 32, D) for c in range(8)])
    k_prompt = np.stack([rs[c]["kp"] for c in range(4)], axis=1).reshape(2, 4, 2048, 16, 128)
    v_prompt = np.stack([rs[c]["vp"] for c in range(4)], axis=1).reshape(2, 4, 2048, 16, 128)
    h_prompt = np.stack([rs[c]["hp"] for c in range(4)], axis=1)
    conv_prompt = np.stack([rs[c]["cp"] for c in range(4)], axis=1)
    pool_prompt = np.stack([rs[c]["pp"] for c in range(4)], axis=1)
    k_sample = np.concatenate([rs[c]["ksn"].reshape(2, 4, 32, 16, 128) for c in range(8)], axis=1)
    v_sample = np.concatenate([rs[c]["vsn"].reshape(2, 4, 32, 16, 128) for c in range(8)], axis=1)
    h_sample = np.concatenate([rs[c]["hsn"] for c in range(8)], axis=1)
    conv_sample = np.concatenate([rs[c]["csn"].reshape(2, 4, 3, D) for c in range(8)], axis=1)
    pool_sample = np.concatenate([rs[c]["psn"].reshape(2, 4, 15, 1024) for c in range(8)], axis=1)
    outs = (y_prompt, y_sample, k_prompt, v_prompt, h_prompt, conv_prompt, pool_prompt,
            k_sample, v_sample, h_sample, conv_sample, pool_sample)
    return tuple(np.ascontiguousarray(o, dtype=np.float32) for o in outs)
```

```python
import contextlib
import os
import numpy as np
import concourse.bass as bass
import concourse.mybir as mybir
from concourse.bass_utils import run_bass_kernel_spmd

F32 = mybir.dt.float32
BF16 = mybir.dt.bfloat16
ALU = mybir.AluOpType
AF = mybir.ActivationFunctionType

ENGS = ("pe", "act", "dve", "pool", "sp")
NEG = -10000.0
EPS = 1e-6
D = 2048
NCH = 16
NTA = 1152
NTB = 1024
NTT = NTA + NTB
SCALE = 128 ** -0.5
DEPTH = 4
KMODE = 31


class Buf:
    __slots__ = ("name", "w", "r")

    def __init__(self, name=""):
        self.name = name
        self.w = None
        self.r = {}


class Chan:
    def __init__(self, sem):
        self.sem = sem
        self.cnt = 0


class Rec:
    def __init__(self, nc, stack):
        self.nc = nc
        self.stack = stack
        self.ops = {e: [] for e in ENGS}
        self.esem = {e: stack.enter_context(nc.semaphore("es_" + e)) for e in ENGS}
        self.chans = []

    def chan(self, name=None):
        c = Chan(self.stack.enter_context(self.nc.semaphore(name or ("ch%d" % len(self.chans)))))
        self.chans.append(c)
        return c

    def _add(self, eng, fn, reads, writes, chan=None):
        deps = []
        for b in reads:
            if b.w is not None:
                deps.append(b.w)
        for b in writes:
            if b.w is not None:
                deps.append(b.w)
            deps.extend(b.r.values())
        ops = self.ops[eng]
        idx = len(ops)
        if chan is not None:
            chan.cnt += 16
            ev = ("d", chan, chan.cnt)
            key = ("d", id(chan))
        else:
            ev = ("e", eng, idx)
            key = ("e", eng)
        ops.append({"fn": fn, "deps": deps, "chan": chan, "inc": False})
        for b in reads:
            b.r[key] = ev
        for b in writes:
            b.w = ev
            b.r = {}
        return ev

    def op(self, eng, fn, reads=(), writes=()):
        return self._add(eng, fn, reads, writes)

    def dma(self, chan, out, in_, reads=(), writes=(), eng="sp"):
        def fn(e, out=out, in_=in_):
            return e.dma_start(out=out, in_=in_)
        return self._add(eng, fn, reads, writes, chan=chan)

    def fence(self, srcs, dsts):
        snap = [(id(s), s.w, list(s.r.items())) for s in srcs]
        for d in dsts:
            for sid, w, items in snap:
                if w is not None:
                    d.r[("w", sid)] = w
                for k, ev in items:
                    d.r[(k, sid)] = ev

    def emit(self):
        nc = self.nc
        for eng in ENGS:
            for o in self.ops[eng]:
                nd = []
                for d in o["deps"]:
                    if d[0] == "e":
                        if d[1] == "pe" and eng == "pe":
                            continue
                        self.ops[d[1]][d[2]]["inc"] = True
                    nd.append(d)
                o["deps"] = nd
        val = {}
        for eng in ENGS:
            c = 0
            v = []
            for o in self.ops[eng]:
                if o["inc"] and o["chan"] is None:
                    c += 1
                v.append(c)
            val[eng] = v
        self.stats = {e: len(self.ops[e]) for e in ENGS}
        engobj = {"pe": "tensor", "act": "scalar", "dve": "vector", "pool": "gpsimd", "sp": "sync"}
        with nc.Block() as block:
            for eng in ENGS:
                ops = self.ops[eng]
                if not ops and eng != "sp":
                    continue

                def body(e, eng=eng, ops=ops):
                    known = {}
                    for o in ops:
                        need = {}
                        for d in o["deps"]:
                            if d[0] == "e":
                                sem = self.esem[d[1]]
                                v = val[d[1]][d[2]]
                            else:
                                sem = d[1].sem
                                v = d[2]
                            k = id(sem)
                            if known.get(k, 0) >= v:
                                continue
                            if k not in need or need[k][1] < v:
                                need[k] = (sem, v)
                        for k, (sem, v) in need.items():
                            e.wait_ge(sem, v)
                            known[k] = v
                        ins = o["fn"](e)
                        if o["chan"] is not None:
                            ins.then_inc(o["chan"].sem, 16)
                        elif o["inc"]:
                            ins.then_inc(self.esem[eng], 1)
                    if eng == "sp":
                        for c in self.chans:
                            if c.cnt > 0:
                                e.wait_ge(c.sem, c.cnt)
                getattr(block, engobj[eng])(body)


def build(nlayers=DEPTH, ltypes=None):
    nc = bass.Bass("TRN2", target_bir_lowering=False)

    def din(name, shape, dt=F32):
        return nc.dram_tensor(name, list(shape), dt, kind="ExternalInput").ap()

    def dout(name, shape):
        return nc.dram_tensor(name, list(shape), F32, kind="ExternalOutput").ap()

    def dscr(name, shape, dt):
        return nc.dram_tensor(name, list(shape), dt, kind="Internal").ap()

    xp_d = din("xp", [2048, D]); xs_d = din("xs", [128, D])
    ck_d = din("ck", [2, 4, 2048, D]); cv_d = din("cv", [2, 4, 2048, D])
    sh_d = din("sh", [2, 4, D]); sc_d = din("sc", [2, 12, D]); spl_d = din("spl", [2, 60, 1024])
    norm_rec = din("norm_rec", [2, D]); w_in_rec = din("w_in_rec", [2, D, 6144])
    conv_w = din("conv_w", [2, 4, D]); conv_b = din("conv_b", [2, D])
    gate_r_w = din("gate_r_w", [2, 16, 128, 128]); gate_r_b = din("gate_r_b", [2, D])
    gate_i_w = din("gate_i_w", [2, 16, 128, 128]); gate_i_b = din("gate_i_b", [2, D])
    rg_lambda = din("rg_lambda", [2, D]); pool_w = din("pool_w", [2, 4, 256, 256])
    pool_scale = din("pool_scale", [2, 1024]); w_out_rec = din("w_out_rec", [2, 3072, D])
    norm_att = din("norm_att", [2, D]); w_in_att = din("w_in_att", [2, D, 8192])
    w_out_att = din("w_out_att", [2, D, D]); norm_final = din("norm_final", [1, D])
    ident_d = din("ident", [128, 128]); dmask_d = din("dmask", [128, 512]); smask_d = din("smask", [128, 32])
    invc_d = din("invc", [128, 64])

    yp_o = dout("yp", [2048, D]); ys_o = dout("ys", [128, D])
    kp_o = dout("kp", [2, 2048, D]); vp_o = dout("vp", [2, 2048, D])
    hp_o = dout("hp", [2, D]); cp_o = dout("cp", [2, 3, D]); pp_o = dout("pp", [2, 15, 1024])
    ksn_o = dout("ksn", [2, 128, D]); vsn_o = dout("vsn", [2, 128, D])
    hsn_o = dout("hsn", [2, 4, D]); csn_o = dout("csn", [2, 12, D]); psn_o = dout("psn", [2, 60, 1024])

    XS = [dscr("xscr%d" % l, [NCH, 128, NTT], F32) for l in range(DEPTH + 1)]
    XSb = [[[Buf() for _ in range(2)] for _ in range(NCH)] for _ in range(DEPTH + 1)]
    KTS = [dscr("kts%d" % j, [16, 128, 1024], BF16) for j in range(2)]
    VSS = [dscr("vss%d" % j, [16, 128, 1024], BF16) for j in range(2)]
    KTSb = [[Buf() for _ in range(16)] for _ in range(2)]
    VSSb = [[Buf() for _ in range(16)] for _ in range(2)]

    PASSES = [dict(name="A", NT=NTA, col0=0, tcs=[(0, 512), (512, 512), (1024, 128)], nblk=9, pi=0),
              dict(name="B", NT=NTB, col0=NTA, tcs=[(0, 512), (512, 512)], nblk=8, pi=1)]

    with contextlib.ExitStack() as st:
        R = Rec(nc, st)

        def SB(name, shape, dt=F32):
            return st.enter_context(nc.sbuf_tensor("s_" + name, list(shape), dt))

        def X(eng, meth, *args, r=(), w=(), **kw):
            R.op(eng, lambda e: getattr(e, meth)(*args, **kw), reads=r, writes=w)

        PS = [st.enter_context(nc.psum_tensor("ps%d" % i, [128, 512], F32)) for i in range(7)]
        PSb = [Buf("ps%d" % i) for i in range(7)]
        PT = st.enter_context(nc.psum_tensor("pt", [128, 1024], BF16))
        PTh = [PT[:, 0:512], PT[:, 0:512]]
        PThb = [Buf("pt0")] * 2
        PTf = PT[:, :].bitcast(F32)
        PK = [PS[6][:, :].bitcast(BF16), PS[6][:, :].bitcast(BF16)]
        PKb = [PSb[6], PSb[6]]
        po = [PS[2][:, 0:128], PS[3][:, 0:128]]
        pob = [PSb[2], PSb[3]]

        hT = SB("hT", [128, NCH, NTA], BF16); hTb = Buf("hT")
        mixT = SB("mixT", [128, 24 * NTA], BF16)
        mixb = [Buf("mix%d" % i) for i in range(24)]

        def mixv(f):
            return mixT[:, f * NTA:(f + 1) * NTA]

        NSCR = 8
        SCW = 1232
        xa2 = SB("xa2", [128, SCW]); xa2b = Buf("xa2"); xa2ch = R.chan()
        scr = [SB("scr%d" % i, [128, SCW]) for i in range(NSCR)]
        scrb = [Buf("scr%d" % i) for i in range(NSCR)]
        scrch = [R.chan() for _ in range(NSCR)]
        ab = 16 * NTA
        def alias(off, n):
            return mixT[:, ab + off: ab + off + n]
        QT = alias(0, 1152); KTr = alias(1152, 2048); Vb = alias(3200, 1152); Vbo = alias(4352, 1024)
        KcT0 = alias(5376, 2048); Wn0 = alias(7424, 512); WT0 = alias(7936, 512); Vsn = alias(8448, 512)
        KSr = alias(8960, 128)
        QT = [QT[:, 0:1024], xa2[:, 0:512].bitcast(BF16)]; QTb = [Buf(), Buf()]
        KTr = [KTr, scr[4][:, 0:1024].bitcast(BF16)]; KTrb = [Buf(), Buf()]
        Vb = [Vb, scr[7][:, 0:576].bitcast(BF16)]; Vbb = [Buf(), Buf()]
        Vbo = [Vbo, scr[7][:, 576:1088].bitcast(BF16)]; Vbob = [Buf(), Buf()]
        Vsn = [Vsn, xa2[:, 832:1088].bitcast(BF16)]; Vsnb = [Buf(), Buf()]
        KSr = [KSr, alias(9088, 128)]; KSrb = [Buf(), Buf()]
        sacc = xa2[:, 1088:1216]; saccb = Buf()
        U = SB("U", [128, 10240], BF16)
        kraw = [U[:, i * 2048:(i + 1) * 2048] for i in range(2)]; krawb = [[Buf() for _ in range(4)] for _ in range(2)]
        vch = [U[:, 4096 + i * 2048:4096 + (i + 1) * 2048] for i in range(2)]; vchb = [[Buf() for _ in range(4)] for _ in range(2)]
        KcTc = [KcT0, U[:, 8192:10240]]; KcTcb = [Buf(), Buf()]
        Wn = [Wn0, SB("Wn1", [128, 512], BF16)]; Wnb = [Buf(), Buf()]
        WT = [WT0, SB("WT1", [128, 512], BF16)]; WTb = [Buf(), Buf()]
        Qpad = [SB("Qpad", [128, 640], BF16), xa2[:, 512:832].bitcast(BF16)]; Qpadb = [Buf(), Buf()]
        att_alias_bufs = ([QTb[0], KTrb[0], Vbb[0], Vbob[0], Vsnb[0]] + KSrb + krawb[0] + krawb[1] + vchb[0] + vchb[1] + KcTcb + Wnb + WTb
                          + [QTb[1], Vsnb[1], Qpadb[1], saccb])
        scr47_att = [KTrb[1], Vbb[1], Vbob[1]]

        hb16 = [U[:, 6144 + i * NTA:6144 + (i + 1) * NTA] for i in range(3)]
        hb16b = [Buf() for _ in range(3)]
        wslot = [SB("wslot%d" % i, [128, 4096], BF16) for i in range(2)]
        wslotb = [[Buf(), Buf()] for _ in range(2)]
        wslotch = [[R.chan(), R.chan()] for _ in range(2)]
        wctr = [0]
        gw = U[:, 0:4096].rearrange("p (a n d) -> p a n d", a=2, n=16); gwb = Buf(); gwch = R.chan()
        pw = U[:, 4096:6144].rearrange("p (g i d) -> p g i d", g=4, i=2); pwb = Buf(); pwch = R.chan()
        rec_alias_bufs = [gwb, pwb] + hb16b + [xa2b]
        ktok = [scr[5][:, 0:1152].rearrange("p (b d) -> p b d", d=128), scr[6][:, 0:1152].rearrange("p (b d) -> p b d", d=128)]
        ktokb = [scrb[5], scrb[6]]
        ktokch = [scrch[5], scrch[6]]
        kcbch = [R.chan(), R.chan()]; vcbch = [R.chan(), R.chan()]
        nbt = [SB("nbt%d" % i, [128, 512]) for i in range(2)]; nbtb = [Buf(), Buf()]
        bt = [SB("bt%d" % i, [128, 512]) for i in range(2)]; btb = [Buf(), Buf()]
        scarry = SB("scarry", [128, 1]); scarryb = Buf()
        dmask_b = SB("dmask_b", [128, 128], BF16); dmaskbb = Buf()
        smask_b = SB("smask_b", [128, 32], BF16); smaskbb = Buf()
        ptile = [SB("ptile%d" % i, [128, 513]) for i in range(2)]
        ptileb = [Buf(), Buf()]
        zeros_f = SB("zeros_f", [128, 512]); zerosb = Buf()
        ident_f = SB("ident_f", [128, 128]); identfb = Buf()
        ident_b = SB("ident_b", [128, 128], BF16); identbb = Buf()
        ones_f = SB("ones_f", [128, 128]); onesb = Buf()
        dmask = SB("dmask", [128, 128]); dmaskb = Buf()
        smask = SB("smask", [128, 32]); smaskb = Buf()
        invc = SB("invc", [128, 64]); invcb = Buf()
        eps_t = SB("eps_t", [128, 1]); epsb = Buf()
        stg = SB("stg", [64, 1024]); stgch = R.chan()
        ost = SB("ost", [64, 1024]); ostb = Buf(); ostch = R.chan()
        vecs = SB("vecs", [128, NCH, 12]); vecsb = Buf()
        psc = SB("psc", [128, 8, 1]); pscb = Buf()
        clt = SB("clt", [128, NCH, 2]); cltb = Buf()
        shT = SB("shT", [128, NCH, 4]); shTb = Buf()
        scT = SB("scT", [128, NCH, 12]); scTb = Buf()
        spT = SB("spT", [128, 8, 60]); spTb = Buf()
        CT = SB("CT", [128, NCH, 3]); CTb = Buf()
        CH = SB("CH", [128, NCH]); CHb = Buf()
        CP = SB("CP", [128, 8, 15]); CPb = Buf()
        STt = SB("STt", [128, NCH, 16]); STtb = Buf()
        STp = SB("STp", [128, 8, 60]); STpb = Buf()
        cch = R.chan()

        R.dma(cch, ident_f[:], ident_d, writes=[identfb])
        R.dma(cch, dmask[:], dmask_d[:, 0:128], writes=[dmaskb])
        R.dma(cch, smask[:], smask_d, writes=[smaskb])
        evc = R.dma(cch, invc[:], invc_d, writes=[invcb])
        for b_ in (identfb, dmaskb, smaskb, invcb):
            b_.w = evc
        X("dve", "tensor_copy", ident_b[:], ident_f[:], r=[identfb], w=[identbb])
        X("dve", "memset", ones_f[:], 1.0, w=[onesb])
        X("dve", "memset", zeros_f[:], 0.0, w=[zerosb])
        X("dve", "memset", eps_t[:], EPS, w=[epsb])
        X("dve", "memset", Qpad[0][:], 0.0, w=[Qpadb[0]])
        X("dve", "tensor_copy", dmask_b[:], dmask[:], r=[dmaskb], w=[dmaskbb])
        X("dve", "tensor_copy", smask_b[:], smask[:], r=[smaskb], w=[smaskbb])

        stg_prev = []

        def load_rows_T_safe(dst, dstb, srcs, ncols):
            nonlocal stg_prev
            for h0 in range(0, ncols, 1024):
                bufs = []
                r0 = 0
                for ap, nr in srcs:
                    b = Buf()
                    R.fence(stg_prev, [b])
                    R.dma(stgch, stg[r0:r0 + nr, 0:1024], ap[:, h0:h0 + 1024], writes=[b])
                    bufs.append(b)
                    r0 += nr
                nrows = r0
                cb0 = h0 // 128
                for i in range(8):
                    X("pe", "transpose", PS[6][0:128, i * nrows:(i + 1) * nrows], stg[0:nrows, i * 128:(i + 1) * 128],
                      ident_f[0:nrows, 0:nrows], r=bufs + [identfb], w=[PSb[6]])
                X("act", "copy", dst[:, cb0:cb0 + 8, 0:nrows], PS[6][:, 0:8 * nrows].rearrange("p (g r) -> p g r", r=nrows),
                  r=[PSb[6]], w=[dstb])
                stg_prev = bufs

        def store_rows(src, srcb, nch, ncol, dests):
            for h0 in range(0, nch, 8):
                for g0 in range(h0, h0 + 8, 4):
                    for i in range(4):
                        X("pe", "transpose", PS[6][0:ncol, i * 128:(i + 1) * 128], src[:, g0 + i, 0:ncol], ident_f[:, :],
                          r=[srcb, identfb], w=[PSb[6]])
                    X("act", "copy", ost[0:ncol, (g0 - h0) * 128:(g0 - h0 + 4) * 128], PS[6][0:ncol, 0:512], r=[PSb[6]], w=[ostb])
                for r0, nr, ap in dests:
                    R.dma(ostch, ap[:, h0 * 128:(h0 + 8) * 128], ost[r0:r0 + nr, 0:1024], reads=[ostb])

        def load_w(panels, K):
            s = wctr[0] % 2
            wctr[0] += 1
            flat = wslot[s]
            np_ = len(panels)
            view = flat[:, 0:K * 128 * np_].rearrange("p (k n) -> p k n", n=128 * np_)
            R.fence(wslotb[s], wslotb[s])
            for i, ap in enumerate(panels):
                R.dma(wslotch[s][i], view[:, :, i * 128:(i + 1) * 128], ap, writes=[wslotb[s][i]], eng="pool")
            return view, wslotb[s]

        def proj(view, pidx, wbuf, K, rhs_fn, rhs_bufs, tcs, banks, evac):
            for tci, (c0, n) in enumerate(tcs):
                bk = banks[tci % len(banks)]
                for k in range(K):
                    X("pe", "matmul", PS[bk][:, 0:n], view[:, k, pidx * 128:(pidx + 1) * 128], rhs_fn(k, c0, n),
                      start=(k == 0), stop=(k == K - 1), r=[wbuf] + rhs_bufs, w=[PSb[bk]])
                evac(tci, c0, n, bk)

        def init_phase():
            for tb in range(17):
                if tb < 8:
                    pi, col = 0, tb * 128
                elif tb < 16:
                    pi, col = 1, NTA + (tb - 8) * 128
                else:
                    pi, col = 0, 1024
                for hf in range(2):
                    src = (xp_d[tb * 128:(tb + 1) * 128, hf * 1024:(hf + 1) * 1024] if tb < 16
                           else xs_d[:, hf * 1024:(hf + 1) * 1024])
                    ii = (tb % 2) * 2 + hf; oi = 4 + ii
                    xin = scr[ii]; xinb = scrb[ii]
                    xo_ = scr[oi]; xob = scrb[oi]
                    R.dma(scrch[ii], xin[:, 0:1024], src, writes=[xinb])
                    for g in range(2):
                        bk = hf * 2 + g
                        for i in range(4):
                            c = g * 4 + i
                            X("pe", "transpose", PS[bk][:, i * 128:(i + 1) * 128], xin[:, c * 128:(c + 1) * 128], ident_f[:, :],
                              r=[xinb, identfb], w=[PSb[bk]])
                        if g == 0:
                            X("act", "copy", xo_[:, 0:512], PS[bk][:, :], r=[PSb[bk]], w=[xob])
                        else:
                            X("dve", "tensor_copy", xo_[:, 512:1024], PS[bk][:, :], r=[PSb[bk]], w=[xob])
                    dst = XS[0][hf * 8:(hf + 1) * 8, :, col:col + 128].rearrange("c p t -> p c t")
                    wb = [XSb[0][c][pi] for c in range(hf * 8, hf * 8 + 8)]
                    R.dma(scrch[oi], dst, xo_[:, 0:1024].rearrange("p (c t) -> p c t", t=128), reads=[xob], writes=wb)

        def stats(l, P):
            NT = P["NT"]; col0 = P["col0"]; pi = P["pi"]
            acc = scr[4]; accb = scrb[4]
            for c in range(NCH):
                s = c % 2
                R.dma(scrch[s], scr[s][:, 0:NT], XS[l][c, :, col0:col0 + NT], reads=[XSb[l][c][pi]], writes=[scrb[s]])
                if c == 0:
                    X("act", "activation", acc[:, 0:NT], scr[s][:, 0:NT], AF.Square, r=[scrb[s]], w=[accb])
                else:
                    X("act", "activation", scr[2 + s][:, 0:NT], scr[s][:, 0:NT], AF.Square, r=[scrb[s]], w=[scrb[2 + s]])
                    X("dve", "tensor_tensor", acc[:, 0:NT], acc[:, 0:NT], scr[2 + s][:, 0:NT], ALU.add,
                      r=[accb, scrb[2 + s]], w=[accb])
            rstd = scr[7]; rstdb = scrb[7]
            for tci, (c0, n) in enumerate(P["tcs"]):
                X("pe", "matmul", PS[tci][:, 0:n], ones_f[:, :], acc[:, c0:c0 + n], start=True, stop=True,
                  r=[onesb, accb], w=[PSb[tci]])
                X("act", "activation", rstd[:, c0:c0 + n], PS[tci][:, 0:n], AF.Sqrt, bias=eps_t[:, 0:1], scale=1.0 / D,
                  r=[PSb[tci], epsb], w=[rstdb])
            X("dve", "reciprocal", rstd[:, 0:NT], rstd[:, 0:NT], r=[rstdb], w=[rstdb])

        def phase1(l, P, gcol):
            NT = P["NT"]; col0 = P["col0"]; pi = P["pi"]
            stats(l, P)
            for c in range(NCH):
                s = c % 2
                R.dma(scrch[s], scr[s][:, 0:NT], XS[l][c, :, col0:col0 + NT], reads=[XSb[l][c][pi]], writes=[scrb[s]])
                X("dve", "scalar_tensor_tensor", hT[:, c, 0:NT], scr[s][:, 0:NT], gcol(c), scr[7][:, 0:NT], ALU.mult, ALU.mult,
                  r=[scrb[s], scrb[7], vecsb], w=[hTb])

        def phase3(l, P, wout, F):
            NT = P["NT"]; col0 = P["col0"]; pi = P["pi"]
            wv = wout.rearrange("(f p) n -> p f n", p=128)
            for j in range(NCH):
                view, wb = load_w([wv[:, :, j * 128:(j + 1) * 128]], F)
                s = j % 2
                R.dma(scrch[s], scr[s][:, 0:NT], XS[l][j, :, col0:col0 + NT], reads=[XSb[l][j][pi]], writes=[scrb[s]])

                def evac(tci, c0, n, bk, s=s):
                    X("dve", "tensor_tensor", scr[2 + s][:, c0:c0 + n], PS[bk][:, 0:n], scr[s][:, c0:c0 + n], ALU.add,
                      r=[PSb[bk], scrb[s]], w=[scrb[2 + s]])
                proj(view, 0, wb[0], F, lambda k, c0, n: mixv(k)[:, c0:c0 + n], [mixb[k] for k in range(F)], P["tcs"],
                     [0, 1, 2], evac)
                R.dma(scrch[2 + s], XS[l + 1][j, :, col0:col0 + NT], scr[2 + s][:, 0:NT], reads=[scrb[2 + s]],
                      writes=[XSb[l + 1][j][pi]])

        def rec_layer(l, j):
            load_rows_T_safe(vecs, vecsb, [(norm_rec[j:j + 1, :], 1), (conv_w[j], 4), (conv_b[j:j + 1, :], 1),
                                           (gate_r_b[j:j + 1, :], 1), (gate_i_b[j:j + 1, :], 1), (rg_lambda[j:j + 1, :], 1)], D)
            load_rows_T_safe(psc, pscb, [(pool_scale[j:j + 1, :], 1)], 1024)
            load_rows_T_safe(shT, shTb, [(sh_d[j], 4)], D)
            load_rows_T_safe(scT, scTb, [(sc_d[j], 12)], D)
            load_rows_T_safe(spT, spTb, [(spl_d[j], 60)], 1024)
            lam = vecs[:, :, 8]
            X("act", "activation", clt[:, :, 0], lam, AF.Exp, scale=-1.0, r=[vecsb], w=[cltb])
            X("act", "activation", clt[:, :, 0], clt[:, :, 0], AF.Ln, bias=ones_f[:, 0:1], r=[cltb, onesb], w=[cltb])
            X("dve", "tensor_scalar", clt[:, :, 1], clt[:, :, 0], -16.0, None, ALU.mult, r=[cltb], w=[cltb])
            X("dve", "tensor_scalar", clt[:, :, 0], clt[:, :, 0], -8.0, None, ALU.mult, r=[cltb], w=[cltb])
            R.dma(gwch, gw[:, 0], gate_r_w[j].rearrange("n c d -> c n d"), writes=[gwb], eng="pool")
            R.dma(gwch, gw[:, 1], gate_i_w[j].rearrange("n c d -> c n d"), writes=[gwb], eng="pool")
            R.dma(pwch, pw[:], pool_w[j].rearrange("g (i p) d -> p g i d", p=128), writes=[pwb], eng="pool")
            wv = w_in_rec[j].rearrange("(k p) n -> p k n", p=128)
            R.fence(att_alias_bufs, mixb[16:24] + rec_alias_bufs)
            for P in PASSES:
                rec_pass(l, j, P, wv)

        def rec_pass(l, j, P, wv):
            NT = P["NT"]; isA = P["pi"] == 0
            phase1(l, P, lambda c: vecs[:, c, 0:1])
            tcs = P["tcs"]
            xc, xcb_ = scr[1], scrb[1]
            rt, rtb = scr[2], scrb[2]
            it, itb = scr[3], scrb[3]
            At, Atb = scr[4], scrb[4]
            Ht, Htb = scr[5], scrb[5]
            Gs = [(scr[6], scrb[6]), (scr[7], scrb[7])]
            PADW = 3
            def seg_views(tile, pad, off):
                v = [tile[:, off:off + 1024]]
                if isA:
                    base = pad + 1024
                    v.append(tile[:, base:base + 4 * (32 + pad)].rearrange("p (b t) -> p b t", t=32 + pad)[:, :, off:off + 32])
                return v

            def out_views(tile):
                v = [tile[:, 0:1024]]
                if isA:
                    v.append(tile[:, 1024:1152].rearrange("p (b t) -> p b t", t=32))
                return v

            def evac_padded(dst, dstb, pad, eng="act"):
                def ev(tci, c0, n, bk):
                    if c0 < 1024:
                        X(eng, "copy" if eng == "act" else "tensor_copy", dst[:, pad + c0:pad + c0 + n], PS[bk][:, 0:n],
                          r=[PSb[bk]], w=[dstb])
                    else:
                        base = pad + 1024
                        X(eng, "copy" if eng == "act" else "tensor_copy",
                          dst[:, base:base + 4 * (32 + pad)].rearrange("p (b t) -> p b t", t=32 + pad)[:, :, pad:pad + 32],
                          PS[bk][:, 0:128].rearrange("p (b t) -> p b t", t=32), r=[PSb[bk]], w=[dstb])
                return ev

            def evac_silu(G, Gb):
                def ev(tci, c0, n, bk):
                    X("act", "activation", G[:, c0:c0 + n], PS[bk][:, 0:n], AF.Sigmoid, r=[PSb[bk]], w=[Gb])
                    X("dve", "tensor_tensor", G[:, c0:c0 + n], G[:, c0:c0 + n], PS[bk][:, 0:n], ALU.mult,
                      r=[PSb[bk], Gb], w=[Gb])
                return ev

            def set_pads(dst, dstb, pad, carry_ap, state_view):
                if isA:
                    X("dve", "memset", dst[:, 0:pad], 0.0, w=[dstb])
                    base = pad + 1024
                    X("dve", "tensor_copy",
                      dst[:, base:base + 4 * (32 + pad)].rearrange("p (b t) -> p b t", t=32 + pad)[:, :, 0:pad],
                      state_view[0], r=[state_view[1]], w=[dstb])
                else:
                    X("dve", "tensor_copy", dst[:, 0:pad], carry_ap[0], r=[carry_ap[1]], w=[dstb])

            rhsT = lambda k, c0, n: hT[:, k, c0:c0 + n]
            XAs = [(scr[0], scrb[0]), (xa2, xa2b)]
            gbanks = [(PS[6], PSb[6]), (PTf, PThb[0])]

            def pre(c):
                XA, XAb = XAs[c % 2]
                view, wb = load_w([wv[:, :, c * 128:(c + 1) * 128], wv[:, :, 2048 + c * 128:2048 + (c + 1) * 128]], NCH)
                G, Gb = Gs[c % 2]
                set_pads(XA, XAb, 3, (CT[:, c, :], CTb), (scT[:, c, :].rearrange("p (b k) -> p b k", k=3), scTb))
                proj(view, 0, wb[0], NCH, rhsT, [hTb], tcs, [0, 1, 2], evac_padded(XA, XAb, 3))
                proj(view, 1, wb[1], NCH, rhsT, [hTb], tcs, [3, 4, 5], evac_silu(G, Gb))

            H2 = [dict(lo=0, hi=512, tcis=[0]), dict(lo=512, hi=NT, tcis=list(range(1, len(tcs))))]
            hbufs = {k: [Buf(), Buf()] for k in ("xc", "rt", "it", "At", "Ht", "xh")}
            allh = [b_ for v in hbufs.values() for b_ in v]
            R.fence([scrb[1], scrb[2], scrb[3], scrb[4], scrb[5], hb16b[0]], allh)

            def post(c, part):
                XA, XAb = XAs[c % 2]
                G, Gb = Gs[c % 2]
                xh = hb16[0]
                cw = lambda k: vecs[:, c, 1 + k:2 + k]

                def conv(h):
                    lo, hi = H2[h]["lo"], min(H2[h]["hi"], 1024)
                    xb_ = hbufs["xc"][h]
                    segs = [(lambda k: XA[:, k + lo:k + hi], xc[:, lo:hi])]
                    if h == 1 and isA:
                        segs.append((lambda k: XA[:, 1027:1167].rearrange("p (b t) -> p b t", t=35)[:, :, k:k + 32],
                                     xc[:, 1024:1152].rearrange("p (b t) -> p b t", t=32)))
                    for iv, ov in segs:
                        X("dve", "tensor_scalar", ov, iv(0), cw(0), vecs[:, c, 5:6], ALU.mult, ALU.add, r=[XAb, vecsb], w=[xb_])
                    for k in range(1, 4):
                        for iv, ov in segs:
                            X("dve", "scalar_tensor_tensor", ov, iv(k), cw(k), ov, ALU.mult, ALU.add, r=[XAb, vecsb, xb_], w=[xb_])

                def xhc(h):
                    lo, hi = H2[h]["lo"], H2[h]["hi"]
                    X("act", "copy", xh[:, lo:hi], xc[:, lo:hi], r=[hbufs["xc"][h]], w=[hbufs["xh"][h]])
                    if h == 1:
                        if isA:
                            X("act", "copy", CT[:, c, :], XA[:, 1024:1027], r=[XAb], w=[CTb])
                            X("act", "copy", STt[:, c, 0:12].rearrange("p (b k) -> p b k", k=3),
                              XA[:, 1027:1167].rearrange("p (b t) -> p b t", t=35)[:, :, 32:35], r=[XAb], w=[STtb])
                        else:
                            X("act", "copy", STt[:, c, 0:3], XA[:, 1024:1027], r=[XAb], w=[STtb])

                def gates(h):
                    for tci in H2[h]["tcis"]:
                        c0, n = tcs[tci]
                        X("pe", "matmul", PS[6][:, 0:n], gw[:, 0, c, :], xh[:, c0:c0 + n], start=True, stop=True,
                          r=[gwb, hbufs["xh"][h]], w=[PSb[6]])
                        X("act", "activation", rt[:, c0:c0 + n], PS[6][:, 0:n], AF.Sigmoid, bias=vecs[:, c, 6:7],
                          r=[PSb[6], vecsb], w=[hbufs["rt"][h]])
                        X("pe", "matmul", PTf[:, 0:n], gw[:, 1, c, :], xh[:, c0:c0 + n], start=True, stop=True,
                          r=[gwb, hbufs["xh"][h]], w=[PThb[0]])
                        X("act", "activation", it[:, c0:c0 + n], PTf[:, 0:n], AF.Sigmoid, bias=vecs[:, c, 7:8],
                          r=[PThb[0], vecsb], w=[hbufs["it"][h]])

                def exps(h):
                    lo, hi = H2[h]["lo"], H2[h]["hi"]
                    rb, ab_ = hbufs["rt"][h], hbufs["At"][h]
                    X("act", "activation", At[:, lo:hi], rt[:, lo:hi], AF.Exp, scale=clt[:, c, 0:1], r=[rb, cltb], w=[ab_])
                    X("act", "activation", rt[:, lo:hi], rt[:, lo:hi], AF.Exp, scale=clt[:, c, 1:2], r=[rb, cltb], w=[rb])

                def itxc(h):
                    lo, hi = H2[h]["lo"], H2[h]["hi"]
                    X("dve", "tensor_tensor", it[:, lo:hi], it[:, lo:hi], xc[:, lo:hi], ALU.mult,
                      r=[hbufs["it"][h], hbufs["xc"][h]], w=[hbufs["it"][h]])
                    X("dve", "tensor_scalar", rt[:, lo:hi], rt[:, lo:hi], -1.0, 1.0, ALU.mult, ALU.add,
                      r=[hbufs["rt"][h]], w=[hbufs["rt"][h]])

                def sqrt_(h):
                    lo, hi = H2[h]["lo"], H2[h]["hi"]
                    X("act", "activation", rt[:, lo:hi], rt[:, lo:hi], AF.Sqrt, r=[hbufs["rt"][h]], w=[hbufs["rt"][h]])

                def scan(h):
                    lo, hi = H2[h]["lo"], H2[h]["hi"]
                    ab_, ib_, hb_ = hbufs["At"][h], hbufs["it"][h], hbufs["Ht"][h]
                    X("dve", "tensor_tensor", it[:, lo:hi], it[:, lo:hi], rt[:, lo:hi], ALU.mult, r=[ib_, hbufs["rt"][h]], w=[ib_])
                    if h == 0:
                        if isA:
                            X("dve", "tensor_tensor_scan", Ht[:, 0:512], At[:, 0:512], it[:, 0:512], 0.0, ALU.mult, ALU.add,
                              r=[ab_, ib_], w=[hb_])
                        else:
                            X("dve", "tensor_tensor_scan", Ht[:, 0:512], At[:, 0:512], it[:, 0:512], CH[:, c:c + 1],
                              ALU.mult, ALU.add, r=[ab_, ib_, CHb], w=[hb_])
                    else:
                        X("dve", "tensor_tensor_scan", Ht[:, 512:1024], At[:, 512:1024], it[:, 512:1024], Ht[:, 511:512],
                          ALU.mult, ALU.add, r=[ab_, ib_, hbufs["Ht"][0]], w=[hb_])
                        if isA:
                            for b in range(4):
                                s0 = 1024 + 32 * b
                                X("dve", "tensor_tensor_scan", Ht[:, s0:s0 + 32], At[:, s0:s0 + 32], it[:, s0:s0 + 32],
                                  shT[:, c, b:b + 1], ALU.mult, ALU.add, r=[ab_, ib_, shTb], w=[hb_])
                    X("dve", "tensor_tensor", mixv(c)[:, lo:hi], Ht[:, lo:hi], G[:, lo:hi], ALU.mult, r=[hb_, Gb], w=[mixb[c]])

                def states():
                    hb_ = hbufs["Ht"][1]
                    if isA:
                        X("act", "copy", CH[:, c:c + 1], Ht[:, 1023:1024], r=[hb_], w=[CHb])
                        X("act", "copy", STt[:, c, 12:16], Ht[:, 1024:1152].rearrange("p (b t) -> p b t", t=32)[:, :, 31],
                          r=[hb_], w=[STtb])
                    else:
                        X("act", "copy", STt[:, c, 3:4], Ht[:, 1023:1024], r=[hb_], w=[STtb])

                if part == 0:
                    conv(0); xhc(0); conv(1); xhc(1)
                else:
                    gates(0); gates(1)
                    exps(0); itxc(0); exps(1); sqrt_(0); itxc(1); sqrt_(1)
                    scan(0); scan(1)
                    states()

            for c in range(NCH):
                if c > 0:
                    post(c - 1, 0)
                pre(c)
                if c > 0:
                    post(c - 1, 1)
            post(NCH - 1, 0)
            post(NCH - 1, 1)
            R.fence(allh, [scrb[1], scrb[2], scrb[3], scrb[4], scrb[5], hb16b[0]])
            if isA:
                store_rows(STt, STtb, NCH, 16, [(0, 12, csn_o[j]), (12, 4, hsn_o[j])])
            else:
                store_rows(STt, STtb, NCH, 4, [(0, 3, cp_o[j]), (3, 1, hp_o[j:j + 1, :])])
            Sa, Sab = scr[1], scrb[1]
            Sb_, Sbb = scr[2], scrb[2]
            W = (15 + 1024 + 4 * 47) if isA else (15 + 1024)
            for g in range(4):
                wwin = 2 << g
                mbs = []
                for o in range(2):
                    cb = 2 * g + o
                    view, wb = load_w([wv[:, :, 4096 + cb * 128:4096 + (cb + 1) * 128],
                                       wv[:, :, 5120 + cb * 128:5120 + (cb + 1) * 128]], NCH)
                    G, Gb = Gs[o]
                    XB, XBb = XAs[o]
                    set_pads(XB, XBb, 15, (CP[:, cb, :], CPb), (spT[:, cb, :].rearrange("p (b k) -> p b k", k=15), spTb))
                    proj(view, 0, wb[0], NCH, rhsT, [hTb], tcs, [0, 1, 2], evac_padded(XB, XBb, 15))
                    proj(view, 1, wb[1], NCH, rhsT, [hTb], tcs, [3, 4, 5], evac_silu(G, Gb))
                    if isA:
                        X("act", "copy", CP[:, cb, :], XB[:, 1024:1039], r=[XBb], w=[CPb])
                        X("act", "copy", STp[:, cb, :].rearrange("p (b k) -> p b k", k=15),
                          XB[:, 1039:1227].rearrange("p (b t) -> p b t", t=47)[:, :, 32:47], r=[XBb], w=[STpb])
                    else:
                        X("act", "copy", STp[:, cb, 0:15], XB[:, 1024:1039], r=[XBb], w=[STpb])
                    src, srcb = XB, XBb
                    sh = 1
                    tgl = [(Sa, Sab), (Sb_, Sbb)]
                    ti = 0
                    lo = 0
                    while sh < wwin:
                        dstt, dstb = tgl[ti]
                        lo2 = lo + sh
                        X("dve", "tensor_tensor", dstt[:, lo2:W], src[:, lo2:W], src[:, lo:W - sh], ALU.add,
                          r=[srcb], w=[dstb])
                        src, srcb = dstt, dstb
                        lo = lo2
                        ti ^= 1
                        sh *= 2
                    mb = hb16[1 + o]; mbb = hb16b[1 + o]
                    for sv, xv, ov in zip(seg_views(src, 15, 15), seg_views(XB, 15, 15), out_views(mb)):
                        X("dve", "scalar_tensor_tensor", ov, sv, 1.0 / wwin, xv, ALU.mult, ALU.subtract,
                          r=[srcb, XBb], w=[mbb])
                    if isA:
                        nf = wwin - 1
                        tmp = tgl[ti][0]; tmpb = tgl[ti][1]
                        X("dve", "tensor_tensor", tmp[:, 0:nf], src[:, 15:15 + nf], invc[:, g * 16:g * 16 + nf], ALU.mult,
                          r=[srcb, invcb], w=[tmpb])
                        X("dve", "tensor_tensor", mb[:, 0:nf], tmp[:, 0:nf], XB[:, 15:15 + nf], ALU.subtract,
                          r=[tmpb, XBb], w=[mbb])
                    mbs.append((mb, mbb))
                    if os.environ.get("KDEBUG") == "2" and l == 0 and isA and g == 3 and o == 0:
                        d1 = nc.dram_tensor("dbg_s16", [128, SCW], F32, kind="ExternalOutput").ap()
                        d2 = nc.dram_tensor("dbg_xb", [128, SCW], F32, kind="ExternalOutput").ap()
                        d3 = nc.dram_tensor("dbg_mb", [128, NTA], BF16, kind="ExternalOutput").ap()
                        R.dma(cch_kv[0], d1, src[:, :], reads=[srcb])
                        R.dma(cch_kv[1], d2, XB[:, :], reads=[XBb])
                        R.dma(cch_kv[2], d3, mb[:, :], reads=[mbb])
                for o in range(2):
                    cbo = 2 * g + o
                    G, Gb = Gs[o]
                    for tci, (c0, n) in enumerate(tcs):
                        gbk, gbkb = gbanks[tci % 2]
                        for i in range(2):
                            X("pe", "matmul", gbk[:, 0:n], pw[:, g, i, o * 128:(o + 1) * 128], mbs[i][0][:, c0:c0 + n],
                              start=(i == 0), stop=(i == 1), r=[pwb, mbs[i][1]], w=[gbkb])
                        X("dve", "scalar_tensor_tensor", mixv(16 + cbo)[:, c0:c0 + n], gbk[:, 0:n], psc[:, cbo, 0:1],
                          G[:, c0:c0 + n], ALU.mult, ALU.mult, r=[gbkb, pscb, Gb], w=[mixb[16 + cbo]])
            if isA:
                store_rows(STp, STpb, 8, 60, [(0, 60, psn_o[j])])
            else:
                store_rows(STp, STpb, 8, 15, [(0, 15, pp_o[j])])
            if os.environ.get("KDEBUG") and l == 0 and isA:
                dbg = nc.dram_tensor("dbg_mix", [24, 128, NTA], BF16, kind="ExternalOutput").ap()
                for f in range(24):
                    R.dma(cch_kv[0], dbg[f], mixv(f), reads=[mixb[f]])
            phase3(l, P, w_out_rec[j], 24)

        def stA(T):
            zb = T["i"] % 2
            for (c0, nn, lhsT, rhs, rb, s_, e_) in T["qk"]:
                X("pe", "matmul", PS[zb][:, c0:c0 + nn], lhsT, rhs, start=s_, stop=e_, r=rb, w=[PSb[zb]])

        def stB(T):
            s = T["i"] % 2; n = T["n"]
            X("act", "activation", nbt[s][:, 0:n], PS[s][:, 0:n], AF.Sigmoid, scale=-1.0, r=[PSb[s]], w=[nbtb[s]])
            X("act", "activation", bt[s][:, 0:n], PS[s][:, 0:n], AF.Sigmoid, r=[PSb[s]], w=[btb[s]])

        def stC(T):
            s = T["i"] % 2; n = T["n"]
            pt_ = ptile[s]; ptb = ptileb[s]
            if T["first"]:
                X("dve", "memset", pt_[:, 0:1], 1.0, w=[ptb])
                init = 1.0; ib = []
            elif T["chain"] == "p":
                pp_ = ptile[1 - s]; npv = T["prev_n"]
                X("dve", "tensor_copy", pt_[:, 0:1], pp_[:, npv:npv + 1], r=[ptileb[1 - s]], w=[ptb])
                init = pp_[:, npv:npv + 1]; ib = [ptileb[1 - s]]
            else:
                X("dve", "tensor_copy", pt_[:, 0:1], scarry[:, 0:1], r=[scarryb], w=[ptb])
                init = scarry[:, 0:1]; ib = [scarryb]
            X("dve", "tensor_tensor_scan", pt_[:, 1:n + 1], nbt[s][:, 0:n], zeros_f[:, 0:n], init, ALU.mult, ALU.add,
              r=[nbtb[s], zerosb] + ib, w=[ptb])
            if T["chain"] == "s" and not T["lastc"]:
                X("dve", "tensor_copy", scarry[:, 0:1], pt_[:, n:n + 1], r=[ptb], w=[scarryb])
            X("dve", "tensor_tensor", Wn[s][:, 0:n][:, ::-1], bt[s][:, 0:n], pt_[:, 0:n], ALU.mult,
              r=[btb[s], ptb], w=[Wnb[s]])

        def stD(T):
            s = T["i"] % 2
            for bi, (c0, kk) in enumerate(T["tr"]):
                X("pe", "transpose", PTh[s][0:kk, bi * 128:(bi + 1) * 128], Wn[s][:, c0:c0 + kk], ident_b[:, :],
                  r=[Wnb[s], identbb], w=[PThb[s]])

        def stE(T):
            s = T["i"] % 2
            kk, ncl = T["ecopy"]
            X("act", "copy", WT[s][0:kk, 0:ncl], PTh[s][0:kk, 0:ncl], r=[PThb[s]], w=[WTb[s]])

        def stF(T):
            s = T["i"] % 2; pr = T["pr"]
            for (c0, nn, lhsT, lb, kk, wc0, s_, e_) in T["pv"]:
                X("pe", "matmul", po[pr][:, c0:c0 + nn], lhsT, WT[s][0:kk, wc0:wc0 + nn], start=s_, stop=e_,
                  r=[lb, WTb[s]], w=[pob[pr]])
            if T.get("fin"):
                T["fin"](pr)
            for h in T.get("postF", ()):
                h()

        def run_pipeline(tasks, pre=(), filler=()):
            n = len(tasks); ng = len(filler); gi = 0
            grp = -1
            for i, T in enumerate(tasks):
                T["i"] = i
                if T["chain"] == "s" or T["first"]:
                    grp += 1
                T["pr"] = grp % 2
            for h in pre:
                h()
            for t in range(n + 3):
                if 0 <= t - 3 < n:
                    stF(tasks[t - 3])
                if 0 <= t - 2 < n:
                    stD(tasks[t - 2]); stE(tasks[t - 2])
                if 0 <= t - 1 < n:
                    stC(tasks[t - 1])
                if t < n:
                    stA(tasks[t]); stB(tasks[t])
                    for h in tasks[t].get("post", ()):
                        h()
                    tgt = min(ng, ((t + 1) * ng + n - 1) // n)
                    while gi < tgt:
                        filler[gi](); gi += 1
            while gi < ng:
                filler[gi](); gi += 1

        pctr = [0]

        def prompt_tasks(hd, ql, isA, G, Gb):
            p = hd % 2
            qa = ql + (0 if isA else 8)
            s0 = (15 - qa) * 128
            q_ap = QT[p][:, ql * 128:(ql + 1) * 128]
            pr = pctr[0] % 2
            pctr[0] += 1
            spans = []
            s = s0
            while s < 2048:
                n = min(512, 2048 - s)
                spans.append((s, n, 2048 - s - n))
                s += n
            total = sum(n // 128 for _, n, _ in spans)
            bd = 0
            tasks = []
            qkb = [QTb[p], KTrb[p]]
            for ci, (s, n, k0) in enumerate(spans):
                if ci == 0:
                    qk = [(0, 128, q_ap, KTr[p][:, s:s + 128], qkb, True, False),
                          (0, 128, ident_b[:, :], dmask_b[:, :], [identbb, dmaskbb], False, True)]
                    if n > 128:
                        qk.append((128, n - 128, q_ap, KTr[p][:, s + 128:s + n], qkb, True, True))
                else:
                    qk = [(0, n, q_ap, KTr[p][:, s:s + n], qkb, True, True)]
                pv = []
                for bi in range(n // 128):
                    kb_ = k0 // 128 + bi
                    if isA:
                        vap, vbuf = Vb[p][:, kb_ * 128:(kb_ + 1) * 128], Vbb[p]
                    elif kb_ < 8:
                        vap, vbuf = Vbo[p][:, kb_ * 128:(kb_ + 1) * 128], Vbob[p]
                    else:
                        vap, vbuf = Vb[p][:, (kb_ - 8) * 128:(kb_ - 7) * 128], Vbb[p]
                    pv.append((0, 128, vap, vbuf, 128, bi * 128, bd == 0, bd == total - 1))
                    bd += 1
                tasks.append(dict(n=n, chain="p", first=(ci == 0), prev_n=(spans[ci - 1][1] if ci else None), lastc=False,
                                  pr=pr, qk=qk, tr=[(bi * 128, 128) for bi in range(n // 128)], ecopy=(128, n), pv=pv,
                                  post=[], postF=[]))

            def fin(pr, hd=hd, ql=ql, G=G, Gb=Gb):
                X("dve", "tensor_tensor", mixv(hd)[:, ql * 128:(ql + 1) * 128], po[pr][:, 0:128], G[:, ql * 128:(ql + 1) * 128],
                  ALU.mult, r=[pob[pr], Gb], w=[mixb[hd]])
            tasks[-1]["fin"] = fin
            return tasks

        def sample_tasks(hd, G, Gb):
            p = hd % 2
            tasks = []

            def mkfin(k):
                def fin(pr, hd=hd, G=G, Gb=Gb):
                    if k == 0:
                        X("dve", "tensor_copy", sacc[:, :], po[pr][:, 0:128], r=[pob[pr]], w=[saccb])
                    else:
                        X("dve", "tensor_tensor", sacc[:, :], sacc[:, :], po[pr][:, 0:128], ALU.add, r=[pob[pr], saccb], w=[saccb])
                    if k == 4:
                        X("dve", "tensor_tensor", mixv(hd)[:, 1024:1152], sacc[:, :], G[:, 1024:1152],
                          ALU.mult, r=[saccb, Gb], w=[mixb[hd]])
                return fin
            pr = pctr[0] % 2; pctr[0] += 1
            tasks.append(dict(n=32, chain="s", first=True, lastc=False, pr=pr,
                              qk=[(0, 32, Qpad[p][:, b * 128:(b + 1) * 128], KSr[p][:, 32 * b:32 * b + 32], [Qpadb[p], KSrb[p]],
                                   b == 0, False) for b in range(4)]
                                 + [(0, 32, ident_b[:, :], smask_b[:, :], [identbb, smaskbb], False, True)],
                              tr=[(0, 32)], ecopy=(32, 128),
                              pv=[(32 * b, 32, Vsn[p][0:32, b * 128:(b + 1) * 128], Vsnb[p], 32, 32 * b, b == 0, b == 3)
                                  for b in range(4)],
                              post=[], postF=[], fin=mkfin(0)))
            for cc in range(4):
                s = cc % 2
                pr = pctr[0] % 2; pctr[0] += 1
                tasks.append(dict(n=512, chain="s", first=False, lastc=(cc == 3), pr=pr,
                                  qk=[(0, 512, Qpad[p][:, b * 128:(b + 1) * 128], KcTc[s][:, b * 512:(b + 1) * 512],
                                       [Qpadb[p], KcTcb[s]], b == 0, b == 3) for b in range(4)],
                                  tr=[(bi * 128, 128) for bi in range(4)], ecopy=(128, 512),
                                  pv=[(32 * b, 32, vch[s][:, (b * 4 + blk) * 128:(b * 4 + blk + 1) * 128], vchb[s][b], 128,
                                       blk * 128 + 32 * b, (blk == 0 and b == 0), (blk == 3 and b == 3))
                                      for blk in range(4) for b in range(4)],
                                  post=[], postF=[], fin=mkfin(cc + 1)))
            return tasks

        def att_layer(l, j):
            load_rows_T_safe(vecs, vecsb, [(norm_att[j:j + 1, :], 1)], D)
            wv = w_in_att[j].rearrange("(k p) n -> p k n", p=128)
            R.fence(mixb[16:24] + rec_alias_bufs, att_alias_bufs)
            for P in PASSES:
                att_pass(l, j, P, wv)

        def att_pass(l, j, P, wv):
            NT = P["NT"]; isA = P["pi"] == 0; tcs = P["tcs"]; nblk = P["nblk"]
            phase1(l, P, lambda c: vecs[:, c, 0:1])
            R.fence([scrb[4], scrb[7]], scr47_att)
            rhsT = lambda k, c0, n: hT[:, k, c0:c0 + n]
            KT32, KT32b = scr[0], scrb[0]
            VT32, VT32b = scr[1], scrb[1]
            Gs = [(scr[2], scrb[2]), (scr[3], scrb[3])]
            tok0 = 0 if isA else 1024
            wviews = {}

            def load_qk(hd):
                wviews[(hd, 0)] = load_w([wv[:, :, hd * 128:(hd + 1) * 128], wv[:, :, 2048 + hd * 128:2048 + (hd + 1) * 128]], NCH)

            def load_vg(hd):
                wviews[(hd, 1)] = load_w([wv[:, :, 4096 + hd * 128:4096 + (hd + 1) * 128],
                                          wv[:, :, 6144 + hd * 128:6144 + (hd + 1) * 128]], NCH)

            def dma_K(hd, cc):
                s = cc % 2; k0 = 1536 - 512 * cc
                for b in range(4):
                    ev = R.dma(kcbch[s], kraw[s][:, b * 512:(b + 1) * 512].rearrange("p (k d) -> p k d", d=128),
                               ck_d[j, b, k0:k0 + 512, hd * 128:(hd + 1) * 128].rearrange("(k p) d -> p k d", p=128),
                               writes=[krawb[s][b]], eng="pool")
                for b in range(4):
                    krawb[s][b].w = ev

            def dma_V(hd, cc):
                s = cc % 2; k0 = 1536 - 512 * cc
                for b in range(4):
                    ev = R.dma(vcbch[s], vch[s][:, b * 512:(b + 1) * 512].rearrange("p (k d) -> p k d", d=128),
                               cv_d[j, b, k0:k0 + 512, hd * 128:(hd + 1) * 128].rearrange("(k p) d -> p k d", p=128),
                               writes=[vchb[s][b]], eng="pool")
                for b in range(4):
                    vchb[s][b].w = ev

            def prepT(cc):
                s = cc % 2
                for half in range(2):
                    for bb in range(2):
                        for blk in range(4):
                            idx = (2 * half + bb) * 4 + blk
                            X("pe", "transpose", PK[half][:, (bb * 4 + blk) * 128:(bb * 4 + blk + 1) * 128],
                              kraw[s][:, idx * 128:(idx + 1) * 128], ident_b[:, :], r=[krawb[s][2 * half + bb], identbb], w=[PKb[half]])
                    X("dve", "tensor_copy",
                      KcTc[s][:, 2 * half * 512:(2 * half + 2) * 512].rearrange("p (b k) -> p b k", k=512)[:, :, ::-1],
                      PK[half][:, 0:1024].rearrange("p (b k) -> p b k", k=512), r=[PKb[half]], w=[KcTcb[s]])

            def head_groups(hd):
                p = hd % 2
                G, Gb = Gs[p]
                groups = []

                def evq(tci, c0, n, bk):
                    if c0 < 1024:
                        X("act", "mul", QT[p][:, c0:c0 + n], PS[bk][:, 0:n], SCALE, r=[PSb[bk]], w=[QTb[p]])
                    else:
                        X("act", "mul", Qpad[p][:, 0:640].rearrange("p (b t) -> p b t", t=160)[:, :, 0:32],
                          PS[bk][:, 0:128].rearrange("p (b t) -> p b t", t=32), SCALE, r=[PSb[bk]], w=[Qpadb[p]])

                def evk(tci, c0, n, bk):
                    X("act", "copy", KT32[:, c0:c0 + n], PS[bk][:, 0:n], r=[PSb[bk]], w=[KT32b])
                    if c0 < 1024:
                        hi = 2047 - (tok0 + c0)
                        X("dve", "tensor_copy", KTr[p][:, hi - n + 1:hi + 1][:, ::-1], KT32[:, c0:c0 + n], r=[KT32b], w=[KTrb[p]])
                    else:
                        X("dve", "tensor_copy", KSr[p][:, 0:128].rearrange("p (b t) -> p b t", t=32)[:, :, ::-1],
                          KT32[:, 1024:1152].rearrange("p (b t) -> p b t", t=32), r=[KT32b], w=[KSrb[p]])

                def evv(tci, c0, n, bk):
                    X("act", "copy", VT32[:, c0:c0 + n], PS[bk][:, 0:n], r=[PSb[bk]], w=[VT32b])

                def evg(tci, c0, n, bk):
                    X("act", "activation", G[:, c0:c0 + n], PS[bk][:, 0:n], AF.Sigmoid, r=[PSb[bk]], w=[Gb])
                    X("dve", "tensor_tensor", G[:, c0:c0 + n], G[:, c0:c0 + n], PS[bk][:, 0:n], ALU.mult,
                      r=[PSb[bk], Gb], w=[Gb])

                if not isA:
                    def ld():
                        R.dma(cch_kv[0][p], KTr[p][:, 1024:2048], KTS[j][hd], reads=[KTSb[j][hd]], writes=[KTrb[p]])
                        R.dma(cch_kv[1][p], Vbo[p][:, :], VSS[j][hd], reads=[VSSb[j][hd]], writes=[Vbob[p]])
                    groups.append(ld)
                gctr = 0
                for wi, pidx, ev in [(0, 0, evq), (0, 1, evk), (1, 0, evv), (1, 1, evg)]:
                    for tci, tc in enumerate(tcs):
                        bk = [4, 5][gctr % 2]; gctr += 1
                        for half in range(4):
                            def g(wi=wi, pidx=pidx, ev=ev, tci=tci, tc=tc, bk=bk, half=half):
                                view, wb = wviews[(hd, wi)]
                                c0, n = tc
                                for k in range(4 * half, 4 * half + 4):
                                    X("pe", "matmul", PS[bk][:, 0:n], view[:, k, pidx * 128:(pidx + 1) * 128], hT[:, k, c0:c0 + n],
                                      start=(k == 0), stop=(k == NCH - 1), r=[wb[pidx], hTb], w=[PSb[bk]])
                                if half == 3:
                                    ev(tci, c0, n, bk)
                            groups.append(g)
                    if (wi, pidx) == (0, 1) and hd + 1 < 16:
                        groups.append(lambda: load_qk(hd + 1))
                    if (wi, pidx) == (1, 1) and hd + 1 < 16:
                        groups.append(lambda: load_vg(hd + 1))
                for which, (src32, src32b, outp, outs) in enumerate([(KT32, KT32b, kp_o, ksn_o), (VT32, VT32b, vp_o, vsn_o)]):
                    kt = ktok[which]; ktb = ktokb[which]
                    for g0 in range(0, nblk, 4):
                        def g(g0=g0, src32=src32, src32b=src32b, kt=kt, ktb=ktb):
                            g1 = min(nblk, g0 + 4)
                            for i, tb in enumerate(range(g0, g1)):
                                X("pe", "transpose", PS[6][:, i * 128:(i + 1) * 128], src32[:, tb * 128:(tb + 1) * 128], ident_f[:, :],
                                  r=[src32b, identfb], w=[PSb[6]])
                            X("act", "copy", kt[:, g0:g1, :], PS[6][:, 0:(g1 - g0) * 128].rearrange("p (b d) -> p b d", d=128),
                              r=[PSb[6]], w=[ktb])
                        groups.append(g)

                    def st(which=which, kt=kt, ktb=ktb, outp=outp, outs=outs):
                        R.dma(ktokch[which], outp[j, tok0:tok0 + 1024, hd * 128:(hd + 1) * 128].rearrange("(b p) d -> p b d", p=128),
                              kt[:, 0:8, :], reads=[ktb])
                        if isA:
                            R.dma(ktokch[which], outs[j, :, hd * 128:(hd + 1) * 128], kt[:, 8, :], reads=[ktb])
                        if which == 1:
                            X("dve", "tensor_copy", Vb[p][:, 0:nblk * 128].rearrange("p (b d) -> p b d", d=128), kt[:, 0:nblk, :],
                              r=[ktb], w=[Vbb[p]])
                    groups.append(st)
                if isA:
                    def fin_a():
                        R.dma(cch_kv[2][p], KTS[j][hd], KTr[p][:, 1024:2048], reads=[KTrb[p]], writes=[KTSb[j][hd]])
                        R.dma(cch_kv[3][p], VSS[j][hd], Vb[p][:, 0:1024], reads=[Vbb[p]], writes=[VSSb[j][hd]])
                        for b in range(4):
                            X("pe", "transpose", PS[6][0:32, b * 128:(b + 1) * 128], VT32[:, 1024 + 32 * b:1056 + 32 * b], ident_f[:, :],
                              r=[VT32b, identfb], w=[PSb[6]])
                        X("act", "copy", Vsn[p][0:32, 0:512], PS[6][0:32, 0:512], r=[PSb[6]], w=[Vsnb[p]])
                    groups.append(fin_a)
                return groups

            X("dve", "memset", Qpad[1][:, :], 0.0, w=[Qpadb[1]])
            if isA:
                dma_K(0, 0); dma_V(0, 0); dma_K(0, 1); dma_V(0, 1)
            load_qk(0); load_vg(0)
            for g in head_groups(0):
                g()
            for hd in range(16):
                G, Gb = Gs[hd % 2]
                filler = head_groups(hd + 1) if hd + 1 < 16 else []
                Pq = [prompt_tasks(hd, ql, isA, G, Gb) for ql in range(8)]
                if isA:
                    S = sample_tasks(hd, G, Gb)
                    nxt = hd + 1 < 16
                    tasks = Pq[0] + [S[0]] + Pq[1] + [S[1]] + Pq[2] + Pq[3] + [S[2]] + Pq[4] + Pq[5] + [S[3]] + Pq[6] + Pq[7] + [S[4]]
                    pre = [lambda: prepT(0), lambda hd=hd: dma_K(hd, 2), lambda: prepT(1), lambda hd=hd: dma_K(hd, 3)]
                    S[1]["post"].append(lambda: prepT(2))
                    S[2]["post"].append(lambda: prepT(3))
                    S[1]["postF"].append(lambda hd=hd: dma_V(hd, 2))
                    S[2]["postF"].append(lambda hd=hd: dma_V(hd, 3))
                    if nxt:
                        S[1]["post"].append(lambda hd=hd: dma_K(hd + 1, 0))
                        S[2]["post"].append(lambda hd=hd: dma_K(hd + 1, 1))
                        S[3]["postF"].append(lambda hd=hd: dma_V(hd + 1, 0))
                        S[4]["postF"].append(lambda hd=hd: dma_V(hd + 1, 1))
                    run_pipeline(tasks, pre, filler)
                else:
                    run_pipeline([T for q in Pq for T in q], (), filler)
            R.fence(scr47_att, [scrb[4], scrb[7]])
            phase3(l, P, w_out_att[j], 16)

        cch_kv = [[R.chan(), R.chan()] for _ in range(4)]

        def final_phase(l):
            load_rows_T_safe(vecs, vecsb, [(norm_final, 1)], D)
            R.fence(att_alias_bufs, [xa2b])
            for P in PASSES:
                NT = P["NT"]; col0 = P["col0"]; pi = P["pi"]
                stats(l, P)
                rstd = scr[7]
                for tb in range(P["nblk"]):
                    for hf in range(2):
                        if tb % 2 == 0:
                            xb_ = scr[hf]; xbb = scrb[hf]; xch = scrch[hf]
                        else:
                            xb_, xbb, xch = [(scr[6], scrb[6], scrch[6]), (xa2, xa2b, xa2ch)][hf]
                        yn = scr[2 + hf]; ynb = scrb[2 + hf]
                        yt = scr[4 + hf]; ytb = scrb[4 + hf]
                        R.dma(xch, xb_[:, 0:1024].rearrange("p (c t) -> p c t", t=128),
                              XS[l][hf * 8:(hf + 1) * 8, :, col0 + tb * 128:col0 + (tb + 1) * 128].rearrange("c p t -> p c t"),
                              reads=[XSb[l][c][pi] for c in range(hf * 8, hf * 8 + 8)], writes=[xbb])
                        for ci in range(8):
                            c = hf * 8 + ci
                            X("dve", "scalar_tensor_tensor", yn[:, ci * 128:(ci + 1) * 128], xb_[:, ci * 128:(ci + 1) * 128],
                              vecs[:, c, 0:1], rstd[:, tb * 128:(tb + 1) * 128], ALU.mult, ALU.mult,
                              r=[xbb, vecsb, scrb[7]], w=[ynb])
                        for g in range(2):
                            bk = hf * 2 + g
                            for i in range(4):
                                ci = g * 4 + i
                                X("pe", "transpose", PS[bk][:, i * 128:(i + 1) * 128], yn[:, ci * 128:(ci + 1) * 128], ident_f[:, :],
                                  r=[ynb, identfb], w=[PSb[bk]])
                            X("act", "copy", yt[:, g * 512:(g + 1) * 512], PS[bk][:, :], r=[PSb[bk]], w=[ytb])
                        if pi == 0 and tb == 8:
                            dst = ys_o[:, hf * 1024:(hf + 1) * 1024]
                        else:
                            t0 = tb * 128 + (0 if pi == 0 else 1024)
                            dst = yp_o[t0:t0 + 128, hf * 1024:(hf + 1) * 1024]
                        R.dma(scrch[4 + hf], dst, yt[:, 0:1024], reads=[ytb])

        init_phase()
        if ltypes is None:
            ltypes = "rara"[:nlayers]
        nlayers = len(ltypes)
        cnt = {"r": 0, "a": 0}
        for l, t in enumerate(ltypes):
            if t == "r":
                rec_layer(l, cnt["r"])
            else:
                att_layer(l, cnt["a"])
            cnt[t] += 1
        final_phase(nlayers)
        R.emit()
        build.stats = R.stats
    return nc


_CACHE = {}


def _consts():
    ident = np.eye(128, dtype=np.float32)
    p = np.arange(128)[:, None]
    i = np.arange(512)[None, :]
    dmask = np.where((i < 128) & (i + p < 128), NEG, 0.0).astype(np.float32)
    t = (np.arange(128) % 32)[:, None]
    ii = np.arange(32)[None, :]
    smask = np.where(ii + t < 32, NEG, 0.0).astype(np.float32)
    invc = np.zeros((128, 64), np.float32)
    for g, w in enumerate((2, 4, 8, 16)):
        for tt in range(16):
            invc[:, g * 16 + tt] = 1.0 / min(w, tt + 1)
    return ident, dmask, smask, invc


def kernel(x_prompt, x_sample, cache_k, cache_v, state_h, state_conv, state_pool,
           norm_rec, w_in_rec, conv_w, conv_b, gate_r_w, gate_r_b, gate_i_w, gate_i_b, rg_lambda,
           pool_w, pool_scale, w_out_rec, norm_att, w_in_att, w_out_att, norm_final):
    f = lambda a: np.ascontiguousarray(np.asarray(a, dtype=np.float32))
    if "nc" not in _CACHE:
        _CACHE["nc"] = build()
    nc = _CACHE["nc"]
    ident, dmask, smask, invc = _consts()
    x_prompt = f(x_prompt); x_sample = f(x_sample); cache_k = f(cache_k); cache_v = f(cache_v)
    state_h = f(state_h); state_conv = f(state_conv); state_pool = f(state_pool)
    shared = dict(norm_rec=f(norm_rec), w_in_rec=f(w_in_rec), conv_w=f(conv_w), conv_b=f(conv_b),
                  gate_r_w=f(gate_r_w), gate_r_b=f(gate_r_b), gate_i_w=f(gate_i_w), gate_i_b=f(gate_i_b),
                  rg_lambda=f(rg_lambda), pool_w=f(pool_w), pool_scale=f(pool_scale), w_out_rec=f(w_out_rec),
                  norm_att=f(norm_att), w_in_att=f(w_in_att), w_out_att=f(w_out_att),
                  norm_final=f(norm_final).reshape(1, D), ident=ident, dmask=dmask, smask=smask, invc=invc)
    in_maps = []
    for c in range(8):
        sl = slice(4 * c, 4 * c + 4)
        m = dict(shared)
        m["xp"] = x_prompt[c % 4]
        m["xs"] = x_sample[sl].reshape(128, D)
        m["ck"] = cache_k[:, sl].reshape(2, 4, 2048, D)
        m["cv"] = cache_v[:, sl].reshape(2, 4, 2048, D)
        m["sh"] = state_h[:, sl]
        m["sc"] = state_conv[:, sl].reshape(2, 12, D)
        m["spl"] = state_pool[:, sl].reshape(2, 60, 1024)
        in_maps.append({k: np.ascontiguousarray(v) for k, v in m.items()})
    res = run_bass_kernel_spmd(nc, in_maps, core_ids=list(range(8)))
    rs = res.results
    y_prompt = np.stack([rs[c]["yp"] for c in range(4)])
    y_sample = np.concatenate([rs[c]["ys"].reshape(4, 32, D) for c in range(8)])
    k_prompt = np.stack([rs[c]["kp"] for c in range(4)], axis=1).reshape(2, 4, 2048, 16, 128)
    v_prompt = np.stack([rs[c]["vp"] for c in range(4)], axis=1).reshape(2, 4, 2048, 16, 128)
    h_prompt = np.stack([rs[c]["hp"] for c in range(4)], axis=1)
    conv_prompt = np.stack([rs[c]["cp"] for c in range(4)], axis=1)
    pool_prompt = np.stack([rs[c]["pp"] for c in range(4)], axis=1)
    k_sample = np.concatenate([rs[c]["ksn"].reshape(2, 4, 32, 16, 128) for c in range(8)], axis=1)
    v_sample = np.concatenate([rs[c]["vsn"].reshape(2, 4, 32, 16, 128) for c in range(8)], axis=1)
    h_sample = np.concatenate([rs[c]["hsn"] for c in range(8)], axis=1)
    conv_sample = np.concatenate([rs[c]["csn"].reshape(2, 4, 3, D) for c in range(8)], axis=1)
    pool_sample = np.concatenate([rs[c]["psn"].reshape(2, 4, 15, 1024) for c in range(8)], axis=1)
    outs = (y_prompt, y_sample, k_prompt, v_prompt, h_prompt, conv_prompt, pool_prompt,
            k_sample, v_sample, h_sample, conv_sample, pool_sample)
    return tuple(np.ascontiguousarray(o, dtype=np.float32) for o in outs)
```
